# Optimizing a Trainium2 kernel written in Bass

```python
import math
import jax, jax.numpy as jnp
from jax import lax
import numpy as np

D_MODEL = 2048
BATCH = 32
SEQ = 256
DEPTH = 2
DEC_BATCH = 8
DEC_SEQ = 2048
PAST_LEN = 256

GRID_W = 64
N_EVEN = (DEPTH + 1) // 2
N_ODD = DEPTH // 2
MOD_COUNT = 6
EPS = 1e-6

GLA_HEADS = 4
GLA_DK = D_MODEL // 2 // GLA_HEADS
GLA_DV = D_MODEL // GLA_HEADS
GLA_RANK = 16
GLA_TAU = 16.0
GLA_CHUNK = 64
ROPE_BASE = 10000.0
GLA_QK = GLA_HEADS * GLA_DK
GLA_V = GLA_HEADS * GLA_DV

RNN_WIDTH = D_MODEL
RNN_BLOCKS = 16
RNN_BLOCK = RNN_WIDTH // RNN_BLOCKS
RNN_CONV = 4
RNN_C = 8.0

SSD_INNER = 2 * D_MODEL
SSD_HEAD_DIM = 64
SSD_HEADS = SSD_INNER // SSD_HEAD_DIM
SSD_STATE = 128
SSD_GROUPS = 8
SSD_CONV = 4
SSD_CHUNK = 128
SSD_XBC = SSD_INNER + 2 * SSD_GROUPS * SSD_STATE

D_FF = (((8 * D_MODEL + 2) // 3 + 255) // 256) * 256

EVEN_SPLITS = (GLA_QK, 2 * GLA_QK, 2 * GLA_QK + GLA_V, 2 * GLA_QK + 2 * GLA_V,
               2 * GLA_QK + 2 * GLA_V + 2 * GLA_RANK, 2 * GLA_QK + 2 * GLA_V + 2 * GLA_RANK + RNN_WIDTH)
EVEN_PROJ = EVEN_SPLITS[-1] + RNN_WIDTH
ODD_PROJ = SSD_INNER + SSD_XBC + 2 * SSD_HEADS

kernel_name = "hybrid_gla_rglru_ssd_prefix_diffusion_step"


def rmsnorm(x, g):
    xf = x.astype(jnp.float32)
    y = xf * lax.rsqrt(jnp.mean(xf * xf, axis=-1, keepdims=True) + EPS)
    return (y * g.astype(jnp.float32)).astype(x.dtype)


def flip(t):
    return jnp.flip(t, axis=1)


def adaln(cvec, w, b):
    return (jax.nn.silu(cvec) @ w + b).reshape(cvec.shape[0], MOD_COUNT, D_MODEL)


def modulated_input(x, mod, j, g):
    return rmsnorm(x, g) * (1.0 + mod[:, 3 * j + 1, None]) + mod[:, 3 * j, None]


def gated_residual(x, out, mod, j, g):
    return x + mod[:, 3 * j + 2, None] * rmsnorm(out, g)


def dwconv_centred(x, w, b):
    K, C = w.shape
    left = K // 2
    y = lax.conv_general_dilated(x, w[:, None, :].astype(x.dtype), window_strides=(1,),
                                 padding=[(left, K - 1 - left)],
                                 dimension_numbers=('NWC', 'WIO', 'NWC'), feature_group_count=C)
    return y + b.astype(x.dtype)


def _rotate(x, pos):
    nf = x.shape[-1] // 2
    inv = ROPE_BASE ** (-jnp.arange(nf, dtype=jnp.float32) / nf)
    ang = pos.astype(jnp.float32)[:, None] * inv
    cos = jnp.cos(ang)[None, :, None, :]
    sin = jnp.sin(ang)[None, :, None, :]
    x1, x2 = x[..., :nf], x[..., nf:]
    return jnp.concatenate([x1 * cos - x2 * sin, x1 * sin + x2 * cos], axis=-1)


def rope_2d(x, rows):
    row = jnp.repeat(jnp.arange(rows), GRID_W)
    col = jnp.tile(jnp.arange(GRID_W), rows)
    half = x.shape[-1] // 2
    return jnp.concatenate([_rotate(x[..., :half], row), _rotate(x[..., half:], col)], axis=-1)


def gla_scan(q, k, v, log_a, s0):
    Bsz, L, H, DK = q.shape
    DV = v.shape[-1]
    nc = L // GLA_CHUNK
    tril = jnp.tril(jnp.ones((GLA_CHUNK, GLA_CHUNK), dtype=bool))

    def chunks(t):
        return jnp.moveaxis(t.reshape(Bsz, nc, GLA_CHUNK, H, t.shape[-1]), 1, 0)

    def step(S, inp):
        qc, kc, vc, ac = inp
        b = jnp.cumsum(ac, axis=1)
        b_last = b[:, -1]
        q_dec = qc * jnp.exp(b)
        scores = jnp.einsum('bthk,bshk->bhts', q_dec, kc * jnp.exp(-b))
        scores = jnp.where(tril, scores, 0.0)
        o = jnp.einsum('bhts,bshv->bthv', scores, vc) + jnp.einsum('bthk,bhkv->bthv', q_dec, S)
        S = jnp.exp(b_last)[..., None] * S + jnp.einsum('bshk,bshv->bhkv', kc * jnp.exp(b_last[:, None] - b), vc)
        return S, o

    S, o = lax.scan(step, s0, tuple(chunks(t) for t in (q, k, v, log_a)))
    return jnp.moveaxis(o, 0, 1).reshape(Bsz, L, H, DV), S


def linear_scan(a, u, h0):
    def comb(x, y):
        return x[0] * y[0], y[0] * x[1] + y[1]
    A, Bc = lax.associative_scan(comb, (a, u), axis=1)
    h = Bc + A * h0[:, None]
    return h, h[:, -1]


def ssd_scan(x, dt, A, Bm, Cm, s0):
    Bsz, L, H, P = x.shape
    G, N = Bm.shape[2], Bm.shape[3]
    HG = H // G
    nc = L // SSD_CHUNK
    xdt = (x * dt[..., None]).reshape(Bsz, nc, SSD_CHUNK, G, HG, P)
    la = (dt * A).reshape(Bsz, nc, SSD_CHUNK, G, HG)
    Bc = Bm.reshape(Bsz, nc, SSD_CHUNK, G, N)
    Cc = Cm.reshape(Bsz, nc, SSD_CHUNK, G, N)
    tril = jnp.tril(jnp.ones((SSD_CHUNK, SSD_CHUNK), dtype=bool))

    def step(S, inp):
        xdt_c, la_c, B_c, C_c = inp
        cum = jnp.cumsum(la_c, axis=1)
        seg = cum[:, :, None] - cum[:, None, :]
        decay = jnp.exp(jnp.where(tril[None, :, :, None, None], seg, -jnp.inf))
        cb = jnp.einsum('btgn,bsgn->btsg', C_c, B_c)
        y = jnp.einsum('btsg,btsgj,bsgjp->btgjp', cb, decay, xdt_c)
        y = y + jnp.einsum('btgn,bgjpn->btgjp', C_c, S) * jnp.exp(cum)[..., None]
        to_end = jnp.exp(cum[:, -1:] - cum)
        S = jnp.exp(cum[:, -1])[..., None, None] * S + jnp.einsum('bsgn,bsgj,bsgjp->bgjpn', B_c, to_end, xdt_c)
        return S, y

    xs = tuple(jnp.moveaxis(t, 1, 0) for t in (xdt, la, Bc, Cc))
    S, y = lax.scan(step, s0.reshape(Bsz, G, HG, P, N), xs)
    return jnp.moveaxis(y, 0, 1).reshape(Bsz, L, H, P), S.reshape(Bsz, H, P, N)


def even_mixer(h, w_in, w_out, gla_w_up, gla_b_up, gla_norm_g, conv_w, conv_b, w_r, b_r, w_i, b_i, lam,
               gla_s0, rnn_h0, rows):
    f32 = jnp.float32
    Bsz, L, _ = h.shape
    q, k, v, g, lr, xr, yr = jnp.split(h @ w_in, EVEN_SPLITS, axis=-1)
    q = q.astype(f32).reshape(Bsz, L, GLA_HEADS, GLA_DK) * (GLA_DK ** -0.5)
    k = k.astype(f32).reshape(Bsz, L, GLA_HEADS, GLA_DK)
    if rows is not None:
        q = rope_2d(q, rows)
        k = rope_2d(k, rows)
    v = v.astype(f32).reshape(Bsz, L, GLA_HEADS, GLA_DV)
    lr = lr.astype(f32).reshape(Bsz, L, 2, GLA_RANK)
    log_a = jax.nn.log_sigmoid(jnp.einsum('bldr,drk->bldk', lr, gla_w_up.astype(f32)) + gla_b_up.astype(f32)) / GLA_TAU
    log_a = log_a.reshape(Bsz, L, 2, GLA_HEADS, GLA_DK)
    s0 = gla_s0.astype(f32)
    o_f, s_f = gla_scan(q, k, v, log_a[:, :, 0], s0[:, 0])
    o_b, s_b = gla_scan(flip(q), flip(k), flip(v), flip(log_a[:, :, 1]), s0[:, 1])
    o = rmsnorm(o_f + flip(o_b), gla_norm_g.reshape(GLA_HEADS, GLA_DV))
    o = o.reshape(Bsz, L, GLA_V) * jax.nn.silu(g.astype(f32))
    xc = dwconv_centred(xr, conv_w, conv_b).astype(f32)
    xb = xc.reshape(Bsz, L, RNN_BLOCKS, RNN_BLOCK)
    r = jax.nn.sigmoid(jnp.einsum('blni,dnij->bldnj', xb, w_r.astype(f32)).reshape(Bsz, L, 2, RNN_WIDTH) + b_r.astype(f32))
    i = jax.nn.sigmoid(jnp.einsum('blni,dnij->bldnj', xb, w_i.astype(f32)).reshape(Bsz, L, 2, RNN_WIDTH) + b_i.astype(f32))
    log_ar = RNN_C * r * jax.nn.log_sigmoid(lam.astype(f32))
    a = jnp.exp(log_ar)
    u = jnp.sqrt(-jnp.expm1(2.0 * log_ar)) * i * xc[:, :, None]
    h0 = rnn_h0.astype(f32)
    h_f, hl_f = linear_scan(a[:, :, 0], u[:, :, 0], h0[:, 0])
    h_b, hl_b = linear_scan(flip(a[:, :, 1]), flip(u[:, :, 1]), h0[:, 1])
    y_rnn = (h_f + flip(h_b)) * jax.nn.gelu(yr.astype(f32))
    out = jnp.concatenate([o, y_rnn], axis=-1).astype(h.dtype) @ w_out
    return out, jnp.stack([s_f, s_b], axis=1), jnp.stack([hl_f, hl_b], axis=1)


def odd_mixer(h, w_in, w_out, conv_w, conv_b, dt_bias, a_log, d_skip, norm_g, s0):
    f32 = jnp.float32
    Bsz, L, _ = h.shape
    z, xbc, dt_raw = jnp.split(h @ w_in, [SSD_INNER, SSD_INNER + SSD_XBC], axis=-1)
    xbc = jax.nn.silu(dwconv_centred(xbc, conv_w, conv_b).astype(f32))
    x, Bm, Cm = jnp.split(xbc, [SSD_INNER, SSD_INNER + SSD_GROUPS * SSD_STATE], axis=-1)
    x = x.reshape(Bsz, L, SSD_HEADS, SSD_HEAD_DIM)
    Bm = Bm.reshape(Bsz, L, SSD_GROUPS, SSD_STATE)
    Cm = Cm.reshape(Bsz, L, SSD_GROUPS, SSD_STATE)
    dt = jax.nn.softplus(dt_raw.astype(f32).reshape(Bsz, L, 2, SSD_HEADS) + dt_bias.astype(f32))
    A = -jnp.exp(a_log.astype(f32))
    s0 = s0.astype(f32)
    y_f, s_f = ssd_scan(x, dt[:, :, 0], A[0], Bm, Cm, s0[:, 0])
    y_b, s_b = ssd_scan(flip(x), flip(dt[:, :, 1]), A[1], flip(Bm), flip(Cm), s0[:, 1])
    y = y_f + flip(y_b) + d_skip.astype(f32)[:, None] * x
    y = y.reshape(Bsz, L, SSD_INNER) * jax.nn.silu(z.astype(f32))
    y = rmsnorm(y.reshape(Bsz, L, SSD_GROUPS, SSD_INNER // SSD_GROUPS),
                norm_g.reshape(SSD_GROUPS, SSD_INNER // SSD_GROUPS)).reshape(Bsz, L, SSD_INNER)
    return y.astype(h.dtype) @ w_out, jnp.stack([s_f, s_b], axis=1)


def swiglu(h, w_gate, w_up, w_down):
    return (jax.nn.silu(h @ w_gate) * (h @ w_up)) @ w_down


def setup_inputs(seed: int = 0) -> dict:
    key = jax.random.key(seed)
    ks = iter(jax.random.split(key, 48))
    f32 = jnp.float32

    def nrm(shape, scale):
        return scale * jax.random.normal(next(ks), shape, f32)

    def unif(shape, lo, hi):
        return jax.random.uniform(next(ks), shape, f32, lo, hi)

    u = unif((N_EVEN, 2, RNN_WIDTH), 0.9, 0.999)
    s = u ** (1.0 / RNN_C)
    rnn_lam = jnp.log(s) - jnp.log1p(-s)
    dt0 = jnp.exp(unif((N_ODD, 2, SSD_HEADS), math.log(1e-3), math.log(1e-1)))
    ssd_dt_bias = dt0 + jnp.log(-jnp.expm1(-dt0))
    return {
        'x_prompt': nrm((BATCH, SEQ, D_MODEL), 1.0),
        'x_sample': nrm((DEC_BATCH, DEC_SEQ, D_MODEL), 1.0),
        'state_gla': nrm((DEC_BATCH, N_EVEN, 2, GLA_HEADS, GLA_DK, GLA_DV), 0.1),
        'state_rglru': nrm((DEC_BATCH, N_EVEN, 2, RNN_WIDTH), 0.5),
        'state_ssd': nrm((DEC_BATCH, N_ODD, 2, SSD_HEADS, SSD_HEAD_DIM, SSD_STATE), 0.1),
        'c': nrm((DEC_BATCH, D_MODEL), 1.0),
        'c_ctx': nrm((D_MODEL,), 1.0),
        'w_ada': nrm((DEPTH, D_MODEL, MOD_COUNT * D_MODEL), 0.5 * D_MODEL ** -0.5),
        'b_ada': nrm((DEPTH, MOD_COUNT * D_MODEL), 0.01),
        'norm_g': 1.0 + nrm((DEPTH, 4, D_MODEL), 0.05),
        'ev_w_in': nrm((N_EVEN, D_MODEL, EVEN_PROJ), D_MODEL ** -0.5),
        'ev_w_out': nrm((N_EVEN, GLA_V + RNN_WIDTH, D_MODEL), (GLA_V + RNN_WIDTH) ** -0.5),
        'gla_w_up': nrm((N_EVEN, 2, GLA_RANK, GLA_QK), GLA_RANK ** -0.5),
        'gla_b_up': nrm((N_EVEN, 2, GLA_QK), 0.1),
        'gla_norm_g': 1.0 + nrm((N_EVEN, GLA_V), 0.05),
        'rnn_conv_w': nrm((N_EVEN, RNN_CONV, RNN_WIDTH), RNN_CONV ** -0.5),
        'rnn_conv_b': nrm((N_EVEN, RNN_WIDTH), 0.01),
        'rnn_w_r': nrm((N_EVEN, 2, RNN_BLOCKS, RNN_BLOCK, RNN_BLOCK), RNN_BLOCK ** -0.5),
        'rnn_b_r': nrm((N_EVEN, 2, RNN_WIDTH), 0.01),
        'rnn_w_i': nrm((N_EVEN, 2, RNN_BLOCKS, RNN_BLOCK, RNN_BLOCK), RNN_BLOCK ** -0.5),
        'rnn_b_i': nrm((N_EVEN, 2, RNN_WIDTH), 0.01),
        'rnn_lam': rnn_lam,
        'od_w_in': nrm((N_ODD, D_MODEL, ODD_PROJ), D_MODEL ** -0.5),
        'od_w_out': nrm((N_ODD, SSD_INNER, D_MODEL), SSD_INNER ** -0.5),
        'ssd_conv_w': nrm((N_ODD, SSD_CONV, SSD_XBC), SSD_CONV ** -0.5),
        'ssd_conv_b': nrm((N_ODD, SSD_XBC), 0.01),
        'ssd_dt_bias': ssd_dt_bias,
        'ssd_a_log': jnp.log(unif((N_ODD, 2, SSD_HEADS), 1.0, 16.0)),
        'ssd_d': 1.0 + nrm((N_ODD, SSD_HEADS), 0.1),
        'ssd_norm_g': 1.0 + nrm((N_ODD, SSD_INNER), 0.05),
        'ffn_w_gate': nrm((DEPTH, D_MODEL, D_FF), D_MODEL ** -0.5),
        'ffn_w_up': nrm((DEPTH, D_MODEL, D_FF), D_MODEL ** -0.5),
        'ffn_w_down': nrm((DEPTH, D_FF, D_MODEL), D_FF ** -0.5),
    }


def reference(x_prompt, x_sample, state_gla, state_rglru, state_ssd, c, c_ctx,
              w_ada, b_ada, norm_g, ev_w_in, ev_w_out, gla_w_up, gla_b_up, gla_norm_g,
              rnn_conv_w, rnn_conv_b, rnn_w_r, rnn_b_r, rnn_w_i, rnn_b_i, rnn_lam,
              od_w_in, od_w_out, ssd_conv_w, ssd_conv_b, ssd_dt_bias, ssd_a_log, ssd_d, ssd_norm_g,
              ffn_w_gate, ffn_w_up, ffn_w_down):
    rows = x_sample.shape[1] // GRID_W
    bp = x_prompt.shape[0]
    f32 = jnp.float32
    zeros_gla = jnp.zeros((bp, 2, GLA_HEADS, GLA_DK, GLA_DV), f32)
    zeros_rnn = jnp.zeros((bp, 2, RNN_WIDTH), f32)
    zeros_ssd = jnp.zeros((bp, 2, SSD_HEADS, SSD_HEAD_DIM, SSD_STATE), f32)
    yp, ys = x_prompt, x_sample
    new_gla, new_rnn, new_ssd = [], [], []
    for l in range(DEPTH):
        mod_p = adaln(c_ctx[None], w_ada[l], b_ada[l])
        mod_s = adaln(c, w_ada[l], b_ada[l])
        hp = modulated_input(yp, mod_p, 0, norm_g[l, 0])
        hs = modulated_input(ys, mod_s, 0, norm_g[l, 0])
        if l % 2 == 0:
            e = l // 2
            ev = (ev_w_in[e], ev_w_out[e], gla_w_up[e], gla_b_up[e], gla_norm_g[e], rnn_conv_w[e], rnn_conv_b[e],
                  rnn_w_r[e], rnn_b_r[e], rnn_w_i[e], rnn_b_i[e], rnn_lam[e])
            out_p, sg, sr = even_mixer(hp, *ev, zeros_gla, zeros_rnn, None)
            out_s, _, _ = even_mixer(hs, *ev, state_gla[:, e], state_rglru[:, e], rows)
            new_gla.append(sg)
            new_rnn.append(sr)
        else:
            o = l // 2
            od = (od_w_in[o], od_w_out[o], ssd_conv_w[o], ssd_conv_b[o], ssd_dt_bias[o], ssd_a_log[o], ssd_d[o],
                  ssd_norm_g[o])
            out_p, ss = odd_mixer(hp, *od, zeros_ssd)
            out_s, _ = odd_mixer(hs, *od, state_ssd[:, o])
            new_ssd.append(ss)
        yp = gated_residual(yp, out_p, mod_p, 0, norm_g[l, 1])
        ys = gated_residual(ys, out_s, mod_s, 0, norm_g[l, 1])
        fp = swiglu(modulated_input(yp, mod_p, 1, norm_g[l, 2]), ffn_w_gate[l], ffn_w_up[l], ffn_w_down[l])
        fs = swiglu(modulated_input(ys, mod_s, 1, norm_g[l, 2]), ffn_w_gate[l], ffn_w_up[l], ffn_w_down[l])
        yp = gated_residual(yp, fp, mod_p, 1, norm_g[l, 3])
        ys = gated_residual(ys, fs, mod_s, 1, norm_g[l, 3])
    new_state_gla = jnp.stack(new_gla, axis=1)
    new_state_rglru = jnp.stack(new_rnn, axis=1)
    new_state_ssd = jnp.stack(new_ssd, axis=1)
    return (yp, ys, new_state_gla, new_state_rglru, new_state_ssd)
```

```python
import math
import numpy as np
from contextlib import ExitStack
import concourse.bass as bass
import concourse.mybir as mybir
from concourse.bass_utils import run_bass_kernel_spmd

F32 = mybir.dt.float32
BF16 = mybir.dt.bfloat16
AF = mybir.ActivationFunctionType
ALU = mybir.AluOpType

D = 2048
KC = 16
DFF = 5632
EPS = 1e-6
SAME_ENGINE_SYNC = True
EPOCH = 16000
N_EPOCH = {'pe': 8, 'act': 6, 'dve': 6, 'pool': 3, 'sp': 3}
N_DMA = {'sp': 16, 'pool': 8, 'act': 8}


class Prog:
    ENG = ['pe', 'act', 'dve', 'pool', 'sp']

    def __init__(self, nc):
        self.nc = nc
        self.st = ExitStack()
        self.eng = {'pe': nc.tensor, 'act': nc.scalar, 'dve': nc.vector, 'pool': nc.gpsimd, 'sp': nc.sync}
        self.ecount = {e: 0 for e in self.ENG}
        self.seen = {e: {} for e in self.ENG}
        self.regs = {}
        self.dma_val = {}
        self.dma_rr = {e: 0 for e in self.ENG}
        self.sems = {}
        for e in self.ENG:
            for i in range(N_EPOCH[e]):
                k = 'e_%s_%d' % (e, i)
                self.sems[k] = self.st.enter_context(nc.semaphore(k))
        for e, n in N_DMA.items():
            for i in range(n):
                k = 'd_%s_%d' % (e, i)
                self.sems[k] = self.st.enter_context(nc.semaphore(k))
        self.nops = 0
        self.nwaits = 0

    def sb(self, name, shape, dt, st=None):
        self.nalloc = getattr(self, 'nalloc', 0) + 1
        return (st or self.st).enter_context(self.nc.sbuf_tensor("%s_%d" % (name, self.nalloc), list(shape), dt))

    def ps(self, name, shape, dt, st=None):
        return (st or self.st).enter_context(self.nc.psum_tensor(name, list(shape), dt))

    def _wait(self, eng, k, v):
        if self.seen[eng].get(k, 0) >= v:
            return
        self.seen[eng][k] = v
        self.eng[eng].wait_ge(self.sems[k], v)
        self.nwaits += 1

    def op(self, eng, fn, reads=(), writes=(), dma=False):
        psr = [k for k in reads if k.startswith('ps')]
        if psr:
            reads = [k for k in reads if not k.startswith('ps')]
            writes = list(writes) + psr
        deps = []
        for k in reads:
            r = self.regs.get(k)
            if r and r['w']:
                deps.append(r['w'])
        for k in writes:
            r = self.regs.get(k)
            if r:
                if r['w']:
                    deps.append(r['w'])
                deps.extend(r['r'].items())
        if dma:
            sk = 'd_%s_%d' % (eng, self.dma_rr[eng] % N_DMA[eng])
            self.dma_rr[eng] += 1
            prev = self.dma_val.get(sk, 0)
            if prev:
                deps.append((sk, prev))
            self.dma_val[sk] = prev + 16
            tok = (sk, prev + 16)
            inc = 16
        else:
            c = self.ecount[eng]
            self.ecount[eng] = c + 1
            assert c // EPOCH < N_EPOCH[eng], eng
            sk = 'e_%s_%d' % (eng, c // EPOCH)
            tok = (sk, c % EPOCH + 1)
            inc = 1
        own = 'e_%s_' % eng
        for (k, v) in deps:
            if k.startswith(own) and (eng == 'pe' or not SAME_ENGINE_SYNC):
                continue
            self._wait(eng, k, v)
        fn(self.eng[eng]).then_inc(self.sems[sk], inc)
        self.nops += 1
        for k in reads:
            r = self.regs.setdefault(k, {'w': None, 'r': {}})
            if r['r'].get(tok[0], 0) < tok[1]:
                r['r'][tok[0]] = tok[1]
        for k in writes:
            self.regs[k] = {'w': tok, 'r': {}}

    def _final_tokens(self):
        final = {}
        for e in self.ENG:
            c = self.ecount[e]
            if c:
                final['e_%s_%d' % (e, (c - 1) // EPOCH)] = (c - 1) % EPOCH + 1
        for k, v in self.dma_val.items():
            final[k] = v
        return final

    def barrier(self):
        final = self._final_tokens()
        for e in self.ENG:
            own = 'e_%s_' % e
            for k, v in final.items():
                if k.startswith(own):
                    continue
                self._wait(e, k, v)
        self.regs = {}

    def finish(self):
        final = self._final_tokens()
        for k, v in final.items():
            if k.startswith('e_sp_'):
                continue
            self._wait('sp', k, v)

    def close(self):
        self.st.close()


class Ring:
    def __init__(self, items):
        self.items = items
        self.i = 0

    def next(self):
        it = self.items[self.i % len(self.items)]
        self.i += 1
        return it


EV_Q, EV_K, EV_V, EV_G, EV_LR, EV_XR, EV_YR = 0, 1024, 2048, 4096, 6144, 6176, 8224
OD_Z, OD_XBC, OD_DT = 0, 4096, 10240


def build(NP, LP, LS, debug=False, mode='full'):
    T = NP * LP + LS
    NT = T // 128
    assert T % 512 == 0 and LP % 128 == 0 and LS % 128 == 0
    NG = T // 512
    seqs = [(i * LP // 128, LP // 128, 0) for i in range(NP)] + [(NP * LP // 128, LS // 128, 1)]
    tile_var = []
    for (t0, n, v) in seqs:
        tile_var += [v] * n
    S0 = NP * LP

    nc = bass.Bass("TRN2", target_bir_lowering=False)
    P = Prog(nc)

    def inp(name, shape, dt=F32):
        return nc.dram_tensor(name, list(shape), dt, kind="ExternalInput").ap()

    def outp(name, shape, dt=F32):
        return nc.dram_tensor(name, list(shape), dt, kind="ExternalOutput").ap()

    def scr(name, shape, dt=F32):
        return nc.dram_tensor(name, list(shape), dt, kind=("ExternalOutput" if debug else "Internal")).ap()

    x_in = inp("x", [T, D])
    cT_in = inp("cT", [128, KC, 2])
    sg_in = inp("sg", [2, 4, 256, 512])
    srT_in = inp("srT", [128, 2, KC])
    ssT_in = inp("ssT", [2, 8, 128, 512])
    w_ada = inp("w_ada", [2, D, 6 * D])
    b_adaT = inp("b_adaT", [2, 128, 96])
    ngT = inp("ngT", [2, 128, 4, KC])
    ev_w_in = inp("ev_w_in", [D, 10272])
    ev_w_out = inp("ev_w_out", [4096, D])
    gla_w_up = inp("gla_w_up", [2, 16, 1024])
    gla_b_up = inp("gla_b_up", [2, 1024])
    gla_ng = inp("gla_ng", [1, 2048])
    rnn_cwT = inp("rnn_cwT", [128, KC, 4])
    rnn_cbT = inp("rnn_cbT", [128, KC])
    rnn_w_r = inp("rnn_w_r", [2, 16, 128, 128])
    rnn_w_i = inp("rnn_w_i", [2, 16, 128, 128])
    rnn_brT = inp("rnn_brT", [128, 2, KC])
    rnn_biT = inp("rnn_biT", [128, 2, KC])
    rnn_lamT = inp("rnn_lamT", [128, 2, KC])
    od_w_in = inp("od_w_in", [D, 10368])
    od_w_out = inp("od_w_out", [4096, D])
    ssd_cwT = inp("ssd_cwT", [128, 48, 4])
    ssd_cbT = inp("ssd_cbT", [128, 48])
    ssd_dtbT = inp("ssd_dtbT", [128, 1])
    ssd_alogT = inp("ssd_alogT", [128, 1])
    ssd_d = inp("ssd_d", [1, 64])
    ssd_ng = inp("ssd_ng", [1, 4096])
    ffn_wg = inp("ffn_wg", [2, D, DFF])
    ffn_wu = inp("ffn_wu", [2, D, DFF])
    ffn_wd = inp("ffn_wd", [2, DFF, D])
    consts = inp("consts", [8, 128, 128])
    rope = inp("rope", [4, 128, LS])

    y_out = outp("y", [T, D])
    ng_out = outp("ng", [NP, 2, 4, 256, 512])
    nr_out = outp("nr", [NP * 2 * KC, 128])
    ns_out = outp("ns", [NP, 2, 64, 64, 128])

    xres = scr("xres", [T, D])
    raw = scr("raw", [T, D])
    actT = scr("actT", [NT, 128, 44, 128], BF16)
    dbg = {}

    cst = P.sb("cst", [128, 8, 128], F32)
    identb = P.sb("identb", [128, 128], BF16)
    P.op('sp', lambda e: e.dma_start(out=cst[:], in_=consts.rearrange("c p n -> p c n")), writes=['cst'], dma=True)
    P.op('pool', lambda e: e.dma_start(out=identb[:], in_=consts[0]), writes=['identb'], dma=True)
    identf, onesf = cst[:, 0, :], cst[:, 1, :]
    INC = [cst[:, 2, :], cst[:, 3, :]]
    EXC = [cst[:, 4, :], cst[:, 5, :]]
    GINC = [cst[:, 6, :], cst[:, 7, :]]
    modT = P.sb("modT", [128, 2, 96, 2], F32)
    tabs = P.sb("tabs", [128, 2, 6, KC, 2], F32)
    ngs = P.sb("ngs", [128, 2, 4, KC], F32)
    ggh = [None]
    ssq = P.sb("ssq", [128, NT, 4], F32)
    psM = [P.ps("psM%d" % i, [128, 512], F32) for i in range(6)]
    psT = [P.ps("psT%d" % i, [128, 1024], BF16) for i in range(2)]
    rM = Ring([0, 1, 2, 3, 4, 5])
    rT = Ring([0, 1])

    def wload(wbuf, key, Wd, pieces, kc_n, kc0=0):
        off = 0
        for (c0, w) in pieces:
            src = Wd[kc0 * 128:(kc0 + kc_n) * 128, c0:c0 + w].rearrange("(kc p) n -> p kc n", p=128)
            P.op('pool', lambda e, o=off, w=w, src=src: e.dma_start(out=wbuf[:, 0:kc_n, o:o + w], in_=src),
                 writes=[key], dma=True)
            off += w

    cTs = P.sb("cTs", [128, KC, 2], F32)
    cTb = P.sb("cTb", [128, KC, 2], BF16)
    bad = P.sb("bad", [128, 2, 96], F32)
    P.op('sp', lambda e: e.dma_start(out=cTs[:], in_=cT_in), writes=['cTs'], dma=True)
    P.op('sp', lambda e: e.dma_start(out=bad[:], in_=b_adaT.rearrange("l p c -> p l c")), writes=['bad'], dma=True)
    P.op('sp', lambda e: e.dma_start(out=ngs[:], in_=ngT.rearrange("l p j k -> p l j k")), writes=['ngs'], dma=True)
    P.op('act', lambda e: e.activation(cTb[:], cTs[:], AF.Silu), reads=['cTs'], writes=['cTb'])

    def adaln_gen(l, wb, wk, ncols, pa, pak):
        nsub = ncols // 128
        for blk in range(6 * D // ncols):
            b = blk % 2
            wload(wb[b], wk + '%d' % b, w_ada[l], [(blk * ncols, ncols)], KC)
            for sub in range(nsub):
                cbi = blk * nsub + sub
                for kc in range(KC):
                    P.op('pe', lambda e, b=b, sub=sub, kc=kc, cbi=cbi: e.matmul(
                        pa[:, cbi * 2:cbi * 2 + 2], wb[b][:, kc, sub * 128:(sub + 1) * 128], cTb[:, kc, :],
                        start=(kc == 0), stop=(kc == KC - 1)),
                        reads=[wk + '%d' % b, 'cTb'], writes=[pak])
            yield
        P.op('dve', lambda e: e.tensor_tensor(
            modT[:, l], pa[:, 0:192].rearrange("p (c v) -> p c v", v=2),
            bad[:, l].unsqueeze(2).broadcast_to([128, 96, 2]), ALU.add),
            reads=[pak, 'bad'], writes=['modT'])
        for (ti, mi, gi, kind) in ((0, 1, 0, 's'), (1, 0, None, 'c'), (2, 2, 1, 'g'), (3, 4, 2, 's'), (4, 3, None, 'c'), (5, 5, 3, 'g')):
            src_ = modT[:, l, mi * 16:(mi + 1) * 16, :]
            dst = tabs[:, l, ti]
            if kind == 'c':
                P.op('dve', lambda e, dst=dst, src_=src_: e.tensor_copy(dst, src_), reads=['modT'], writes=['tabs'])
            else:
                gb = ngs[:, l, gi].unsqueeze(2).broadcast_to([128, KC, 2])
                add = 1.0 if kind == 's' else 0.0
                P.op('dve', lambda e, dst=dst, src_=src_, gb=gb, add=add: e.scalar_tensor_tensor(
                    dst, src_, add, gb, ALU.add, ALU.mult), reads=['modT', 'ngs'], writes=['tabs'])
        yield

    eager_layers = (0,) if mode in ('full', 'even') else (0, 1)
    with ExitStack() as st:
        wb_ = [P.sb("adaw%d" % i, [128, KC, 512], BF16, st) for i in range(2)]
        for l in eager_layers:
            pm_ = rM.next()
            for _ in adaln_gen(l, wb_, 'adaw', 512, psM[pm_], 'psM%d' % pm_):
                pass
    P.barrier()

    def build_ggbc(l, ti):
        with ExitStack() as st:
            dg = [P.sb("dg%d" % i, [128, 128], F32, st) for i in range(2)]
            for v in range(2):
                for q4 in range(4):
                    pm = rM.next()
                    for j in range(4):
                        kc = q4 * 4 + j
                        b = kc % 2
                        P.op('dve', lambda e, b=b, kc=kc, v=v: e.tensor_scalar(
                            dg[b][:], identf, tabs[:, l, ti, kc, v:v + 1], None, ALU.mult),
                            reads=['cst', 'tabs'], writes=['dg%d' % b])
                        P.op('pe', lambda e, b=b, j=j, pm=pm: e.matmul(
                            psM[pm][:, j * 128:(j + 1) * 128], onesf, dg[b][:], start=True, stop=True),
                            reads=['cst', 'dg%d' % b], writes=['psM%d' % pm])
                    P.op('act', lambda e, v=v, q4=q4, pm=pm: e.copy(ggh[0][:, v, q4 * 512:(q4 + 1) * 512], psM[pm][:]),
                         reads=['psM%d' % pm], writes=['ggbc'])
            P.barrier()

    def norm_phase(hT, l, which, src):
        ts_, tsh = (0, 1) if which == 0 else (3, 4)
        with ExitStack() as st:
            xt = [P.sb("nx%d" % i, [128, D], F32, st) for i in range(3)]
            xn = [P.sb("nxn%d" % i, [128, D], BF16, st) for i in range(3)]
            junk = P.sb("njunk", [128, D], BF16, st)
            sm = [P.sb("nsm%d" % i, [128, 4], F32, st) for i in range(3)]
            for i in range(NT):
                b = i % 3
                v = tile_var[i]
                P.op('sp', lambda e, b=b, i=i: e.dma_start(out=xt[b][:], in_=src[i * 128:(i + 1) * 128, :]),
                     writes=['nx%d' % b], dma=True)
                P.op('act', lambda e, b=b: e.activation(junk[:], xt[b][:], AF.Square, accum_out=sm[b][:, 0:1]),
                     reads=['nx%d' % b], writes=['njunk', 'nsm%d' % b])
                P.op('dve', lambda e, b=b: e.tensor_scalar(sm[b][:, 1:2], sm[b][:, 0:1], 1.0 / D, EPS, ALU.mult, ALU.add),
                     reads=['nsm%d' % b], writes=['nsm%d' % b])
                P.op('act', lambda e, b=b: e.sqrt(sm[b][:, 3:4], sm[b][:, 1:2]),
                     reads=['nsm%d' % b], writes=['nsm%d' % b])
                P.op('dve', lambda e, b=b: e.reciprocal(sm[b][:, 2:3], sm[b][:, 3:4]),
                     reads=['nsm%d' % b], writes=['nsm%d' % b])
                P.op('dve', lambda e, b=b: e.tensor_scalar(xn[b][:], xt[b][:], sm[b][:, 2:3], None, ALU.mult),
                     reads=['nsm%d' % b, 'nx%d' % b], writes=['nxn%d' % b])
                for h8 in range(2):
                    pt = rT.next()
                    for j in range(8):
                        kc = h8 * 8 + j
                        P.op('pe', lambda e, b=b, kc=kc, j=j, pt=pt: e.transpose(
                            psT[pt][:, j * 128:(j + 1) * 128], xn[b][:, kc * 128:(kc + 1) * 128], identb[:]),
                            reads=['nxn%d' % b, 'identb'], writes=['psT%d' % pt])
                    for j in range(8):
                        kc = h8 * 8 + j
                        if j % 2 == 0:
                            P.op('act', lambda e, kc=kc, j=j, pt=pt, i=i, v=v: e.activation(
                                hT[:, kc, i * 128:(i + 1) * 128], psT[pt][:, j * 128:(j + 1) * 128], AF.Identity,
                                bias=tabs[:, l, tsh, kc, v:v + 1], scale=tabs[:, l, ts_, kc, v:v + 1]),
                                reads=['psT%d' % pt, 'tabs'], writes=['hT%d' % (kc % 2)])
                        else:
                            P.op('dve', lambda e, kc=kc, j=j, pt=pt, i=i, v=v: e.tensor_scalar(
                                hT[:, kc, i * 128:(i + 1) * 128], psT[pt][:, j * 128:(j + 1) * 128],
                                tabs[:, l, ts_, kc, v:v + 1], tabs[:, l, tsh, kc, v:v + 1], ALU.mult, ALU.add),
                                reads=['psT%d' % pt, 'tabs'], writes=['hT%d' % (kc % 2)])
        P.barrier()

    def proj_fm(hT, Wd, blocks, dst_fn, evac='copy', tok0=0, tok1=None, wname='pw'):
        tok1 = T if tok1 is None else tok1
        with ExitStack() as st:
            wb = [P.sb(wname + "%d" % i, [128, KC, 512], BF16, st) for i in range(2)]
            ob = [P.sb(wname + "o%d" % i, [128, 512], dst_fn.dt, st) for i in range(4)]
            ro = Ring([0, 1, 2, 3])
            ci = 0
            for bi, (pieces, nch) in enumerate(blocks):
                b = bi % 2
                wload(wb[b], wname + '%d' % b, Wd, pieces, KC)
                for sub in range(nch):
                    for g in range(tok0 // 512, tok1 // 512):
                        pm = rM.next()
                        for kc in range(KC):
                            P.op('pe', lambda e, b=b, sub=sub, kc=kc, g=g, pm=pm: e.matmul(
                                psM[pm][:], wb[b][:, kc, sub * 128:(sub + 1) * 128], hT[:, kc, g * 512:(g + 1) * 512],
                                start=(kc == 0), stop=(kc == KC - 1)),
                                reads=[wname + '%d' % b, 'hT'], writes=['psM%d' % pm])
                        o = ro.next()
                        eng = 'act' if (ci + g) % 2 == 0 else 'dve'
                        if eng == 'act':
                            P.op('act', lambda e, o=o, pm=pm: e.copy(ob[o][:], psM[pm][:]),
                                 reads=['psM%d' % pm], writes=[wname + 'o%d' % o])
                        else:
                            P.op('dve', lambda e, o=o, pm=pm: e.tensor_copy(ob[o][:], psM[pm][:]),
                                 reads=['psM%d' % pm], writes=[wname + 'o%d' % o])
                        dap, dkey = dst_fn(ci, g)
                        P.op('sp', lambda e, o=o, dap=dap: e.dma_start(out=dap, in_=ob[o][:]),
                             reads=[wname + 'o%d' % o], writes=[dkey], dma=True)
                    ci += 1
        P.barrier()

    def proj_tm(hT, Wd, blocks, dst_fn, wname='pt'):
        with ExitStack() as st:
            wb = [P.sb(wname + "%d" % i, [128, KC, 512], BF16, st) for i in range(2)]
            ob = [P.sb(wname + "o%d" % i, [128, 512], dst_fn.dt, st) for i in range(4)]
            ro = Ring([0, 1, 2, 3])
            for bi, pieces in enumerate(blocks):
                b = bi % 2
                wload(wb[b], wname + '%d' % b, Wd, pieces, KC)
                for i in range(NT):
                    pm = rM.next()
                    for kc in range(KC):
                        P.op('pe', lambda e, b=b, kc=kc, i=i, pm=pm: e.matmul(
                            psM[pm][:], hT[:, kc, i * 128:(i + 1) * 128], wb[b][:, kc, :],
                            start=(kc == 0), stop=(kc == KC - 1)),
                            reads=[wname + '%d' % b, 'hT'], writes=['psM%d' % pm])
                    o = ro.next()
                    if (bi + i) % 2 == 0:
                        P.op('act', lambda e, o=o, pm=pm: e.copy(ob[o][:], psM[pm][:]),
                             reads=['psM%d' % pm], writes=[wname + 'o%d' % o])
                    else:
                        P.op('dve', lambda e, o=o, pm=pm: e.tensor_copy(ob[o][:], psM[pm][:]),
                             reads=['psM%d' % pm], writes=[wname + 'o%d' % o])
                    dap, dkey = dst_fn(bi, i)
                    P.op('sp', lambda e, o=o, dap=dap: e.dma_start(out=dap, in_=ob[o][:]),
                         reads=[wname + 'o%d' % o], writes=[dkey], dma=True)
        P.barrier()

    def out_proj(Wd, kcn):
        with ExitStack() as st:
            wb = [P.sb("ow%d" % i, [128, kcn, 512], BF16, st) for i in range(2)]
            ab = [P.sb("oa%d" % i, [128, kcn, 128], BF16, st) for i in range(3)]
            ob = [P.sb("oo%d" % i, [128, 512], F32, st) for i in range(3)]
            junk = P.sb("ojunk", [128, 512], BF16, st)
            ra, ro = Ring([0, 1, 2]), Ring([0, 1, 2])
            import os
            dbgl = int(os.environ.get('OPDBG', '9'))
            for nb in range(4):
                b = nb % 2
                half = kcn // 2
                wload(wb[b], 'ow%d' % b, Wd, [(nb * 512, 512)], half, 0)
                src = Wd[half * 128:kcn * 128, nb * 512:(nb + 1) * 512].rearrange("(kc p) n -> p kc n", p=128)
                P.op('pool', lambda e, b=b, src=src, half=half: e.dma_start(out=wb[b][:, half:kcn, :], in_=src),
                     writes=['ow%d' % b], dma=True)
                for i in range(NT):
                    if dbgl < 1:
                        break
                    a = ra.next()
                    P.op('sp', lambda e, a=a, i=i: e.dma_start(out=ab[a][:], in_=actT[i, :, 0:kcn, :]),
                         writes=['oa%d' % a], dma=True)
                    if dbgl < 2:
                        continue
                    pm = rM.next()
                    for kc in range(kcn):
                        P.op('pe', lambda e, a=a, b=b, kc=kc, pm=pm: e.matmul(
                            psM[pm][:], ab[a][:, kc, :], wb[b][:, kc, :], start=(kc == 0), stop=(kc == kcn - 1)),
                            reads=['oa%d' % a, 'ow%d' % b], writes=['psM%d' % pm])
                    if dbgl < 3:
                        continue
                    o = ro.next()
                    P.op('dve', lambda e, o=o, pm=pm: e.tensor_copy(ob[o][:], psM[pm][:]),
                         reads=['psM%d' % pm], writes=['oo%d' % o])
                    if dbgl < 4:
                        continue
                    P.op('act', lambda e, o=o, i=i, nb=nb: e.activation(junk[:], ob[o][:], AF.Square,
                                                                      accum_out=ssq[:, i, nb:nb + 1]),
                         reads=['oo%d' % o], writes=['ojunk', 'ssq'])
                    P.op('act', lambda e, o=o, i=i, nb=nb: e.dma_start(
                        out=raw[i * 128:(i + 1) * 128, nb * 512:(nb + 1) * 512], in_=ob[o][:]),
                        reads=['oo%d' % o], writes=['raw'], dma=True)
        P.barrier()

    def residual_pass(src, dst):
        with ExitStack() as st:
            xt = [P.sb("rx%d" % i, [128, D], F32, st) for i in range(3)]
            rt = [P.sb("rr%d" % i, [128, D], F32, st) for i in range(3)]
            sm = [P.sb("rs%d" % i, [128, 4], F32, st) for i in range(3)]
            for i in range(NT):
                b = i % 3
                v = tile_var[i]
                P.op('sp', lambda e, b=b, i=i: e.dma_start(out=xt[b][:], in_=src[i * 128:(i + 1) * 128, :]),
                     writes=['rx%d' % b], dma=True)
                P.op('sp', lambda e, b=b, i=i: e.dma_start(out=rt[b][:], in_=raw[i * 128:(i + 1) * 128, :]),
                     reads=['raw'], writes=['rr%d' % b], dma=True)
                P.op('dve', lambda e, b=b, i=i: e.tensor_reduce(sm[b][:, 0:1], ssq[:, i, :], mybir.AxisListType.X, ALU.add),
                     reads=['ssq'], writes=['rs%d' % b])
                P.op('dve', lambda e, b=b: e.tensor_scalar(sm[b][:, 1:2], sm[b][:, 0:1], 1.0 / D, EPS, ALU.mult, ALU.add),
                     reads=['rs%d' % b], writes=['rs%d' % b])
                P.op('act', lambda e, b=b: e.sqrt(sm[b][:, 3:4], sm[b][:, 1:2]),
                     reads=['rs%d' % b], writes=['rs%d' % b])
                P.op('dve', lambda e, b=b: e.reciprocal(sm[b][:, 2:3], sm[b][:, 3:4]),
                     reads=['rs%d' % b], writes=['rs%d' % b])
                P.op('dve', lambda e, b=b, v=v: e.scalar_tensor_tensor(
                    rt[b][:], rt[b][:], sm[b][:, 2:3], ggh[0][:, v, :], ALU.mult, ALU.mult),
                    reads=['rs%d' % b, 'ggbc', 'rr%d' % b], writes=['rr%d' % b])
                P.op('pool', lambda e, b=b: e.tensor_tensor(xt[b][:], xt[b][:], rt[b][:], ALU.add),
                     reads=['rr%d' % b, 'rx%d' % b], writes=['rx%d' % b])
                P.op('act', lambda e, b=b, i=i: e.dma_start(out=dst[i * 128:(i + 1) * 128, :], in_=xt[b][:]),
                     reads=['rx%d' % b], writes=['dstres'], dma=True)
        P.barrier()

    def ffn(hT, l):
        with ExitStack() as st:
            wg = [P.sb("fg%d" % i, [128, KC, 256], BF16, st) for i in range(2)]
            wu = [P.sb("fu%d" % i, [128, KC, 256], BF16, st) for i in range(2)]
            sg_ = [P.sb("fs%d" % i, [128, 512], F32, st) for i in range(2)]
            hb = [P.sb("fh%d" % i, [128, 512], BF16, st) for i in range(3)]
            rh = Ring([0, 1, 2])
            for blk in range(22):
                b = blk % 2
                wload(wg[b], 'fg%d' % b, ffn_wg[l], [(blk * 256, 256)], KC)
                wload(wu[b], 'fu%d' % b, ffn_wu[l], [(blk * 256, 256)], KC)
                for sub in range(2):
                    c = blk * 2 + sub
                    for g in range(NG):
                        pg, pu = rM.next(), rM.next()
                        for kc in range(KC):
                            P.op('pe', lambda e, b=b, sub=sub, kc=kc, g=g, pg=pg: e.matmul(
                                psM[pg][:], wg[b][:, kc, sub * 128:(sub + 1) * 128], hT[:, kc, g * 512:(g + 1) * 512],
                                start=(kc == 0), stop=(kc == KC - 1)), reads=['fg%d' % b, 'hT'], writes=['psM%d' % pg])
                        for kc in range(KC):
                            P.op('pe', lambda e, b=b, sub=sub, kc=kc, g=g, pu=pu: e.matmul(
                                psM[pu][:], wu[b][:, kc, sub * 128:(sub + 1) * 128], hT[:, kc, g * 512:(g + 1) * 512],
                                start=(kc == 0), stop=(kc == KC - 1)), reads=['fu%d' % b, 'hT'], writes=['psM%d' % pu])
                        s = (c + g) % 2
                        P.op('act', lambda e, s=s, pg=pg: e.activation(sg_[s][:], psM[pg][:], AF.Silu),
                             reads=['psM%d' % pg], writes=['fs%d' % s])
                        h = rh.next()
                        P.op('dve', lambda e, s=s, h=h, pu=pu: e.tensor_tensor(hb[h][:], sg_[s][:], psM[pu][:], ALU.mult),
                             reads=['fs%d' % s, 'psM%d' % pu], writes=['fh%d' % h])
                        P.op('sp', lambda e, h=h, g=g, c=c: e.dma_start(
                            out=actT[g * 4:(g + 1) * 4, :, c, :].rearrange("t p n -> p t n"),
                            in_=hb[h][:].rearrange("p (t n) -> p t n", t=4)),
                            reads=['fh%d' % h], writes=['actT'], dma=True)
        P.barrier()

    def gated_res(l, ti, src_, dst_):
        with ExitStack() as st:
            ggh[0] = P.sb("ggbc", [128, 2, D], F32, st)
            build_ggbc(l, ti)
            residual_pass(src_, dst_)
        P.barrier()

    def with_hT(fn):
        with ExitStack() as st:
            hT = P.sb("hT", [128, KC, T], BF16, st)
            fn(hT)
        P.barrier()

    env = dict(locals())
    env['scr'] = scr
    cur = x_in
    if mode.startswith('ffn_only'):
        lvl = int(mode[8:] or 9)
        if lvl >= 2:
            with_hT(lambda hT: (norm_phase(hT, 0, 1, cur), ffn(hT, 0)))
        if lvl >= 3:
            out_proj(ffn_wd[0], 44)
        if lvl >= 5:
            gated_res(0, 5, cur, y_out)
    else:
        for l in ((1,) if mode == 'odd' else range(2)):
            if l == 0:
                with_hT(lambda hT: (norm_phase(hT, l, 0, cur), even_proj(env, hT)))
                even_mix(env)
            else:
                with_hT(lambda hT: (norm_phase(hT, l, 0, cur), odd_proj(env, hT)))
                odd_mix(env)
            if mode in ('gla', 'mix'):
                break
            out_proj(ev_w_out if l == 0 else od_w_out, 32)
            if mode in ('even', 'odd'):
                break
            gated_res(l, 2, cur, xres)
            cur = xres
            with_hT(lambda hT: (norm_phase(hT, l, 1, cur), ffn(hT, l)))
            out_proj(ffn_wd[l], 44)
            gated_res(l, 5, cur, y_out if l == 1 else xres)
    P.finish()
    P.close()
    print("build: ops=%d waits=%d" % (P.nops, P.nwaits))
    return nc


class _NS:
    def __init__(self, d):
        self.__dict__.update(d)


def _dst(fn, dt):
    fn.dt = dt
    return fn


def even_proj(env, hT):
    E = _NS(env)
    P, nc, T, NT, LS, S0 = E.P, E.nc, E.T, E.NT, E.LS, E.S0
    sc = E.scr
    G = env['ev'] = {}
    G['qkT'] = sc("qkT", [16, 128, T], BF16)
    G['qkswT'] = sc("qkswT", [16, 128, LS], BF16)
    G['vtm'] = sc("vtm", [T, 2048], BF16)
    G['gtm'] = sc("gtm", [T, 2048], BF16)
    G['lrT'] = sc("lrT", [32, T], F32)
    G['xrT'] = sc("xrT", [16, 128, T], F32)
    G['yrT'] = sc("yrT", [16, 128, T], BF16)
    W = E.ev_w_in
    E.proj_fm(hT, W, [([(EV_Q + b * 512, 512)], 4) for b in range(4)],
              _dst(lambda c, g: (G['qkT'][c, :, g * 512:(g + 1) * 512], 'qkT'), BF16), wname='pq')
    blocks = []
    for b in range(4):
        pieces = []
        for c in range(4):
            c0 = EV_Q + b * 512 + c * 128
            pieces += [(c0 + 64, 64), (c0, 64)]
        blocks.append((pieces, 4))
    E.proj_fm(hT, W, blocks,
              _dst(lambda c, g: (G['qkswT'][c, :, g * 512 - S0:(g + 1) * 512 - S0], 'qkswT'), BF16),
              tok0=S0, tok1=T, wname='pz')
    E.proj_fm(hT, W, [([(EV_XR + b * 512, 512)], 4) for b in range(4)],
              _dst(lambda c, g: (G['xrT'][c, :, g * 512:(g + 1) * 512], 'xrT'), F32), wname='px')
    E.proj_fm(hT, W, [([(EV_YR + b * 512, 512)], 4) for b in range(4)],
              _dst(lambda c, g: (G['yrT'][c, :, g * 512:(g + 1) * 512], 'yrT'), BF16), wname='py')
    E.proj_tm(hT, W, [[(EV_V + b * 512, 512)] for b in range(4)],
              _dst(lambda b, i: (G['vtm'][i * 128:(i + 1) * 128, b * 512:(b + 1) * 512], 'vtm'), BF16), wname='pv')
    E.proj_tm(hT, W, [[(EV_G + b * 512, 512)] for b in range(4)],
              _dst(lambda b, i: (G['gtm'][i * 128:(i + 1) * 128, b * 512:(b + 1) * 512], 'gtm'), BF16), wname='pg')
    psM, rM = E.psM, E.rM
    with ExitStack() as st:
        wl = P.sb("wl", [128, KC, 32], BF16, st)
        lo = [P.sb("lo%d" % i, [32, 512], F32, st) for i in range(2)]
        P.op('pool', lambda e: e.dma_start(out=wl[:], in_=W[:, EV_LR:EV_LR + 32].rearrange("(kc p) n -> p kc n", p=128)),
             writes=['wl'], dma=True)
        for g in range(T // 512):
            pm = rM.next()
            for kc in range(KC):
                P.op('pe', lambda e, kc=kc, g=g, pm=pm: e.matmul(psM[pm][0:32, :], wl[:, kc, :], hT[:, kc, g * 512:(g + 1) * 512],
                                                            start=(kc == 0), stop=(kc == KC - 1)),
                     reads=['wl', 'hT'], writes=['psM%d' % pm])
            b = g % 2
            P.op('act', lambda e, b=b, pm=pm: e.copy(lo[b][:], psM[pm][0:32, :]), reads=['psM%d' % pm], writes=['lo%d' % b])
            P.op('sp', lambda e, b=b, g=g: e.dma_start(out=G['lrT'][:, g * 512:(g + 1) * 512], in_=lo[b][:]),
                 reads=['lo%d' % b], writes=['lrT'], dma=True)
    P.barrier()


def even_mix(env):
    E = _NS(env)
    P, nc, T, NT, LS, S0, NP = E.P, E.nc, E.T, E.NT, E.LS, E.S0, E.NP
    G = env['ev']
    psM, rM, psT, rT = E.psM, E.rM, E.psT, E.rT
    identb, identf, onesf, INC, GINC = E.identb, E.identf, E.onesf, E.INC, E.GINC
    actT = E.actT
    X = mybir.AxisListType.X
    with ExitStack() as st:
        rp = P.sb("rp", [128, 4, LS], F32, st)
        qa = P.sb("qa", [128, LS], BF16, st)
        qs = P.sb("qs", [128, LS], BF16, st)
        t1 = P.sb("t1", [128, LS], F32, st)
        t2 = P.sb("t2", [128, LS], F32, st)
        qo = P.sb("qo", [128, LS], BF16, st)
        P.op('sp', lambda e: e.dma_start(out=rp[:], in_=E.rope.rearrange("c p n -> p c n")), writes=['rp'], dma=True)
        for c in range(16):
            tb = 0 if c % 2 == 0 else 2
            P.op('sp', lambda e, c=c: e.dma_start(out=qa[:], in_=G['qkT'][c, :, S0:S0 + LS]), writes=['qa'], dma=True)
            P.op('sp', lambda e, c=c: e.dma_start(out=qs[:], in_=G['qkswT'][c, :, :]), writes=['qs'], dma=True)
            P.op('dve', lambda e, tb=tb: e.tensor_tensor(t1[:], qa[:], rp[:, tb, :], ALU.mult), reads=['qa', 'rp'], writes=['t1'])
            P.op('pool', lambda e, tb=tb: e.tensor_tensor(t2[:], qs[:], rp[:, tb + 1, :], ALU.mult), reads=['qs', 'rp'], writes=['t2'])
            P.op('dve', lambda e: e.tensor_tensor(qo[:], t1[:], t2[:], ALU.add), reads=['t1', 't2'], writes=['qo'])
            P.op('sp', lambda e, c=c: e.dma_start(out=G['qkT'][c, :, S0:S0 + LS], in_=qo[:]), reads=['qo'], writes=['qkT'], dma=True)
    P.barrier()
    o_f = E.scr("o_f", [T, 2048], F32)
    GSK = ['S%d' % h for h in range(4)]
    GSBK = ['Sbf%d' % h for h in range(4)]
    with ExitStack() as st:
        wup = P.sb("wup", [16, 2, 1024], F32, st)
        bup = P.sb("bup", [1, 2, 1024], F32, st)
        gnbc = P.sb("gnbc", [128, 2048], F32, st)
        S = P.sb("S", [128, 8, 512], F32, st)
        Sbf = P.sb("Sbf", [128, 8, 512], BF16, st)
        lrd = [P.sb("lrd%d" % i, [16, 128], F32, st) for i in range(2)]
        spt = [P.sb("spt%d" % i, [128, 1024], F32, st) for i in range(2)]
        e_all = [P.sb("e_all%d" % i, [128, 8, 128], F32, st) for i in range(2)]
        einv = [P.sb("einv%d" % i, [128, 8, 128], F32, st) for i in range(2)]
        qT = [P.sb("qT%d" % i, [128, 8, 128], BF16, st) for i in range(2)]
        kT = [P.sb("kT%d" % i, [128, 8, 128], BF16, st) for i in range(2)]
        qdec = [P.sb("qdec%d" % i, [128, 8, 128], BF16, st) for i in range(2)]
        kinc = [P.sb("kinc%d" % i, [128, 8, 128], BF16, st) for i in range(2)]
        krem = [P.sb("krem%d" % i, [128, 8, 128], BF16, st) for i in range(2)]
        kremtm = [P.sb("kremtm%d" % i, [128, 1024], BF16, st) for i in range(2)]
        vt = [P.sb("vt%d" % i, [128, 2048], BF16, st) for i in range(2)]
        PT = [P.sb("PT%d" % i, [128, 128], BF16, st) for i in range(2)]
        ot = P.sb("ot", [128, 2048], F32, st)
        oft = [P.sb("oft%d" % i, [128, 2048], F32, st) for i in range(2)]
        gt = [P.sb("gt%d" % i, [128, 2048], BF16, st) for i in range(2)]
        sg = P.sb("sgl", [128, 2048], F32, st)
        mo = P.sb("mo", [128, 2048], BF16, st)
        moT = P.sb("moT", [128, 16, 128], BF16, st)
        junk = P.sb("gjunk", [128, 512], BF16, st)
        sm = P.sb("gsm", [128, 16], F32, st)
        P.op('sp', lambda e: e.dma_start(out=wup[:], in_=E.gla_w_up.rearrange("d r n -> r d n")), writes=['wup'], dma=True)
        P.op('sp', lambda e: e.dma_start(out=bup[:], in_=E.gla_b_up.rearrange("(o d) n -> o d n", o=1)), writes=['bup'], dma=True)
        P.op('sp', lambda e: e.dma_start(out=gnbc[:], in_=E.gla_ng.partition_broadcast(128)), writes=['gnbc'], dma=True)
        for si, (t0, n, var) in enumerate(E.seqs):
            for d in range(2):
                last = 127 if d == 0 else 0
                if var == 1:
                    P.op('sp', lambda e, d=d: e.dma_start(out=S[:], in_=E.sg_in[d].rearrange("h (kc p) n -> p (h kc) n", p=128)),
                         writes=GSK, dma=True)
                else:
                    P.op('dve', lambda e: e.memset(S[:], 0.0), writes=GSK)
                P.op('act', lambda e: e.copy(Sbf[:], S[:]), reads=GSK, writes=GSBK)
                order = list(range(t0, t0 + n)) if d == 0 else list(range(t0 + n - 1, t0 - 1, -1))

                def prep1(i, q):
                    tok = slice(i * 128, (i + 1) * 128)
                    Q = str(q)
                    P.op('sp', lambda e: e.dma_start(out=lrd[q][:], in_=G['lrT'][d * 16:(d + 1) * 16, tok]), writes=['lrd' + Q], dma=True)
                    P.op('sp', lambda e: e.dma_start(out=qT[q][:], in_=G['qkT'][0:8, :, tok].rearrange("c p n -> p c n")), writes=['qT' + Q], dma=True)
                    P.op('sp', lambda e: e.dma_start(out=kT[q][:], in_=G['qkT'][8:16, :, tok].rearrange("c p n -> p c n")), writes=['kT' + Q], dma=True)
                    P.op('sp', lambda e: e.dma_start(out=vt[q][:], in_=G['vtm'][tok, :]), writes=['vt' + Q], dma=True)
                    if d == 1:
                        P.op('sp', lambda e: e.dma_start(out=oft[q][:], in_=o_f[tok, :]), reads=['o_f'], writes=['oft' + Q], dma=True)
                        P.op('sp', lambda e: e.dma_start(out=gt[q][:], in_=G['gtm'][tok, :]), writes=['gt' + Q], dma=True)
                    for hf in range(2):
                        pm = rM.next()
                        P.op('pe', lambda e, hf=hf, pm=pm: e.matmul(psM[pm][:], lrd[q][:], wup[:, d, hf * 512:(hf + 1) * 512], start=True, stop=False),
                             reads=['lrd' + Q, 'wup'], writes=['psM%d' % pm])
                        P.op('pe', lambda e, hf=hf, pm=pm: e.matmul(psM[pm][:], onesf[0:1, :], bup[:, d, hf * 512:(hf + 1) * 512], start=False, stop=True),
                             reads=['bup'], writes=['psM%d' % pm])
                        P.op('act', lambda e, hf=hf, pm=pm: e.activation(spt[q][:, hf * 512:(hf + 1) * 512], psM[pm][:], AF.Exp, scale=-1.0),
                             reads=['psM%d' % pm], writes=['spt' + Q])
                    P.op('act', lambda e: e.activation(spt[q][:], spt[q][:], AF.Ln, bias=1.0), reads=['spt' + Q], writes=['spt' + Q])

                def prep2(i, q):
                    Q = str(q)
                    for hf in range(2):
                        pm = rM.next()
                        for j in range(4):
                            fb = hf * 4 + j
                            P.op('pe', lambda e, fb=fb, j=j, pm=pm: e.matmul(psM[pm][:, j * 128:(j + 1) * 128], spt[q][:, fb * 128:(fb + 1) * 128], GINC[d], start=True, stop=True),
                                 reads=['spt' + Q], writes=['psM%d' % pm])
                        P.op('act', lambda e, hf=hf, pm=pm: e.activation(e_all[q][:, hf * 4:(hf + 1) * 4, :], psM[pm][:].rearrange("p (a b) -> p a b", a=4), AF.Exp),
                             reads=['psM%d' % pm], writes=['e_all' + Q])
                        P.op('act', lambda e, hf=hf, pm=pm: e.activation(einv[q][:, hf * 4:(hf + 1) * 4, :], psM[pm][:].rearrange("p (a b) -> p a b", a=4), AF.Exp, scale=-1.0),
                             reads=['psM%d' % pm], writes=['einv' + Q])

                def prep3(i, q):
                    Q = str(q)
                    P.op('dve', lambda e: e.scalar_tensor_tensor(qdec[q][:], qT[q][:], 0.0625, e_all[q][:], ALU.mult, ALU.mult),
                         reads=['qT' + Q, 'e_all' + Q], writes=['qdec' + Q])
                    P.op('pool', lambda e: e.tensor_tensor(kinc[q][:], kT[q][:], einv[q][:], ALU.mult), reads=['kT' + Q, 'einv' + Q], writes=['kinc' + Q])
                    P.op('pool', lambda e: e.tensor_tensor(krem[q][:], kinc[q][:], e_all[q][:, :, last:last + 1].broadcast_to([128, 8, 128]), ALU.mult),
                         reads=['kinc' + Q, 'e_all' + Q], writes=['krem' + Q])

                def prep4(i, q):
                    Q = str(q)
                    pt = rT.next()
                    for fb in range(8):
                        P.op('pe', lambda e, fb=fb, pt=pt: e.transpose(psT[pt][:, fb * 128:(fb + 1) * 128], krem[q][:, fb, :], identb[:]),
                             reads=['krem' + Q], writes=['psT%d' % pt])
                    P.op('act', lambda e, pt=pt: e.copy(kremtm[q][:], psT[pt][:]), reads=['psT%d' % pt], writes=['kremtm' + Q])

                def head(i, q, h):
                    Q = str(q)
                    if True:
                        pm = rM.next()
                        for kc in range(2):
                            P.op('pe', lambda e, h=h, kc=kc, pm=pm: e.matmul(psM[pm][:, 0:128], kinc[q][:, 2 * h + kc, :], qdec[q][:, 2 * h + kc, :], start=(kc == 0), stop=(kc == 1)),
                                 reads=['kinc' + Q, 'qdec' + Q], writes=['psM%d' % pm])
                        pb = h % 2
                        P.op('dve', lambda e, pb=pb, pm=pm: e.tensor_tensor(PT[pb][:], psM[pm][:, 0:128], INC[d], ALU.mult),
                             reads=['psM%d' % pm], writes=['PT%d' % pb])
                        po = rM.next()
                        P.op('pe', lambda e, h=h, pb=pb, po=po: e.matmul(psM[po][:], PT[pb][:], vt[q][:, h * 512:(h + 1) * 512], start=True, stop=False),
                             reads=['PT%d' % pb, 'vt' + Q], writes=['psM%d' % po])
                        for kc in range(2):
                            P.op('pe', lambda e, h=h, kc=kc, po=po: e.matmul(psM[po][:], qdec[q][:, 2 * h + kc, :], Sbf[:, 2 * h + kc, :], start=False, stop=(kc == 1)),
                                 reads=['qdec' + Q, 'Sbf%d' % h], writes=['psM%d' % po])
                        if d == 0:
                            P.op('act', lambda e, h=h, po=po: e.copy(ot[:, h * 512:(h + 1) * 512], psM[po][:]), reads=['psM%d' % po], writes=['ot%d' % h])
                        else:
                            P.op('dve', lambda e, h=h, po=po: e.tensor_tensor(ot[:, h * 512:(h + 1) * 512], psM[po][:], oft[q][:, h * 512:(h + 1) * 512], ALU.add),
                                 reads=['psM%d' % po, 'oft' + Q], writes=['ot%d' % h])
                        for kc in range(2):
                            fb = 2 * h + kc
                            pu = rM.next()
                            P.op('pe', lambda e, h=h, fb=fb, pu=pu: e.matmul(psM[pu][:], kremtm[q][:, fb * 128:(fb + 1) * 128], vt[q][:, h * 512:(h + 1) * 512], start=True, stop=True),
                                 reads=['kremtm' + Q, 'vt' + Q], writes=['psM%d' % pu])
                            P.op('dve', lambda e, fb=fb, pu=pu: e.scalar_tensor_tensor(S[:, fb, :], S[:, fb, :], e_all[q][:, fb, last:last + 1], psM[pu][:], ALU.mult, ALU.add),
                                 reads=['psM%d' % pu, 'e_all' + Q, 'S%d' % h], writes=['S%d' % h])
                            P.op('act', lambda e, fb=fb: e.copy(Sbf[:, fb, :], S[:, fb, :]), reads=['S%d' % h], writes=['Sbf%d' % h])

                def finish(i, q):
                    tok = slice(i * 128, (i + 1) * 128)
                    Q = str(q)
                    OK_ = ['ot%d' % h for h in range(4)]
                    if d == 0:
                        P.op('act', lambda e: e.dma_start(out=o_f[tok, :], in_=ot[:]), reads=OK_, writes=['o_f'], dma=True)
                    else:
                        for h in range(4):
                            P.op('act', lambda e, h=h: e.activation(junk[:], ot[:, h * 512:(h + 1) * 512], AF.Square, accum_out=sm[:, h:h + 1]),
                                 reads=['ot%d' % h], writes=['gjunk', 'gsm'])
                        P.op('dve', lambda e: e.tensor_scalar(sm[:, 4:8], sm[:, 0:4], 1.0 / 512, EPS, ALU.mult, ALU.add), reads=['gsm'], writes=['gsm'])
                        P.op('act', lambda e: e.sqrt(sm[:, 8:12], sm[:, 4:8]), reads=['gsm'], writes=['gsm'])
                        P.op('dve', lambda e: e.reciprocal(sm[:, 12:16], sm[:, 8:12]), reads=['gsm'], writes=['gsm'])
                        P.op('act', lambda e: e.activation(sg[:], gt[q][:], AF.Silu), reads=['gt' + Q], writes=['sgl'])
                        for h in range(4):
                            P.op('dve', lambda e, h=h: e.scalar_tensor_tensor(ot[:, h * 512:(h + 1) * 512], ot[:, h * 512:(h + 1) * 512], sm[:, 12 + h:13 + h],
                                                                             gnbc[:, h * 512:(h + 1) * 512], ALU.mult, ALU.mult),
                                 reads=['ot%d' % h, 'gsm', 'gnbc'], writes=['ot%d' % h])
                        P.op('pool', lambda e: e.tensor_tensor(mo[:], ot[:], sg[:], ALU.mult), reads=OK_ + ['sgl'], writes=['mo'])
                        for h8 in range(2):
                            pt = rT.next()
                            for j in range(8):
                                kc = h8 * 8 + j
                                P.op('pe', lambda e, kc=kc, j=j, pt=pt: e.transpose(psT[pt][:, j * 128:(j + 1) * 128], mo[:, kc * 128:(kc + 1) * 128], identb[:]),
                                     reads=['mo'], writes=['psT%d' % pt])
                            P.op('act', lambda e, h8=h8, pt=pt: e.copy(moT[:, h8 * 8:(h8 + 1) * 8, :], psT[pt][:].rearrange("p (a b) -> p a b", a=8)),
                                 reads=['psT%d' % pt], writes=['moT'])
                        P.op('act', lambda e: e.dma_start(out=actT[i, :, 0:16, :], in_=moT[:]), reads=['moT'], writes=['actT'], dma=True)

                preps = (prep1, prep2, prep3, prep4)
                for f_ in preps:
                    f_(order[0], 0)
                for k_, i in enumerate(order):
                    for h in range(4):
                        if k_ + 1 < len(order):
                            preps[h](order[k_ + 1], (k_ + 1) % 2)
                        head(i, k_ % 2, h)
                    finish(i, k_ % 2)
                if var == 0:
                    P.op('sp', lambda e, si=si, d=d: e.dma_start(out=E.ng_out[si, d].rearrange("h (kc p) n -> p (h kc) n", p=128), in_=S[:]),
                         reads=GSK, writes=['ng_out'], dma=True)
    P.barrier()
    if env['mode'] == 'gla':
        return
    Lmax = max(max(n for (_, n, _) in E.seqs) * 128, NP * E.LP)
    with ExitStack() as st:
        wr = P.sb("wr", [128, 2, 16, 128], BF16, st)
        wi = P.sb("wi", [128, 2, 16, 128], BF16, st)
        tb = P.sb("rtb", [128, 8, 2, KC], F32, st)
        cw = P.sb("rcw", [128, KC, 4], F32, st)
        cbs = P.sb("rcb", [128, KC], F32, st)
        hl = P.sb("hl", [128, NP * 32], F32, st)
        hlo = P.sb("hlo", [128, 128], F32, st)
        def mkbufs(tag, Lx, Lpad):
            B = {}
            B['xrp'] = [P.sb("xrp%s%d" % (tag, i), [128, Lpad], F32, st) for i in range(2)]
            B['yr'] = [P.sb("yr%s%d" % (tag, i), [128, Lx], BF16, st) for i in range(2)]
            B['xc'] = P.sb("xc" + tag, [128, Lx], F32, st)
            B['xcb'] = P.sb("xcb" + tag, [128, Lx], BF16, st)
            for nm in ('rr', 'ii', 'aa', 'hh'):
                B[nm] = [P.sb("%s%s%d" % (nm, tag, i), [128, Lx], F32, st) for i in range(2)]
            B['tmp'] = P.sb("rtmp" + tag, [128, Lx], F32, st)
            B['ybf'] = P.sb("ybf" + tag, [128, Lx], BF16, st)
            return B
        bufsets = [mkbufs('A', NP * E.LP, NP * (E.LP + 3)), mkbufs('B', LS, LS + 3)]
        P.op('pool', lambda e: e.dma_start(out=wr[:], in_=E.rnn_w_r.rearrange("d n i j -> i d n j")), writes=['wr'], dma=True)
        P.op('pool', lambda e: e.dma_start(out=wi[:], in_=E.rnn_w_i.rearrange("d n i j -> i d n j")), writes=['wi'], dma=True)
        P.op('sp', lambda e: e.dma_start(out=tb[:, 0], in_=E.rnn_brT), writes=['rtb'], dma=True)
        P.op('sp', lambda e: e.dma_start(out=tb[:, 1], in_=E.rnn_biT), writes=['rtb'], dma=True)
        P.op('sp', lambda e: e.dma_start(out=tb[:, 2], in_=E.rnn_lamT), writes=['rtb'], dma=True)
        P.op('sp', lambda e: e.dma_start(out=tb[:, 4], in_=E.srT_in), writes=['rtb'], dma=True)
        P.op('sp', lambda e: e.dma_start(out=cw[:], in_=E.rnn_cwT), writes=['rcw'], dma=True)
        P.op('sp', lambda e: e.dma_start(out=cbs[:], in_=E.rnn_cbT), writes=['rcb'], dma=True)
        P.op('act', lambda e: e.activation(tb[:, 3], tb[:, 2], AF.Exp, scale=-1.0), reads=['rtb'], writes=['rtb'])
        P.op('act', lambda e: e.activation(tb[:, 3], tb[:, 3], AF.Ln, bias=1.0), reads=['rtb'], writes=['rtb'])
        P.op('dve', lambda e: e.tensor_scalar(tb[:, 3], tb[:, 3], -8.0, None, ALU.mult), reads=['rtb'], writes=['rtb'])
        P.op('dve', lambda e: e.memset(tb[:, 5], 0.0), reads=['rtb'], writes=['rtb'])
        P.op('dve', lambda e: e.memset(hl[:], 0.0), writes=['hl'])
        P.barrier()
        def chain(t0, nb, Lb, var, B, TG):
            xrp, yr, xc, xcb, rr, ii, aa, hh, tmp, ybf = (B[k_] for k_ in ('xrp', 'yr', 'xc', 'xcb', 'rr', 'ii', 'aa', 'hh', 'tmp', 'ybf'))
            it = 0
            L = nb * Lb
            n = L // 128
            tk = slice(t0 * 128, t0 * 128 + L)
            for fc in range(KC):
                xb = it % 2
                it += 1
                XR = xrp[xb][:, 0:nb * (Lb + 3)].rearrange("p (b l) -> p b l", b=nb)
                YR = yr[xb]
                kx, ky = TG + 'xrp%d' % xb, TG + 'yr%d' % xb
                xc3 = xc[:, 0:L].rearrange("p (b l) -> p b l", b=nb)
                P.op('pool', lambda e, XR=XR: e.memset(XR[:, :, 0:2], 0.0), writes=[kx])
                P.op('pool', lambda e, XR=XR, Lb=Lb: e.memset(XR[:, :, Lb + 2:Lb + 3], 0.0), writes=[kx])
                P.op('sp', lambda e, XR=XR, fc=fc, tk=tk, Lb=Lb, nb=nb: e.dma_start(out=XR[:, :, 2:2 + Lb], in_=G['xrT'][fc, :, tk].rearrange("p (b l) -> p b l", b=nb)), writes=[kx], dma=True)
                P.op('sp', lambda e, YR=YR, fc=fc, tk=tk, L=L: e.dma_start(out=YR[:, 0:L], in_=G['yrT'][fc, :, tk]), writes=[ky], dma=True)
                P.op('dve', lambda e, XR=XR, fc=fc, Lb=Lb, xc3=xc3: e.tensor_scalar(xc3, XR[:, :, 0:Lb], cw[:, fc, 0:1], cbs[:, fc:fc + 1], ALU.mult, ALU.add),
                     reads=[kx], writes=[TG + 'xc'])
                for j in range(1, 4):
                    P.op('dve', lambda e, XR=XR, fc=fc, Lb=Lb, j=j, xc3=xc3: e.scalar_tensor_tensor(xc3, XR[:, :, j:j + Lb], cw[:, fc, j:j + 1], xc3, ALU.mult, ALU.add),
                         reads=[kx, TG + 'xc'], writes=[TG + 'xc'])
                P.op('act', lambda e, L=L: e.copy(xcb[:, 0:L], xc[:, 0:L]), reads=[TG + 'xc'], writes=[TG + 'xcb'])
                yield
                for d in range(2):
                    for (gate, wt, dst, bi) in (('r', wr, rr[d], 0), ('i', wi, ii[d], 1)):
                        for c0 in range(0, L, 512):
                            w_ = min(512, L - c0)
                            pm = rM.next()
                            P.op('pe', lambda e, wt=wt, d=d, fc=fc, c0=c0, w_=w_, pm=pm: e.matmul(psM[pm][:, 0:w_], wt[:, d, fc, :], xcb[:, c0:c0 + w_], start=True, stop=True),
                                 reads=['wr', 'wi', TG + 'xcb'], writes=['psM%d' % pm])
                            P.op('act', lambda e, dst=dst, bi=bi, d=d, fc=fc, c0=c0, w_=w_, pm=pm: e.activation(dst[:, c0:c0 + w_], psM[pm][:, 0:w_], AF.Sigmoid, bias=tb[:, bi, d, fc:fc + 1]),
                                 reads=['psM%d' % pm], writes=[TG + 'g%s%d' % (gate, d)])
                yield
                for d in range(2):
                    if d == 1:
                        yield
                    P.op('act', lambda e, d=d, fc=fc, L=L: e.activation(aa[d][:, 0:L], rr[d][:, 0:L], AF.Exp, scale=tb[:, 3, d, fc:fc + 1]), reads=[TG + 'gr%d' % d], writes=[TG + 'aa%d' % d])
                    eng = 'dve' if d == 0 else 'pool'
                    P.op(eng, lambda e, d=d, L=L: e.tensor_tensor(rr[d][:, 0:L], aa[d][:, 0:L], aa[d][:, 0:L], ALU.mult), reads=[TG + 'aa%d' % d], writes=[TG + 'gr%d' % d])
                    P.op(eng, lambda e, d=d, L=L: e.tensor_scalar(rr[d][:, 0:L], rr[d][:, 0:L], 1.0, None, ALU.min), reads=[TG + 'gr%d' % d], writes=[TG + 'gr%d' % d])
                    P.op('act', lambda e, d=d, L=L: e.activation(rr[d][:, 0:L], rr[d][:, 0:L], AF.Sqrt, scale=-1.0, bias=1.0), reads=[TG + 'gr%d' % d], writes=[TG + 'gr%d' % d])
                    P.op(eng, lambda e, d=d, L=L: e.tensor_tensor(ii[d][:, 0:L], ii[d][:, 0:L], xc[:, 0:L], ALU.mult), reads=[TG + 'gi%d' % d, TG + 'xc'], writes=[TG + 'gi%d' % d])
                    P.op(eng, lambda e, d=d, L=L: e.tensor_tensor(ii[d][:, 0:L], ii[d][:, 0:L], rr[d][:, 0:L], ALU.mult), reads=[TG + 'gi%d' % d, TG + 'gr%d' % d], writes=[TG + 'gi%d' % d])
                    if var == 1:
                        h0 = tb[:, 4, d, fc:fc + 1]
                    else:
                        h0 = 0.0
                        bc = 0 if d == 0 else Lb - 1
                        P.op(eng, lambda e, d=d, bc=bc, L=L, Lb=Lb: e.memset(aa[d][:, bc:L:Lb], 0.0), reads=[TG + 'aa%d' % d], writes=[TG + 'aa%d' % d])
                    if d == 0:
                        P.op('dve', lambda e, L=L, h0=h0: e.tensor_tensor_scan(hh[0][:, 0:L], aa[0][:, 0:L], ii[0][:, 0:L], h0, ALU.mult, ALU.add),
                             reads=[TG + 'aa0', TG + 'gi0'], writes=[TG + 'hh0'])
                    else:
                        P.op('dve', lambda e, L=L, h0=h0: e.tensor_tensor_scan(hh[1][:, 0:L][:, ::-1], aa[1][:, 0:L][:, ::-1], ii[1][:, 0:L][:, ::-1], h0, ALU.mult, ALU.add),
                             reads=[TG + 'aa1', TG + 'gi1'], writes=[TG + 'hh1'])
                    if var == 0:
                        lc = Lb - 1 if d == 0 else 0
                        c0_ = d * 16 + fc
                        P.op('act', lambda e, d=d, c0_=c0_, lc=lc, L=L, Lb=Lb, nb=nb: e.copy(hl[:, c0_:c0_ + 32 * (nb - 1) + 1:32], hh[d][:, lc:L:Lb]), reads=[TG + 'hh%d' % d], writes=['hl'])
                yield
                P.op('pool', lambda e, YR=YR, L=L: e.tensor_tensor(tmp[:, 0:L], YR[:, 0:L], YR[:, 0:L], ALU.mult), reads=[ky], writes=[TG + 'rtmp'])
                P.op('pool', lambda e, L=L: e.tensor_scalar(tmp[:, 0:L], tmp[:, 0:L], 0.044715, 1.0, ALU.mult, ALU.add), reads=[TG + 'rtmp'], writes=[TG + 'rtmp'])
                P.op('pool', lambda e, YR=YR, L=L: e.tensor_tensor(tmp[:, 0:L], tmp[:, 0:L], YR[:, 0:L], ALU.mult), reads=[TG + 'rtmp', ky], writes=[TG + 'rtmp'])
                P.op('act', lambda e, L=L: e.activation(tmp[:, 0:L], tmp[:, 0:L], AF.Sigmoid, scale=1.5957691216), reads=[TG + 'rtmp'], writes=[TG + 'rtmp'])
                P.op('pool', lambda e, YR=YR, L=L: e.tensor_tensor(tmp[:, 0:L], tmp[:, 0:L], YR[:, 0:L], ALU.mult), reads=[TG + 'rtmp', ky], writes=[TG + 'rtmp'])
                P.op('dve', lambda e, L=L: e.tensor_tensor(hh[0][:, 0:L], hh[0][:, 0:L], hh[1][:, 0:L], ALU.add), reads=[TG + 'hh0', TG + 'hh1'], writes=[TG + 'hh0'])
                P.op('dve', lambda e, L=L: e.tensor_tensor(ybf[:, 0:L], hh[0][:, 0:L], tmp[:, 0:L], ALU.mult), reads=[TG + 'hh0', TG + 'rtmp'], writes=[TG + 'ybf'])
                P.op('act', lambda e, fc=fc, t0=t0, n=n, L=L: e.dma_start(out=actT[t0:t0 + n, :, 16 + fc, :].rearrange("t p n -> p t n"),
                                                                       in_=ybf[:, 0:L].rearrange("p (t n) -> p t n", n=128)),
                     reads=[TG + 'ybf'], writes=['actT'], dma=True)
                yield

        gens = [chain(0, NP, E.LP, 0, bufsets[0], 'A'), chain(NP * E.LP // 128, 1, LS, 1, bufsets[1], 'B')]
        if env['mode'] in ('full', 'even'):
            wb1 = [P.sb("adb%d" % i, [128, KC, 128], BF16, st) for i in range(2)]
            gens.append(E.adaln_gen(1, wb1, 'adb', 128, psT[1][:].bitcast(F32), 'psT1'))
        alive = list(gens)
        while alive:
            for g_ in list(alive):
                try:
                    next(g_)
                except StopIteration:
                    alive.remove(g_)
        pm = rM.next()
        P.op('pe', lambda e, pm=pm: e.matmul(psM[pm][0:NP * 32, 0:128], hl[:], identf, start=True, stop=True), reads=['hl', 'cst'], writes=['psM%d' % pm])
        P.op('act', lambda e, pm=pm: e.copy(hlo[0:NP * 32, :], psM[pm][0:NP * 32, 0:128]), reads=['psM%d' % pm], writes=['hlo'])
        P.op('sp', lambda e: e.dma_start(out=E.nr_out, in_=hlo[0:NP * 32, :]), reads=['hlo'], writes=['nr_out'], dma=True)
    P.barrier()


def odd_proj(env, hT):
    E = _NS(env)
    P, T = E.P, E.T
    sc = E.scr
    G = env['od'] = {}
    G['ztm'] = sc("ztm", [T, 4096], BF16)
    G['xbcT'] = sc("xbcT", [48, 128, T], BF16)
    G['dtT'] = sc("dtT", [128, T], F32)
    W = E.od_w_in
    E.proj_tm(hT, W, [[(OD_Z + b * 512, 512)] for b in range(8)],
              _dst(lambda b, i: (G['ztm'][i * 128:(i + 1) * 128, b * 512:(b + 1) * 512], 'ztm'), BF16), wname='oz')
    E.proj_fm(hT, W, [([(OD_XBC + b * 512, 512)], 4) for b in range(12)],
              _dst(lambda c, g: (G['xbcT'][c, :, g * 512:(g + 1) * 512], 'xbcT'), BF16), wname='ox')
    E.proj_fm(hT, W, [([(OD_DT, 128)], 1)],
              _dst(lambda c, g: (G['dtT'][:, g * 512:(g + 1) * 512], 'dtT'), F32), wname='od')


def odd_mix(env):
    E = _NS(env)
    P, nc, T, NT, LS, S0, NP = E.P, E.nc, E.T, E.NT, E.LS, E.S0, E.NP
    G = env['od']
    psM, rM, psT, rT = E.psM, E.rM, E.psT, E.rT
    identb, identf, onesf, INC, EXC = E.identb, E.identf, E.onesf, E.INC, E.EXC
    actT = E.actT
    xtm = E.scr("xtm", [T, 4096], BF16)
    Btm = E.scr("Btm", [T, 1024], BF16)
    y_f = E.scr("y_f", [T, 4096], F32)
    Lmax = max(max(n for (_, n, _) in E.seqs) * 128, NP * E.LP)
    nmax = Lmax // 128
    with ExitStack() as st:
        cw = P.sb("scw", [128, 48, 4], F32, st)
        cbs = P.sb("scb", [128, 48], F32, st)
        xp = [P.sb("sxp%d" % i, [128, max(Lmax + 3, NP * (E.LP + 3))], BF16, st) for i in range(2)]
        xc = [P.sb("sxc%d" % i, [128, Lmax], F32, st) for i in range(2)]
        xs = [P.sb("sxs%d" % i, [128, Lmax], BF16, st) for i in range(4)]
        asm = P.sb("sasm", [128, nmax, 512], BF16, st)
        P.op('sp', lambda e: e.dma_start(out=cw[:], in_=E.ssd_cwT), writes=['scw'], dma=True)
        P.op('sp', lambda e: e.dma_start(out=cbs[:], in_=E.ssd_cbT), writes=['scb'], dma=True)
        for (t0, nb, Lb, var) in ((0, NP, E.LP, 0), (NP * E.LP // 128, 1, LS, 1)):
            L = nb * Lb
            n = L // 128
            tk = slice(t0 * 128, t0 * 128 + L)
            for c4 in range(12):
                for cc in range(4):
                    c = c4 * 4 + cc
                    b = c % 2
                    XP = xp[b][:, 0:nb * (Lb + 3)].rearrange("p (b l) -> p b l", b=nb)
                    xc3 = xc[b][:, 0:L].rearrange("p (b l) -> p b l", b=nb)
                    P.op('pool', lambda e, XP=XP: e.memset(XP[:, :, 0:2], 0.0), writes=['sxp%d' % b])
                    P.op('pool', lambda e, XP=XP, Lb=Lb: e.memset(XP[:, :, Lb + 2:Lb + 3], 0.0), writes=['sxp%d' % b])
                    P.op('sp', lambda e, XP=XP, c=c, tk=tk, Lb=Lb, nb=nb: e.dma_start(out=XP[:, :, 2:2 + Lb], in_=G['xbcT'][c, :, tk].rearrange("p (b l) -> p b l", b=nb)), writes=['sxp%d' % b], dma=True)
                    P.op('act', lambda e, XP=XP, xc3=xc3, c=c, Lb=Lb: e.activation(xc3, XP[:, :, 0:Lb], AF.Identity, bias=cbs[:, c:c + 1], scale=cw[:, c, 0:1]),
                         reads=['sxp%d' % b, 'scw', 'scb'], writes=['sxc%d' % b])
                    for j in range(1, 4):
                        eng = 'dve'
                        P.op(eng, lambda e, XP=XP, xc3=xc3, c=c, Lb=Lb, j=j: e.scalar_tensor_tensor(xc3, XP[:, :, j:j + Lb], cw[:, c, j:j + 1], xc3, ALU.mult, ALU.add),
                             reads=['sxp%d' % b, 'scw', 'sxc%d' % b], writes=['sxc%d' % b])
                    P.op('act', lambda e, b=b, cc=cc, L=L: e.activation(xs[cc][:, 0:L], xc[b][:, 0:L], AF.Silu), reads=['sxc%d' % b], writes=['sxs%d' % cc])
                    if c >= 32:
                        P.op('act', lambda e, cc=cc, c=c, tk=tk, L=L: e.dma_start(out=G['xbcT'][c, :, tk], in_=xs[cc][:, 0:L]), reads=['sxs%d' % cc], writes=['xbcT'], dma=True)
                if c4 < 10:
                    for i in range(n):
                        pt = rT.next()
                        for cc in range(4):
                            P.op('pe', lambda e, cc=cc, i=i, pt=pt: e.transpose(psT[pt][:, cc * 128:(cc + 1) * 128], xs[cc][:, i * 128:(i + 1) * 128], identb[:]),
                                 reads=['sxs%d' % cc, 'identb'], writes=['psT%d' % pt])
                        if i % 2 == 0:
                            P.op('act', lambda e, i=i, pt=pt: e.copy(asm[:, i, :], psT[pt][:, 0:512]), reads=['psT%d' % pt], writes=['sasm'])
                        else:
                            P.op('dve', lambda e, i=i, pt=pt: e.tensor_copy(asm[:, i, :], psT[pt][:, 0:512]), reads=['psT%d' % pt], writes=['sasm'])
                    if c4 < 8:
                        dst = xtm[tk, c4 * 512:(c4 + 1) * 512]
                    else:
                        dst = Btm[tk, (c4 - 8) * 512:(c4 - 7) * 512]
                    P.op('act', lambda e, dst=dst, n=n: e.dma_start(out=dst.rearrange("(t p) f -> p t f", p=128), in_=asm[:, 0:n, :]),
                         reads=['sasm'], writes=['xtm'], dma=True)
    P.barrier()
    SK = ['sS%d' % g for g in range(8)]
    SBK = ['sSbf%d' % g for g in range(8)]
    YK = ['syt%d' % g for g in range(8)]
    with ExitStack() as st:
        dtb = P.sb("dtb", [128, 4], F32, st)
        dbc = P.sb("dbc", [128, 64], F32, st)
        ngbc = P.sb("sngbc", [128, 4096], F32, st)
        dl = P.sb("dl", [128, nmax, 256], F32, st)
        S = P.sb("sS", [128, 8, 512], F32, st)
        Sbf = P.sb("sSbf", [128, 8, 512], BF16, st)
        xt = P.sb("sxt", [128, 4096], BF16, st)
        xdt = P.sb("sxdt", [128, 4096], BF16, st)
        xdtw = P.sb("sxdtw", [128, 4096], BF16, st)
        Bt = P.sb("sBt", [128, 1024], BF16, st)
        BT = P.sb("sBT", [128, 8, 128], BF16, st)
        CT = P.sb("sCT", [128, 8, 128], BF16, st)
        yt = P.sb("syt", [128, 4096], F32, st)
        yft = P.sb("syft", [128, 4096], F32, st)
        dtt, lat = yft[:, 0:2048], yft[:, 2048:4096]
        zt = P.sb("szt", [128, 4096], BF16, st)
        sz = zt
        ex = P.sb("sex", [128, 192], F32, st)
        cbm8 = P.sb("scbm8", [128, 8, 128], F32, st)
        Lm = [P.sb("sLm%d" % i, [128, 4, 128], F32, st) for i in range(4)]
        Ee = [P.sb("sEe%d" % i, [128, 4, 128], F32, st) for i in range(4)]
        MT = [P.sb("sMT%d" % i, [128, 4, 128], BF16, st) for i in range(4)]
        tmp = [P.sb("stmp%d" % i, [128, 512], F32, st) for i in range(2)]
        junk = P.sb("sjunk", [128, 512], BF16, st)
        sm = P.sb("ssm", [128, 32], F32, st)
        nso = P.sb("snso", [128, 4, 128], F32, st)
        rL = Ring([0, 1])
        P.op('sp', lambda e: e.dma_start(out=dtb[:, 0:1], in_=E.ssd_dtbT), writes=['dtb'], dma=True)
        P.op('sp', lambda e: e.dma_start(out=dtb[:, 1:2], in_=E.ssd_alogT), writes=['dtb'], dma=True)
        P.op('sp', lambda e: e.dma_start(out=dbc[:], in_=E.ssd_d.partition_broadcast(128)), writes=['dbc'], dma=True)
        P.op('sp', lambda e: e.dma_start(out=ngbc[:], in_=E.ssd_ng.partition_broadcast(128)), writes=['sngbc'], dma=True)
        P.op('act', lambda e: e.activation(dtb[:, 2:3], dtb[:, 1:2], AF.Exp), reads=['dtb'], writes=['dtb'])
        P.op('dve', lambda e: e.tensor_scalar(dtb[:, 2:3], dtb[:, 2:3], -1.0, None, ALU.mult), reads=['dtb'], writes=['dtb'])
        for si, (t0, n, var) in enumerate(E.seqs):
            L = n * 128
            tk = slice(t0 * 128, t0 * 128 + L)
            P.op('sp', lambda e, tk=tk, L=L: e.dma_start(out=dtt[:, 0:L], in_=G['dtT'][:, tk]), writes=['syft'], dma=True)
            P.op('act', lambda e, L=L: e.activation(dtt[:, 0:L], dtt[:, 0:L], AF.Exp, bias=dtb[:, 0:1]), reads=['syft', 'dtb'], writes=['syft'])
            P.op('act', lambda e, L=L: e.activation(dtt[:, 0:L], dtt[:, 0:L], AF.Ln, bias=1.0), reads=['syft'], writes=['syft'])
            P.op('dve', lambda e, L=L: e.tensor_scalar(lat[:, 0:L], dtt[:, 0:L], dtb[:, 2:3], None, ALU.mult), reads=['syft', 'dtb'], writes=['syft'])
            for i in range(n):
                pm = rM.next()
                P.op('pe', lambda e, i=i, pm=pm: e.matmul(psM[pm][:, 0:128], dtt[:, i * 128:(i + 1) * 128], identf, start=True, stop=True),
                     reads=['syft', 'cst'], writes=['psM%d' % pm])
                P.op('pe', lambda e, i=i, pm=pm: e.matmul(psM[pm][:, 128:256], lat[:, i * 128:(i + 1) * 128], identf, start=True, stop=True),
                     reads=['syft', 'cst'], writes=['psM%d' % pm])
                P.op('act', lambda e, i=i, pm=pm: e.copy(dl[:, i, :], psM[pm][:, 0:256]), reads=['psM%d' % pm], writes=['dl'])
            for d in range(2):
                if var == 1:
                    P.op('sp', lambda e, d=d: e.dma_start(out=S[:], in_=E.ssT_in[d].rearrange("g n f -> n g f")), writes=SK, dma=True)
                else:
                    P.op('dve', lambda e: e.memset(S[:], 0.0), writes=SK)
                P.op('act', lambda e: e.copy(Sbf[:], S[:]), reads=SK, writes=SBK)
                order = range(n) if d == 0 else range(n - 1, -1, -1)
                for il in order:
                    i = t0 + il
                    tok = slice(i * 128, (i + 1) * 128)
                    dt_ = dl[:, il, d * 64:(d + 1) * 64]
                    la_ = dl[:, il, 128 + d * 64:128 + (d + 1) * 64]
                    P.op('sp', lambda e, tok=tok: e.dma_start(out=xt[:], in_=xtm[tok, :]), writes=['sxt'], dma=True)
                    P.op('sp', lambda e, tok=tok: e.dma_start(out=Bt[:], in_=Btm[tok, :]), writes=['sBt'], dma=True)
                    P.op('sp', lambda e, tok=tok: e.dma_start(out=BT[:], in_=G['xbcT'][32:40, :, tok].rearrange("g p n -> p g n")), writes=['sBT'], dma=True)
                    P.op('sp', lambda e, tok=tok: e.dma_start(out=CT[:], in_=G['xbcT'][40:48, :, tok].rearrange("g p n -> p g n")), writes=['sCT'], dma=True)
                    if d == 1:
                        P.op('sp', lambda e, tok=tok: e.dma_start(out=yft[:], in_=y_f[tok, :]), reads=['y_f'], writes=['syft'], dma=True)
                        P.op('sp', lambda e, tok=tok: e.dma_start(out=zt[:], in_=G['ztm'][tok, :]), writes=['szt'], dma=True)
                    pm = rM.next()
                    P.op('pe', lambda e, d=d, la_=la_, pm=pm: e.matmul(psM[pm][:, 0:64], INC[d], la_, start=True, stop=True), reads=['cst', 'dl'], writes=['psM%d' % pm])
                    P.op('pe', lambda e, d=d, la_=la_, pm=pm: e.matmul(psM[pm][:, 64:128], EXC[d], la_, start=True, stop=True), reads=['cst', 'dl'], writes=['psM%d' % pm])
                    P.op('pe', lambda e, la_=la_, pm=pm: e.matmul(psM[pm][:, 128:192], onesf, la_, start=True, stop=True), reads=['cst', 'dl'], writes=['psM%d' % pm])
                    P.op('act', lambda e, pm=pm: e.activation(ex[:], psM[pm][:, 0:192], AF.Exp), reads=['psM%d' % pm], writes=['sex'])
                    P.op('dve', lambda e, dt_=dt_: e.tensor_tensor(xdt[:].rearrange("p (h q) -> p h q", q=64), xt[:].rearrange("p (h q) -> p h q", q=64),
                                                                 dt_.unsqueeze(2).broadcast_to([128, 64, 64]), ALU.mult), reads=['sxt', 'dl'], writes=['sxdt'])
                    P.op('pool', lambda e: e.tensor_tensor(xdtw[:].rearrange("p (h q) -> p h q", q=64), xdt[:].rearrange("p (h q) -> p h q", q=64),
                                                        ex[:, 64:128].unsqueeze(2).broadcast_to([128, 64, 64]), ALU.mult), reads=['sxdt', 'sex'], writes=['sxdtw'])
                    PUb = psT[1][:].bitcast(F32)
                    for half in range(2):
                        pcb = rM.next()
                        for gg in range(4):
                            g = half * 4 + gg
                            P.op('pe', lambda e, g=g, gg=gg, pcb=pcb: e.matmul(psM[pcb][:, gg * 128:(gg + 1) * 128], BT[:, g, :], CT[:, g, :], start=True, stop=True),
                                 reads=['sBT', 'sCT'], writes=['psM%d' % pcb])
                        P.op('dve', lambda e, d=d, half=half, pcb=pcb: e.tensor_tensor(cbm8[:, half * 4:(half + 1) * 4, :], psM[pcb][:].rearrange("p (a b) -> p a b", a=4),
                                                                                    INC[d].unsqueeze(1).broadcast_to([128, 4, 128]), ALU.mult),
                             reads=['psM%d' % pcb], writes=['scbm'])

                    def stA1(g):
                        cs = g % 4
                        cb_ = g % 2
                        for hb in range(2):
                            h0 = g * 8 + hb * 4
                            k = (g % 2) * 2 + hb
                            P.op('dve', lambda e, d=d, k=k, la_=la_, h0=h0: e.tensor_tensor(Lm[k][:], EXC[d].unsqueeze(1).broadcast_to([128, 4, 128]),
                                                                                        la_[:, h0:h0 + 4].unsqueeze(2).broadcast_to([128, 4, 128]), ALU.mult),
                                 reads=['dl'], writes=['sLm%d' % k])
                            pg = hb
                            for j in range(4):
                                P.op('pe', lambda e, d=d, k=k, j=j, pg=pg: e.matmul(psM[pg][:, j * 128:(j + 1) * 128], Lm[k][:, j, :], INC[d], start=True, stop=True),
                                     reads=['sLm%d' % k], writes=['psM%d' % pg])
                            P.op('act', lambda e, k=k, pg=pg: e.activation(Ee[k][:], psM[pg][:].rearrange("p (a b) -> p a b", a=4), AF.Exp), reads=['psM%d' % pg], writes=['sEe%d' % k])
                            P.op('pool', lambda e, k=k, g=g: e.tensor_tensor(MT[k][:], Ee[k][:], cbm8[:, g, :].unsqueeze(1).broadcast_to([128, 4, 128]), ALU.mult),
                                 reads=['sEe%d' % k, 'scbm'], writes=['sMT%d' % k])

                    def stA2(g):
                        py, pi = 2 + g % 2, 4 + g % 2
                        for hb in range(2):
                            k = (g % 2) * 2 + hb
                            for j in range(4):
                                h = g * 8 + hb * 4 + j
                                jj = hb * 4 + j
                                P.op('pe', lambda e, k=k, j=j, jj=jj, h=h, py=py: e.matmul(psM[py][:, jj * 64:(jj + 1) * 64], MT[k][:, j, :], xdt[:, h * 64:(h + 1) * 64], start=True, stop=True),
                                     reads=['sMT%d' % k, 'sxdt'], writes=['psM%d' % py])
                        P.op('pe', lambda e, g=g, pi=pi: e.matmul(psM[pi][:], CT[:, g, :], Sbf[:, g, :], start=True, stop=True), reads=['sCT', 'sSbf%d' % g], writes=['psM%d' % pi])
                        P.op('pe', lambda e, g=g: e.matmul(PUb[:, :], Bt[:, g * 128:(g + 1) * 128], xdtw[:, g * 512:(g + 1) * 512], start=True, stop=True),
                             reads=['sBt', 'sxdtw'], writes=['psT1'])

                    def stB(g):
                        py, pi = 2 + g % 2, 4 + g % 2
                        tb_ = g % 2
                        P.op('dve', lambda e, g=g, pi=pi, tb_=tb_: e.tensor_tensor(tmp[tb_][:].rearrange("p (h q) -> p h q", q=64), psM[pi][:].rearrange("p (h q) -> p h q", q=64),
                                                                                ex[:, g * 8:(g + 1) * 8].unsqueeze(2).broadcast_to([128, 8, 64]), ALU.mult),
                             reads=['psM%d' % pi, 'sex'], writes=['stmp%d' % tb_])
                        ysl = yt[:, g * 512:(g + 1) * 512]
                        P.op('dve', lambda e, py=py, tb_=tb_, ysl=ysl: e.tensor_tensor(ysl, tmp[tb_][:], psM[py][:], ALU.add), reads=['psM%d' % py, 'stmp%d' % tb_], writes=['syt%d' % g])
                        if d == 1:
                            P.op('pool', lambda e, g=g, ysl=ysl: e.tensor_tensor(ysl, ysl, yft[:, g * 512:(g + 1) * 512], ALU.add), reads=['syt%d' % g, 'syft'], writes=['syt%d' % g])
                        P.op('dve', lambda e, g=g: e.tensor_tensor(S[:, g, :].rearrange("p (h q) -> p h q", q=64), S[:, g, :].rearrange("p (h q) -> p h q", q=64),
                                                                ex[:, 128 + g * 8:128 + (g + 1) * 8].unsqueeze(2).broadcast_to([128, 8, 64]), ALU.mult),
                             reads=['sS%d' % g, 'sex'], writes=['sS%d' % g])
                        P.op('dve', lambda e, g=g: e.tensor_tensor(S[:, g, :], S[:, g, :], PUb[:, :], ALU.add), reads=['sS%d' % g, 'psT1'], writes=['sS%d' % g])
                        P.op('act', lambda e, g=g: e.copy(Sbf[:, g, :], S[:, g, :]), reads=['sS%d' % g], writes=['sSbf%d' % g])

                    for s_ in range(10):
                        if s_ < 8:
                            stA1(s_)
                        if 2 <= s_:
                            stB(s_ - 2)
                        if 1 <= s_ < 9:
                            stA2(s_ - 1)
                    if d == 0:
                        P.op('sp', lambda e, tok=tok: e.dma_start(out=y_f[tok, :], in_=yt[:]), reads=YK, writes=['y_f'], dma=True)
                    else:
                        mo, moT = xdt, xdtw
                        P.op('dve', lambda e: e.tensor_tensor(yft[:].rearrange("p (h q) -> p h q", q=64), xt[:].rearrange("p (h q) -> p h q", q=64),
                                                           dbc[:].unsqueeze(2).broadcast_to([128, 64, 64]), ALU.mult), reads=['sxt', 'dbc'] + YK, writes=['syft'])
                        P.op('pool', lambda e: e.tensor_tensor(yt[:], yt[:], yft[:], ALU.add), reads=YK + ['syft'], writes=YK)
                        P.op('act', lambda e: e.activation(sz[:], zt[:], AF.Silu), reads=['szt'], writes=['szt'])
                        P.op('dve', lambda e: e.tensor_tensor(yt[:], yt[:], sz[:], ALU.mult), reads=YK + ['szt'], writes=YK)
                        for g in range(8):
                            P.op('act', lambda e, g=g: e.activation(junk[:], yt[:, g * 512:(g + 1) * 512], AF.Square, accum_out=sm[:, g:g + 1]), reads=YK, writes=['sjunk', 'ssm'])
                        P.op('dve', lambda e: e.tensor_scalar(sm[:, 8:16], sm[:, 0:8], 1.0 / 512, EPS, ALU.mult, ALU.add), reads=['ssm'], writes=['ssm'])
                        P.op('act', lambda e: e.sqrt(sm[:, 16:24], sm[:, 8:16]), reads=['ssm'], writes=['ssm'])
                        P.op('dve', lambda e: e.reciprocal(sm[:, 24:32], sm[:, 16:24]), reads=['ssm'], writes=['ssm'])
                        P.op('dve', lambda e: e.tensor_tensor(yt[:].rearrange("p (g q) -> p g q", q=512), yt[:].rearrange("p (g q) -> p g q", q=512),
                                                           sm[:, 24:32].unsqueeze(2).broadcast_to([128, 8, 512]), ALU.mult), reads=YK + ['ssm'], writes=YK)
                        P.op('dve', lambda e: e.tensor_tensor(mo[:], yt[:], ngbc[:], ALU.mult), reads=YK + ['sngbc', 'sxdtw'], writes=['sxdt'])
                        for h8 in range(4):
                            pt = rT.next()
                            for j in range(8):
                                kc = h8 * 8 + j
                                P.op('pe', lambda e, kc=kc, j=j, pt=pt: e.transpose(psT[pt][:, j * 128:(j + 1) * 128], mo[:, kc * 128:(kc + 1) * 128], identb[:]),
                                     reads=['sxdt', 'identb'], writes=['psT%d' % pt])
                            P.op('act', lambda e, h8=h8, pt=pt: e.copy(moT[:, h8 * 1024:(h8 + 1) * 1024], psT[pt][:]), reads=['psT%d' % pt], writes=['sxdtw'])
                        P.op('sp', lambda e, i=i: e.dma_start(out=actT[i, :, 0:32, :], in_=moT[:].rearrange("p (a b) -> p a b", b=128)), reads=['sxdtw'], writes=['actT'], dma=True)
                if var == 0:
                    for g in range(8):
                        pm = rM.next()
                        for q in range(4):
                            P.op('pe', lambda e, g=g, q=q, pm=pm: e.matmul(psM[pm][:, q * 128:(q + 1) * 128], S[:, g, q * 128:(q + 1) * 128], identf, start=True, stop=True),
                                 reads=SK + ['cst'], writes=['psM%d' % pm])
                        P.op('act', lambda e, pm=pm: e.copy(nso[:], psM[pm][:].rearrange("p (a b) -> p a b", a=4)), reads=['psM%d' % pm], writes=['snso'])
                        P.op('sp', lambda e, si=si, d=d, g=g: e.dma_start(out=E.ns_out[si, d, g * 8:(g + 1) * 8].rearrange("(q jj) p n -> (jj p) q n", jj=2), in_=nso[:]),
                             reads=['snso'], writes=['ns_out'], dma=True)
    P.barrier()


def _consts():
    s = np.arange(128)[:, None]
    t = np.arange(128)[None, :]
    c = np.zeros((8, 128, 128), np.float32)
    c[0] = np.eye(128)
    c[1] = 1.0
    c[2] = (s <= t)
    c[3] = (s >= t)
    c[4] = (s > t)
    c[5] = (s < t)
    c[6] = -1.0 * (s <= t) / 16.0
    c[7] = -1.0 * (s >= t) / 16.0
    return c


def _rope_tables(LS):
    GRID_W = 64
    rows = LS // GRID_W
    row = np.repeat(np.arange(rows), GRID_W).astype(np.float32)
    col = np.tile(np.arange(GRID_W), rows).astype(np.float32)
    nf = 64
    inv = (np.float32(10000.0) ** (-np.arange(nf, dtype=np.float32) / nf)).astype(np.float32)
    out = np.zeros((4, 128, LS), np.float32)
    for i, pos in enumerate((row, col)):
        ang = pos[None, :] * inv[:, None]
        cos, sin = np.cos(ang), np.sin(ang)
        out[2 * i, :64], out[2 * i, 64:] = cos, cos
        out[2 * i + 1, :64], out[2 * i + 1, 64:] = -sin, sin
    return out


def _fm(a):
    a = np.asarray(a, np.float32)
    lead = a.shape[:-1]
    k = a.shape[-1] // 128
    a = a.reshape(lead + (k, 128))
    return np.ascontiguousarray(np.moveaxis(a, -1, 0))


def prep(inputs, NP, LP, LS, ncores):
    I = {k: np.asarray(v) for k, v in inputs.items()}
    shared = {
        'w_ada': I['w_ada'], 'b_adaT': np.ascontiguousarray(I['b_ada'].reshape(2, 96, 128).transpose(0, 2, 1)),
        'ngT': np.ascontiguousarray(I['norm_g'].reshape(2, 4, KC, 128).transpose(0, 3, 1, 2)),
        'ev_w_in': I['ev_w_in'][0], 'ev_w_out': I['ev_w_out'][0], 'gla_w_up': I['gla_w_up'][0],
        'gla_b_up': I['gla_b_up'][0], 'gla_ng': I['gla_norm_g'][0][None],
        'rnn_cwT': np.ascontiguousarray(I['rnn_conv_w'][0].reshape(4, KC, 128).transpose(2, 1, 0)),
        'rnn_cbT': _fm(I['rnn_conv_b'][0]), 'rnn_w_r': I['rnn_w_r'][0], 'rnn_w_i': I['rnn_w_i'][0],
        'rnn_brT': _fm(I['rnn_b_r'][0]), 'rnn_biT': _fm(I['rnn_b_i'][0]), 'rnn_lamT': _fm(I['rnn_lam'][0]),
        'od_w_in': I['od_w_in'][0], 'od_w_out': I['od_w_out'][0],
        'ssd_cwT': np.ascontiguousarray(I['ssd_conv_w'][0].reshape(4, 48, 128).transpose(2, 1, 0)),
        'ssd_cbT': _fm(I['ssd_conv_b'][0]),
        'ssd_dtbT': np.ascontiguousarray(I['ssd_dt_bias'][0].reshape(128, 1)),
        'ssd_alogT': np.ascontiguousarray(I['ssd_a_log'][0].reshape(128, 1)),
        'ssd_d': I['ssd_d'][0][None], 'ssd_ng': I['ssd_norm_g'][0][None],
        'ffn_wg': I['ffn_w_gate'], 'ffn_wu': I['ffn_w_up'], 'ffn_wd': I['ffn_w_down'],
        'consts': _consts(), 'rope': _rope_tables(LS),
    }
    maps = []
    for c in range(ncores):
        xp = I['x_prompt'][c * NP:(c + 1) * NP].reshape(NP * LP, D)
        xs = I['x_sample'][c]
        cv = np.stack([I['c_ctx'], I['c'][c]], 0)
        m = dict(shared)
        m['x'] = np.ascontiguousarray(np.concatenate([xp, xs], 0))
        m['cT'] = np.ascontiguousarray(cv.reshape(2, KC, 128).transpose(2, 1, 0))
        m['sg'] = np.ascontiguousarray(I['state_gla'][c, 0])
        m['srT'] = _fm(I['state_rglru'][c, 0])
        ss = I['state_ssd'][c, 0].reshape(2, 8, 8, 64, 128)
        m['ssT'] = np.ascontiguousarray(ss.transpose(0, 1, 4, 2, 3).reshape(2, 8, 128, 512))
        maps.append(m)
    return maps


_NC_CACHE = {}


def kernel(**inputs):
    NP, LP, LS, ncores = 4, 256, 2048, 8
    key = (NP, LP, LS)
    if key not in _NC_CACHE:
        _NC_CACHE[key] = build(NP, LP, LS)
    nc = _NC_CACHE[key]
    maps = prep(inputs, NP, LP, LS, ncores)
    res = run_bass_kernel_spmd(nc, maps, core_ids=list(range(ncores)))
    R = res.results
    T = NP * LP + LS
    yp = np.concatenate([r['y'][:NP * LP].reshape(NP, LP, D) for r in R], 0)
    ys = np.stack([r['y'][NP * LP:] for r in R], 0)
    ng = np.concatenate([r['ng'] for r in R], 0)[:, None]
    nr = np.concatenate([r['nr'].reshape(NP, 2, D) for r in R], 0)[:, None]
    ns = np.concatenate([r['ns'] for r in R], 0)[:, None]
    return (yp.astype(np.float32), ys.astype(np.float32), ng.astype(np.float32), nr.astype(np.float32), ns.astype(np.float32))
```

```python
import math
import numpy as np
from contextlib import ExitStack
import concourse.bass as bass
import concourse.mybir as mybir
from concourse.bass_utils import run_bass_kernel_spmd

F32 = mybir.dt.float32
BF16 = mybir.dt.bfloat16
AF = mybir.ActivationFunctionType
ALU = mybir.AluOpType

D = 2048
KC = 16
DFF = 5632
EPS = 1e-6
SAME_ENGINE_SYNC = True
EPOCH = 16000
N_EPOCH = {'pe': 8, 'act': 6, 'dve': 6, 'pool': 3, 'sp': 3}
N_DMA = {'sp': 16, 'pool': 8, 'act': 8}


class Prog:
    ENG = ['pe', 'act', 'dve', 'pool', 'sp']

    def __init__(self, nc):
        self.nc = nc
        self.st = ExitStack()
        self.eng = {'pe': nc.tensor, 'act': nc.scalar, 'dve': nc.vector, 'pool': nc.gpsimd, 'sp': nc.sync}
        self.ecount = {e: 0 for e in self.ENG}
        self.seen = {e: {} for e in self.ENG}
        self.regs = {}
        self.dma_val = {}
        self.dma_rr = {e: 0 for e in self.ENG}
        self.sems = {}
        for e in self.ENG:
            for i in range(N_EPOCH[e]):
                k = 'e_%s_%d' % (e, i)
                self.sems[k] = self.st.enter_context(nc.semaphore(k))
        for e, n in N_DMA.items():
            for i in range(n):
                k = 'd_%s_%d' % (e, i)
                self.sems[k] = self.st.enter_context(nc.semaphore(k))
        self.nops = 0
        self.nwaits = 0

    def sb(self, name, shape, dt, st=None):
        self.nalloc = getattr(self, 'nalloc', 0) + 1
        return (st or self.st).enter_context(self.nc.sbuf_tensor("%s_%d" % (name, self.nalloc), list(shape), dt))

    def ps(self, name, shape, dt, st=None):
        return (st or self.st).enter_context(self.nc.psum_tensor(name, list(shape), dt))

    def _wait(self, eng, k, v):
        if self.seen[eng].get(k, 0) >= v:
            return
        self.seen[eng][k] = v
        self.eng[eng].wait_ge(self.sems[k], v)
        self.nwaits += 1

    def op(self, eng, fn, reads=(), writes=(), dma=False):
        psr = [k for k in reads if k.startswith('ps')]
        if psr:
            reads = [k for k in reads if not k.startswith('ps')]
            writes = list(writes) + psr
        deps = []
        for k in reads:
            r = self.regs.get(k)
            if r and r['w']:
                deps.append(r['w'])
        for k in writes:
            r = self.regs.get(k)
            if r:
                if r['w']:
                    deps.append(r['w'])
                deps.extend(r['r'].items())
        if dma:
            sk = 'd_%s_%d' % (eng, self.dma_rr[eng] % N_DMA[eng])
            self.dma_rr[eng] += 1
            prev = self.dma_val.get(sk, 0)
            if prev:
                deps.append((sk, prev))
            self.dma_val[sk] = prev + 16
            tok = (sk, prev + 16)
            inc = 16
        else:
            c = self.ecount[eng]
            self.ecount[eng] = c + 1
            assert c // EPOCH < N_EPOCH[eng], eng
            sk = 'e_%s_%d' % (eng, c // EPOCH)
            tok = (sk, c % EPOCH + 1)
            inc = 1
        own = 'e_%s_' % eng
        for (k, v) in deps:
            if k.startswith(own) and (eng == 'pe' or not SAME_ENGINE_SYNC):
                continue
            self._wait(eng, k, v)
        fn(self.eng[eng]).then_inc(self.sems[sk], inc)
        self.nops += 1
        for k in reads:
            r = self.regs.setdefault(k, {'w': None, 'r': {}})
            if r['r'].get(tok[0], 0) < tok[1]:
                r['r'][tok[0]] = tok[1]
        for k in writes:
            self.regs[k] = {'w': tok, 'r': {}}

    def _final_tokens(self):
        final = {}
        for e in self.ENG:
            c = self.ecount[e]
            if c:
                final['e_%s_%d' % (e, (c - 1) // EPOCH)] = (c - 1) % EPOCH + 1
        for k, v in self.dma_val.items():
            final[k] = v
        return final

    def barrier(self):
        final = self._final_tokens()
        for e in self.ENG:
            own = 'e_%s_' % e
            for k, v in final.items():
                if k.startswith(own):
                    continue
                self._wait(e, k, v)
        self.regs = {}

    def finish(self):
        final = self._final_tokens()
        for k, v in final.items():
            if k.startswith('e_sp_'):
                continue
            self._wait('sp', k, v)

    def close(self):
        self.st.close()


class Ring:
    def __init__(self, items):
        self.items = items
        self.i = 0

    def next(self):
        it = self.items[self.i % len(self.items)]
        self.i += 1
        return it


EV_Q, EV_K, EV_V, EV_G, EV_LR, EV_XR, EV_YR = 0, 1024, 2048, 4096, 6144, 6176, 8224
OD_Z, OD_XBC, OD_DT = 0, 4096, 10240


def build(NP, LP, LS, debug=False, mode='full'):
    T = NP * LP + LS
    NT = T // 128
    assert T % 512 == 0 and LP % 128 == 0 and LS % 128 == 0
    NG = T // 512
    seqs = [(i * LP // 128, LP // 128, 0) for i in range(NP)] + [(NP * LP // 128, LS // 128, 1)]
    tile_var = []
    for (t0, n, v) in seqs:
        tile_var += [v] * n
    S0 = NP * LP

    nc = bass.Bass("TRN2", target_bir_lowering=False)
    P = Prog(nc)

    def inp(name, shape, dt=F32):
        return nc.dram_tensor(name, list(shape), dt, kind="ExternalInput").ap()

    def outp(name, shape, dt=F32):
        return nc.dram_tensor(name, list(shape), dt, kind="ExternalOutput").ap()

    def scr(name, shape, dt=F32):
        return nc.dram_tensor(name, list(shape), dt, kind=("ExternalOutput" if debug else "Internal")).ap()

    x_in = inp("x", [T, D])
    cT_in = inp("cT", [128, KC, 2])
    sg_in = inp("sg", [2, 4, 256, 512])
    srT_in = inp("srT", [128, 2, KC])
    ssT_in = inp("ssT", [2, 8, 128, 512])
    w_ada = inp("w_ada", [2, D, 6 * D])
    b_adaT = inp("b_adaT", [2, 128, 96])
    ngT = inp("ngT", [2, 128, 4, KC])
    ev_w_in = inp("ev_w_in", [D, 10272])
    ev_w_out = inp("ev_w_out", [4096, D])
    gla_w_up = inp("gla_w_up", [2, 16, 1024])
    gla_b_up = inp("gla_b_up", [2, 1024])
    gla_ng = inp("gla_ng", [1, 2048])
    rnn_cwT = inp("rnn_cwT", [128, KC, 4])
    rnn_cbT = inp("rnn_cbT", [128, KC])
    rnn_w_r = inp("rnn_w_r", [2, 16, 128, 128])
    rnn_w_i = inp("rnn_w_i", [2, 16, 128, 128])
    rnn_brT = inp("rnn_brT", [128, 2, KC])
    rnn_biT = inp("rnn_biT", [128, 2, KC])
    rnn_lamT = inp("rnn_lamT", [128, 2, KC])
    od_w_in = inp("od_w_in", [D, 10368])
    od_w_out = inp("od_w_out", [4096, D])
    ssd_cwT = inp("ssd_cwT", [128, 48, 4])
    ssd_cbT = inp("ssd_cbT", [128, 48])
    ssd_dtbT = inp("ssd_dtbT", [128, 1])
    ssd_alogT = inp("ssd_alogT", [128, 1])
    ssd_d = inp("ssd_d", [1, 64])
    ssd_ng = inp("ssd_ng", [1, 4096])
    ffn_wg = inp("ffn_wg", [2, D, DFF])
    ffn_wu = inp("ffn_wu", [2, D, DFF])
    ffn_wd = inp("ffn_wd", [2, DFF, D])
    consts = inp("consts", [8, 128, 128])
    rope = inp("rope", [4, 128, LS])

    y_out = outp("y", [T, D])
    ng_out = outp("ng", [NP, 2, 4, 256, 512])
    nr_out = outp("nr", [NP * 2 * KC, 128])
    ns_out = outp("ns", [NP, 2, 64, 64, 128])

    xres = scr("xres", [T, D])
    raw = scr("raw", [T, D])
    actT = scr("actT", [NT, 128, 44, 128], BF16)
    dbg = {}

    cst = P.sb("cst", [128, 8, 128], F32)
    identb = P.sb("identb", [128, 128], BF16)
    P.op('sp', lambda e: e.dma_start(out=cst[:], in_=consts.rearrange("c p n -> p c n")), writes=['cst'], dma=True)
    P.op('pool', lambda e: e.dma_start(out=identb[:], in_=consts[0]), writes=['identb'], dma=True)
    identf, onesf = cst[:, 0, :], cst[:, 1, :]
    INC = [cst[:, 2, :], cst[:, 3, :]]
    EXC = [cst[:, 4, :], cst[:, 5, :]]
    GINC = [cst[:, 6, :], cst[:, 7, :]]
    modT = P.sb("modT", [128, 2, 96, 2], F32)
    tabs = P.sb("tabs", [128, 2, 6, KC, 2], F32)
    ngs = P.sb("ngs", [128, 2, 4, KC], F32)
    ggh = [None]
    ssq = P.sb("ssq", [128, NT, 4], F32)
    psM = [P.ps("psM%d" % i, [128, 512], F32) for i in range(6)]
    psT = [P.ps("psT%d" % i, [128, 1024], BF16) for i in range(2)]
    rM = Ring([0, 1, 2, 3, 4, 5])
    rT = Ring([0, 1])

    def wload(wbuf, key, Wd, pieces, kc_n, kc0=0):
        off = 0
        for (c0, w) in pieces:
            src = Wd[kc0 * 128:(kc0 + kc_n) * 128, c0:c0 + w].rearrange("(kc p) n -> p kc n", p=128)
            P.op('pool', lambda e, o=off, w=w, src=src: e.dma_start(out=wbuf[:, 0:kc_n, o:o + w], in_=src),
                 writes=[key], dma=True)
            off += w

    cTs = P.sb("cTs", [128, KC, 2], F32)
    cTb = P.sb("cTb", [128, KC, 2], BF16)
    bad = P.sb("bad", [128, 2, 96], F32)
    P.op('sp', lambda e: e.dma_start(out=cTs[:], in_=cT_in), writes=['cTs'], dma=True)
    P.op('sp', lambda e: e.dma_start(out=bad[:], in_=b_adaT.rearrange("l p c -> p l c")), writes=['bad'], dma=True)
    P.op('sp', lambda e: e.dma_start(out=ngs[:], in_=ngT.rearrange("l p j k -> p l j k")), writes=['ngs'], dma=True)
    P.op('act', lambda e: e.activation(cTb[:], cTs[:], AF.Silu), reads=['cTs'], writes=['cTb'])

    def adaln_gen(l, wb, wk, ncols, pa, pak):
        nsub = ncols // 128
        for blk in range(6 * D // ncols):
            b = blk % 2
            wload(wb[b], wk + '%d' % b, w_ada[l], [(blk * ncols, ncols)], KC)
            for sub in range(nsub):
                cbi = blk * nsub + sub
                for kc in range(KC):
                    P.op('pe', lambda e, b=b, sub=sub, kc=kc, cbi=cbi: e.matmul(
                        pa[:, cbi * 2:cbi * 2 + 2], wb[b][:, kc, sub * 128:(sub + 1) * 128], cTb[:, kc, :],
                        start=(kc == 0), stop=(kc == KC - 1)),
                        reads=[wk + '%d' % b, 'cTb'], writes=[pak])
            yield
        P.op('dve', lambda e: e.tensor_tensor(
            modT[:, l], pa[:, 0:192].rearrange("p (c v) -> p c v", v=2),
            bad[:, l].unsqueeze(2).broadcast_to([128, 96, 2]), ALU.add),
            reads=[pak, 'bad'], writes=['modT'])
        for (ti, mi, gi, kind) in ((0, 1, 0, 's'), (1, 0, None, 'c'), (2, 2, 1, 'g'), (3, 4, 2, 's'), (4, 3, None, 'c'), (5, 5, 3, 'g')):
            src_ = modT[:, l, mi * 16:(mi + 1) * 16, :]
            dst = tabs[:, l, ti]
            if kind == 'c':
                P.op('dve', lambda e, dst=dst, src_=src_: e.tensor_copy(dst, src_), reads=['modT'], writes=['tabs'])
            else:
                gb = ngs[:, l, gi].unsqueeze(2).broadcast_to([128, KC, 2])
                add = 1.0 if kind == 's' else 0.0
                P.op('dve', lambda e, dst=dst, src_=src_, gb=gb, add=add: e.scalar_tensor_tensor(
                    dst, src_, add, gb, ALU.add, ALU.mult), reads=['modT', 'ngs'], writes=['tabs'])
        yield

    eager_layers = (0,) if mode in ('full', 'even') else (0, 1)
    with ExitStack() as st:
        wb_ = [P.sb("adaw%d" % i, [128, KC, 512], BF16, st) for i in range(2)]
        for l in eager_layers:
            pm_ = rM.next()
            for _ in adaln_gen(l, wb_, 'adaw', 512, psM[pm_], 'psM%d' % pm_):
                pass
    P.barrier()

    def build_ggbc(l, ti):
        with ExitStack() as st:
            dg = [P.sb("dg%d" % i, [128, 128], F32, st) for i in range(2)]
            for v in range(2):
                for q4 in range(4):
                    pm = rM.next()
                    for j in range(4):
                        kc = q4 * 4 + j
                        b = kc % 2
                        P.op('dve', lambda e, b=b, kc=kc, v=v: e.tensor_scalar(
                            dg[b][:], identf, tabs[:, l, ti, kc, v:v + 1], None, ALU.mult),
                            reads=['cst', 'tabs'], writes=['dg%d' % b])
                        P.op('pe', lambda e, b=b, j=j, pm=pm: e.matmul(
                            psM[pm][:, j * 128:(j + 1) * 128], onesf, dg[b][:], start=True, stop=True),
                            reads=['cst', 'dg%d' % b], writes=['psM%d' % pm])
                    P.op('act', lambda e, v=v, q4=q4, pm=pm: e.copy(ggh[0][:, v, q4 * 512:(q4 + 1) * 512], psM[pm][:]),
                         reads=['psM%d' % pm], writes=['ggbc'])
            P.barrier()

    def norm_phase(hT, l, which, src):
        ts_, tsh = (0, 1) if which == 0 else (3, 4)
        with ExitStack() as st:
            xt = [P.sb("nx%d" % i, [128, D], F32, st) for i in range(3)]
            xn = [P.sb("nxn%d" % i, [128, D], BF16, st) for i in range(3)]
            junk = P.sb("njunk", [128, D], BF16, st)
            sm = [P.sb("nsm%d" % i, [128, 4], F32, st) for i in range(3)]
            for i in range(NT):
                b = i % 3
                v = tile_var[i]
                P.op('sp', lambda e, b=b, i=i: e.dma_start(out=xt[b][:], in_=src[i * 128:(i + 1) * 128, :]),
                     writes=['nx%d' % b], dma=True)
                P.op('act', lambda e, b=b: e.activation(junk[:], xt[b][:], AF.Square, accum_out=sm[b][:, 0:1]),
                     reads=['nx%d' % b], writes=['njunk', 'nsm%d' % b])
                P.op('dve', lambda e, b=b: e.tensor_scalar(sm[b][:, 1:2], sm[b][:, 0:1], 1.0 / D, EPS, ALU.mult, ALU.add),
                     reads=['nsm%d' % b], writes=['nsm%d' % b])
                P.op('act', lambda e, b=b: e.sqrt(sm[b][:, 3:4], sm[b][:, 1:2]),
                     reads=['nsm%d' % b], writes=['nsm%d' % b])
                P.op('dve', lambda e, b=b: e.reciprocal(sm[b][:, 2:3], sm[b][:, 3:4]),
                     reads=['nsm%d' % b], writes=['nsm%d' % b])
                P.op('dve', lambda e, b=b: e.tensor_scalar(xn[b][:], xt[b][:], sm[b][:, 2:3], None, ALU.mult),
                     reads=['nsm%d' % b, 'nx%d' % b], writes=['nxn%d' % b])
                for h8 in range(2):
                    pt = rT.next()
                    for j in range(8):
                        kc = h8 * 8 + j
                        P.op('pe', lambda e, b=b, kc=kc, j=j, pt=pt: e.transpose(
                            psT[pt][:, j * 128:(j + 1) * 128], xn[b][:, kc * 128:(kc + 1) * 128], identb[:]),
                            reads=['nxn%d' % b, 'identb'], writes=['psT%d' % pt])
                    for j in range(8):
                        kc = h8 * 8 + j
                        if j % 2 == 0:
                            P.op('act', lambda e, kc=kc, j=j, pt=pt, i=i, v=v: e.activation(
                                hT[:, kc, i * 128:(i + 1) * 128], psT[pt][:, j * 128:(j + 1) * 128], AF.Identity,
                                bias=tabs[:, l, tsh, kc, v:v + 1], scale=tabs[:, l, ts_, kc, v:v + 1]),
                                reads=['psT%d' % pt, 'tabs'], writes=['hT%d' % (kc % 2)])
                        else:
                            P.op('dve', lambda e, kc=kc, j=j, pt=pt, i=i, v=v: e.tensor_scalar(
                                hT[:, kc, i * 128:(i + 1) * 128], psT[pt][:, j * 128:(j + 1) * 128],
                                tabs[:, l, ts_, kc, v:v + 1], tabs[:, l, tsh, kc, v:v + 1], ALU.mult, ALU.add),
                                reads=['psT%d' % pt, 'tabs'], writes=['hT%d' % (kc % 2)])
        P.barrier()

    def proj_fm(hT, Wd, blocks, dst_fn, evac='copy', tok0=0, tok1=None, wname='pw'):
        tok1 = T if tok1 is None else tok1
        with ExitStack() as st:
            wb = [P.sb(wname + "%d" % i, [128, KC, 512], BF16, st) for i in range(2)]
            ob = [P.sb(wname + "o%d" % i, [128, 512], dst_fn.dt, st) for i in range(4)]
            ro = Ring([0, 1, 2, 3])
            ci = 0
            for bi, (pieces, nch) in enumerate(blocks):
                b = bi % 2
                wload(wb[b], wname + '%d' % b, Wd, pieces, KC)
                for sub in range(nch):
                    for g in range(tok0 // 512, tok1 // 512):
                        pm = rM.next()
                        for kc in range(KC):
                            P.op('pe', lambda e, b=b, sub=sub, kc=kc, g=g, pm=pm: e.matmul(
                                psM[pm][:], wb[b][:, kc, sub * 128:(sub + 1) * 128], hT[:, kc, g * 512:(g + 1) * 512],
                                start=(kc == 0), stop=(kc == KC - 1)),
                                reads=[wname + '%d' % b, 'hT'], writes=['psM%d' % pm])
                        o = ro.next()
                        eng = 'act' if (ci + g) % 2 == 0 else 'dve'
                        if eng == 'act':
                            P.op('act', lambda e, o=o, pm=pm: e.copy(ob[o][:], psM[pm][:]),
                                 reads=['psM%d' % pm], writes=[wname + 'o%d' % o])
                        else:
                            P.op('dve', lambda e, o=o, pm=pm: e.tensor_copy(ob[o][:], psM[pm][:]),
                                 reads=['psM%d' % pm], writes=[wname + 'o%d' % o])
                        dap, dkey = dst_fn(ci, g)
                        P.op('sp', lambda e, o=o, dap=dap: e.dma_start(out=dap, in_=ob[o][:]),
                             reads=[wname + 'o%d' % o], writes=[dkey], dma=True)
                    ci += 1
        P.barrier()

    def proj_tm(hT, Wd, blocks, dst_fn, wname='pt'):
        with ExitStack() as st:
            wb = [P.sb(wname + "%d" % i, [128, KC, 512], BF16, st) for i in range(2)]
            ob = [P.sb(wname + "o%d" % i, [128, 512], dst_fn.dt, st) for i in range(4)]
            ro = Ring([0, 1, 2, 3])
            for bi, pieces in enumerate(blocks):
                b = bi % 2
                wload(wb[b], wname + '%d' % b, Wd, pieces, KC)
                for i in range(NT):
                    pm = rM.next()
                    for kc in range(KC):
                        P.op('pe', lambda e, b=b, kc=kc, i=i, pm=pm: e.matmul(
                            psM[pm][:], hT[:, kc, i * 128:(i + 1) * 128], wb[b][:, kc, :],
                            start=(kc == 0), stop=(kc == KC - 1)),
                            reads=[wname + '%d' % b, 'hT'], writes=['psM%d' % pm])
                    o = ro.next()
                    if (bi + i) % 2 == 0:
                        P.op('act', lambda e, o=o, pm=pm: e.copy(ob[o][:], psM[pm][:]),
                             reads=['psM%d' % pm], writes=[wname + 'o%d' % o])
                    else:
                        P.op('dve', lambda e, o=o, pm=pm: e.tensor_copy(ob[o][:], psM[pm][:]),
                             reads=['psM%d' % pm], writes=[wname + 'o%d' % o])
                    dap, dkey = dst_fn(bi, i)
                    P.op('sp', lambda e, o=o, dap=dap: e.dma_start(out=dap, in_=ob[o][:]),
                         reads=[wname + 'o%d' % o], writes=[dkey], dma=True)
        P.barrier()

    def out_proj(Wd, kcn):
        with ExitStack() as st:
            wb = [P.sb("ow%d" % i, [128, kcn, 512], BF16, st) for i in range(2)]
            ab = [P.sb("oa%d" % i, [128, kcn, 128], BF16, st) for i in range(3)]
            ob = [P.sb("oo%d" % i, [128, 512], F32, st) for i in range(3)]
            junk = P.sb("ojunk", [128, 512], BF16, st)
            ra, ro = Ring([0, 1, 2]), Ring([0, 1, 2])
            import os
            dbgl = int(os.environ.get('OPDBG', '9'))
            for nb in range(4):
                b = nb % 2
                half = kcn // 2
                wload(wb[b], 'ow%d' % b, Wd, [(nb * 512, 512)], half, 0)
                src = Wd[half * 128:kcn * 128, nb * 512:(nb + 1) * 512].rearrange("(kc p) n -> p kc n", p=128)
                P.op('pool', lambda e, b=b, src=src, half=half: e.dma_start(out=wb[b][:, half:kcn, :], in_=src),
                     writes=['ow%d' % b], dma=True)
                for i in range(NT):
                    if dbgl < 1:
                        break
                    a = ra.next()
                    P.op('sp', lambda e, a=a, i=i: e.dma_start(out=ab[a][:], in_=actT[i, :, 0:kcn, :]),
                         writes=['oa%d' % a], dma=True)
                    if dbgl < 2:
                        continue
                    pm = rM.next()
                    for kc in range(kcn):
                        P.op('pe', lambda e, a=a, b=b, kc=kc, pm=pm: e.matmul(
                            psM[pm][:], ab[a][:, kc, :], wb[b][:, kc, :], start=(kc == 0), stop=(kc == kcn - 1)),
                            reads=['oa%d' % a, 'ow%d' % b], writes=['psM%d' % pm])
                    if dbgl < 3:
                        continue
                    o = ro.next()
                    P.op('dve', lambda e, o=o, pm=pm: e.tensor_copy(ob[o][:], psM[pm][:]),
                         reads=['psM%d' % pm], writes=['oo%d' % o])
                    if dbgl < 4:
                        continue
                    P.op('act', lambda e, o=o, i=i, nb=nb: e.activation(junk[:], ob[o][:], AF.Square,
                                                                      accum_out=ssq[:, i, nb:nb + 1]),
                         reads=['oo%d' % o], writes=['ojunk', 'ssq'])
                    P.op('act', lambda e, o=o, i=i, nb=nb: e.dma_start(
                        out=raw[i * 128:(i + 1) * 128, nb * 512:(nb + 1) * 512], in_=ob[o][:]),
                        reads=['oo%d' % o], writes=['raw'], dma=True)
        P.barrier()

    def residual_pass(src, dst):
        with ExitStack() as st:
            xt = [P.sb("rx%d" % i, [128, D], F32, st) for i in range(3)]
            rt = [P.sb("rr%d" % i, [128, D], F32, st) for i in range(3)]
            sm = [P.sb("rs%d" % i, [128, 4], F32, st) for i in range(3)]
            for i in range(NT):
                b = i % 3
                v = tile_var[i]
                P.op('sp', lambda e, b=b, i=i: e.dma_start(out=xt[b][:], in_=src[i * 128:(i + 1) * 128, :]),
                     writes=['rx%d' % b], dma=True)
                P.op('sp', lambda e, b=b, i=i: e.dma_start(out=rt[b][:], in_=raw[i * 128:(i + 1) * 128, :]),
                     reads=['raw'], writes=['rr%d' % b], dma=True)
                P.op('dve', lambda e, b=b, i=i: e.tensor_reduce(sm[b][:, 0:1], ssq[:, i, :], mybir.AxisListType.X, ALU.add),
                     reads=['ssq'], writes=['rs%d' % b])
                P.op('dve', lambda e, b=b: e.tensor_scalar(sm[b][:, 1:2], sm[b][:, 0:1], 1.0 / D, EPS, ALU.mult, ALU.add),
                     reads=['rs%d' % b], writes=['rs%d' % b])
                P.op('act', lambda e, b=b: e.sqrt(sm[b][:, 3:4], sm[b][:, 1:2]),
                     reads=['rs%d' % b], writes=['rs%d' % b])
                P.op('dve', lambda e, b=b: e.reciprocal(sm[b][:, 2:3], sm[b][:, 3:4]),
                     reads=['rs%d' % b], writes=['rs%d' % b])
                P.op('dve', lambda e, b=b, v=v: e.scalar_tensor_tensor(
                    rt[b][:], rt[b][:], sm[b][:, 2:3], ggh[0][:, v, :], ALU.mult, ALU.mult),
                    reads=['rs%d' % b, 'ggbc', 'rr%d' % b], writes=['rr%d' % b])
                P.op('pool', lambda e, b=b: e.tensor_tensor(xt[b][:], xt[b][:], rt[b][:], ALU.add),
                     reads=['rr%d' % b, 'rx%d' % b], writes=['rx%d' % b])
                P.op('act', lambda e, b=b, i=i: e.dma_start(out=dst[i * 128:(i + 1) * 128, :], in_=xt[b][:]),
                     reads=['rx%d' % b], writes=['dstres'], dma=True)
        P.barrier()

    def ffn(hT, l):
        with ExitStack() as st:
            wg = [P.sb("fg%d" % i, [128, KC, 256], BF16, st) for i in range(2)]
            wu = [P.sb("fu%d" % i, [128, KC, 256], BF16, st) for i in range(2)]
            sg_ = [P.sb("fs%d" % i, [128, 512], F32, st) for i in range(2)]
            hb = [P.sb("fh%d" % i, [128, 512], BF16, st) for i in range(3)]
            rh = Ring([0, 1, 2])
            for blk in range(22):
                b = blk % 2
                wload(wg[b], 'fg%d' % b, ffn_wg[l], [(blk * 256, 256)], KC)
                wload(wu[b], 'fu%d' % b, ffn_wu[l], [(blk * 256, 256)], KC)
                for sub in range(2):
                    c = blk * 2 + sub
                    for g in range(NG):
                        pg, pu = rM.next(), rM.next()
                        for kc in range(KC):
                            P.op('pe', lambda e, b=b, sub=sub, kc=kc, g=g, pg=pg: e.matmul(
                                psM[pg][:], wg[b][:, kc, sub * 128:(sub + 1) * 128], hT[:, kc, g * 512:(g + 1) * 512],
                                start=(kc == 0), stop=(kc == KC - 1)), reads=['fg%d' % b, 'hT'], writes=['psM%d' % pg])
                        for kc in range(KC):
                            P.op('pe', lambda e, b=b, sub=sub, kc=kc, g=g, pu=pu: e.matmul(
                                psM[pu][:], wu[b][:, kc, sub * 128:(sub + 1) * 128], hT[:, kc, g * 512:(g + 1) * 512],
                                start=(kc == 0), stop=(kc == KC - 1)), reads=['fu%d' % b, 'hT'], writes=['psM%d' % pu])
                        s = (c + g) % 2
                        P.op('act', lambda e, s=s, pg=pg: e.activation(sg_[s][:], psM[pg][:], AF.Silu),
                             reads=['psM%d' % pg], writes=['fs%d' % s])
                        h = rh.next()
                        P.op('dve', lambda e, s=s, h=h, pu=pu: e.tensor_tensor(hb[h][:], sg_[s][:], psM[pu][:], ALU.mult),
                             reads=['fs%d' % s, 'psM%d' % pu], writes=['fh%d' % h])
                        P.op('sp', lambda e, h=h, g=g, c=c: e.dma_start(
                            out=actT[g * 4:(g + 1) * 4, :, c, :].rearrange("t p n -> p t n"),
                            in_=hb[h][:].rearrange("p (t n) -> p t n", t=4)),
                            reads=['fh%d' % h], writes=['actT'], dma=True)
        P.barrier()

    def gated_res(l, ti, src_, dst_):
        with ExitStack() as st:
            ggh[0] = P.sb("ggbc", [128, 2, D], F32, st)
            build_ggbc(l, ti)
            residual_pass(src_, dst_)
        P.barrier()

    def with_hT(fn):
        with ExitStack() as st:
            hT = P.sb("hT", [128, KC, T], BF16, st)
            fn(hT)
        P.barrier()

    env = dict(locals())
    env['scr'] = scr
    cur = x_in
    if mode.startswith('ffn_only'):
        lvl = int(mode[8:] or 9)
        if lvl >= 2:
            with_hT(lambda hT: (norm_phase(hT, 0, 1, cur), ffn(hT, 0)))
        if lvl >= 3:
            out_proj(ffn_wd[0], 44)
        if lvl >= 5:
            gated_res(0, 5, cur, y_out)
    else:
        for l in ((1,) if mode == 'odd' else range(2)):
            if l == 0:
                with_hT(lambda hT: (norm_phase(hT, l, 0, cur), even_proj(env, hT)))
                even_mix(env)
            else:
                with_hT(lambda hT: (norm_phase(hT, l, 0, cur), odd_proj(env, hT)))
                odd_mix(env)
            if mode in ('gla', 'mix'):
                break
            out_proj(ev_w_out if l == 0 else od_w_out, 32)
            if mode in ('even', 'odd'):
                break
            gated_res(l, 2, cur, xres)
            cur = xres
            with_hT(lambda hT: (norm_phase(hT, l, 1, cur), ffn(hT, l)))
            out_proj(ffn_wd[l], 44)
            gated_res(l, 5, cur, y_out if l == 1 else xres)
    P.finish()
    P.close()
    print("build: ops=%d waits=%d" % (P.nops, P.nwaits))
    return nc


class _NS:
    def __init__(self, d):
        self.__dict__.update(d)


def _dst(fn, dt):
    fn.dt = dt
    return fn


def even_proj(env, hT):
    E = _NS(env)
    P, nc, T, NT, LS, S0 = E.P, E.nc, E.T, E.NT, E.LS, E.S0
    sc = E.scr
    G = env['ev'] = {}
    G['qkT'] = sc("qkT", [16, 128, T], BF16)
    G['qkswT'] = sc("qkswT", [16, 128, LS], BF16)
    G['vtm'] = sc("vtm", [T, 2048], BF16)
    G['gtm'] = sc("gtm", [T, 2048], BF16)
    G['lrT'] = sc("lrT", [32, T], F32)
    G['xrT'] = sc("xrT", [16, 128, T], F32)
    G['yrT'] = sc("yrT", [16, 128, T], BF16)
    W = E.ev_w_in
    E.proj_fm(hT, W, [([(EV_Q + b * 512, 512)], 4) for b in range(4)],
              _dst(lambda c, g: (G['qkT'][c, :, g * 512:(g + 1) * 512], 'qkT'), BF16), wname='pq')
    blocks = []
    for b in range(4):
        pieces = []
        for c in range(4):
            c0 = EV_Q + b * 512 + c * 128
            pieces += [(c0 + 64, 64), (c0, 64)]
        blocks.append((pieces, 4))
    E.proj_fm(hT, W, blocks,
              _dst(lambda c, g: (G['qkswT'][c, :, g * 512 - S0:(g + 1) * 512 - S0], 'qkswT'), BF16),
              tok0=S0, tok1=T, wname='pz')
    E.proj_fm(hT, W, [([(EV_XR + b * 512, 512)], 4) for b in range(4)],
              _dst(lambda c, g: (G['xrT'][c, :, g * 512:(g + 1) * 512], 'xrT'), F32), wname='px')
    E.proj_fm(hT, W, [([(EV_YR + b * 512, 512)], 4) for b in range(4)],
              _dst(lambda c, g: (G['yrT'][c, :, g * 512:(g + 1) * 512], 'yrT'), BF16), wname='py')
    E.proj_tm(hT, W, [[(EV_V + b * 512, 512)] for b in range(4)],
              _dst(lambda b, i: (G['vtm'][i * 128:(i + 1) * 128, b * 512:(b + 1) * 512], 'vtm'), BF16), wname='pv')
    E.proj_tm(hT, W, [[(EV_G + b * 512, 512)] for b in range(4)],
              _dst(lambda b, i: (G['gtm'][i * 128:(i + 1) * 128, b * 512:(b + 1) * 512], 'gtm'), BF16), wname='pg')
    psM, rM = E.psM, E.rM
    with ExitStack() as st:
        wl = P.sb("wl", [128, KC, 32], BF16, st)
        lo = [P.sb("lo%d" % i, [32, 512], F32, st) for i in range(2)]
        P.op('pool', lambda e: e.dma_start(out=wl[:], in_=W[:, EV_LR:EV_LR + 32].rearrange("(kc p) n -> p kc n", p=128)),
             writes=['wl'], dma=True)
        for g in range(T // 512):
            pm = rM.next()
            for kc in range(KC):
                P.op('pe', lambda e, kc=kc, g=g, pm=pm: e.matmul(psM[pm][0:32, :], wl[:, kc, :], hT[:, kc, g * 512:(g + 1) * 512],
                                                            start=(kc == 0), stop=(kc == KC - 1)),
                     reads=['wl', 'hT'], writes=['psM%d' % pm])
            b = g % 2
            P.op('act', lambda e, b=b, pm=pm: e.copy(lo[b][:], psM[pm][0:32, :]), reads=['psM%d' % pm], writes=['lo%d' % b])
            P.op('sp', lambda e, b=b, g=g: e.dma_start(out=G['lrT'][:, g * 512:(g + 1) * 512], in_=lo[b][:]),
                 reads=['lo%d' % b], writes=['lrT'], dma=True)
    P.barrier()


def even_mix(env):
    E = _NS(env)
    P, nc, T, NT, LS, S0, NP = E.P, E.nc, E.T, E.NT, E.LS, E.S0, E.NP
    G = env['ev']
    psM, rM, psT, rT = E.psM, E.rM, E.psT, E.rT
    identb, identf, onesf, INC, GINC = E.identb, E.identf, E.onesf, E.INC, E.GINC
    actT = E.actT
    X = mybir.AxisListType.X
    with ExitStack() as st:
        rp = P.sb("rp", [128, 4, LS], F32, st)
        qa = P.sb("qa", [128, LS], BF16, st)
        qs = P.sb("qs", [128, LS], BF16, st)
        t1 = P.sb("t1", [128, LS], F32, st)
        t2 = P.sb("t2", [128, LS], F32, st)
        qo = P.sb("qo", [128, LS], BF16, st)
        P.op('sp', lambda e: e.dma_start(out=rp[:], in_=E.rope.rearrange("c p n -> p c n")), writes=['rp'], dma=True)
        for c in range(16):
            tb = 0 if c % 2 == 0 else 2
            P.op('sp', lambda e, c=c: e.dma_start(out=qa[:], in_=G['qkT'][c, :, S0:S0 + LS]), writes=['qa'], dma=True)
            P.op('sp', lambda e, c=c: e.dma_start(out=qs[:], in_=G['qkswT'][c, :, :]), writes=['qs'], dma=True)
            P.op('dve', lambda e, tb=tb: e.tensor_tensor(t1[:], qa[:], rp[:, tb, :], ALU.mult), reads=['qa', 'rp'], writes=['t1'])
            P.op('pool', lambda e, tb=tb: e.tensor_tensor(t2[:], qs[:], rp[:, tb + 1, :], ALU.mult), reads=['qs', 'rp'], writes=['t2'])
            P.op('dve', lambda e: e.tensor_tensor(qo[:], t1[:], t2[:], ALU.add), reads=['t1', 't2'], writes=['qo'])
            P.op('sp', lambda e, c=c: e.dma_start(out=G['qkT'][c, :, S0:S0 + LS], in_=qo[:]), reads=['qo'], writes=['qkT'], dma=True)
    P.barrier()
    o_f = E.scr("o_f", [T, 2048], F32)
    GSK = ['S%d' % h for h in range(4)]
    GSBK = ['Sbf%d' % h for h in range(4)]
    with ExitStack() as st:
        wup = P.sb("wup", [16, 2, 1024], F32, st)
        bup = P.sb("bup", [1, 2, 1024], F32, st)
        gnbc = P.sb("gnbc", [128, 2048], F32, st)
        S = P.sb("S", [128, 8, 512], F32, st)
        Sbf = P.sb("Sbf", [128, 8, 512], BF16, st)
        lrd = [P.sb("lrd%d" % i, [16, 128], F32, st) for i in range(2)]
        spt = [P.sb("spt%d" % i, [128, 1024], F32, st) for i in range(2)]
        e_all = [P.sb("e_all%d" % i, [128, 8, 128], F32, st) for i in range(2)]
        einv = [P.sb("einv%d" % i, [128, 8, 128], F32, st) for i in range(2)]
        qT = [P.sb("qT%d" % i, [128, 8, 128], BF16, st) for i in range(2)]
        kT = [P.sb("kT%d" % i, [128, 8, 128], BF16, st) for i in range(2)]
        qdec = [P.sb("qdec%d" % i, [128, 8, 128], BF16, st) for i in range(2)]
        kinc = [P.sb("kinc%d" % i, [128, 8, 128], BF16, st) for i in range(2)]
        krem = [P.sb("krem%d" % i, [128, 8, 128], BF16, st) for i in range(2)]
        kremtm = [P.sb("kremtm%d" % i, [128, 1024], BF16, st) for i in range(2)]
        vt = [P.sb("vt%d" % i, [128, 2048], BF16, st) for i in range(2)]
        PT = [P.sb("PT%d" % i, [128, 128], BF16, st) for i in range(2)]
        ot = P.sb("ot", [128, 2048], F32, st)
        oft = [P.sb("oft%d" % i, [128, 2048], F32, st) for i in range(2)]
        gt = [P.sb("gt%d" % i, [128, 2048], BF16, st) for i in range(2)]
        sg = P.sb("sgl", [128, 2048], F32, st)
        mo = P.sb("mo", [128, 2048], BF16, st)
        moT = P.sb("moT", [128, 16, 128], BF16, st)
        junk = P.sb("gjunk", [128, 512], BF16, st)
        sm = P.sb("gsm", [128, 16], F32, st)
        P.op('sp', lambda e: e.dma_start(out=wup[:], in_=E.gla_w_up.rearrange("d r n -> r d n")), writes=['wup'], dma=True)
        P.op('sp', lambda e: e.dma_start(out=bup[:], in_=E.gla_b_up.rearrange("(o d) n -> o d n", o=1)), writes=['bup'], dma=True)
        P.op('sp', lambda e: e.dma_start(out=gnbc[:], in_=E.gla_ng.partition_broadcast(128)), writes=['gnbc'], dma=True)
        for si, (t0, n, var) in enumerate(E.seqs):
            for d in range(2):
                last = 127 if d == 0 else 0
                if var == 1:
                    P.op('sp', lambda e, d=d: e.dma_start(out=S[:], in_=E.sg_in[d].rearrange("h (kc p) n -> p (h kc) n", p=128)),
                         writes=GSK, dma=True)
                else:
                    P.op('dve', lambda e: e.memset(S[:], 0.0), writes=GSK)
                P.op('act', lambda e: e.copy(Sbf[:], S[:]), reads=GSK, writes=GSBK)
                order = list(range(t0, t0 + n)) if d == 0 else list(range(t0 + n - 1, t0 - 1, -1))

                def prep1(i, q):
                    tok = slice(i * 128, (i + 1) * 128)
                    Q = str(q)
                    P.op('sp', lambda e: e.dma_start(out=lrd[q][:], in_=G['lrT'][d * 16:(d + 1) * 16, tok]), writes=['lrd' + Q], dma=True)
                    P.op('sp', lambda e: e.dma_start(out=qT[q][:], in_=G['qkT'][0:8, :, tok].rearrange("c p n -> p c n")), writes=['qT' + Q], dma=True)
                    P.op('sp', lambda e: e.dma_start(out=kT[q][:], in_=G['qkT'][8:16, :, tok].rearrange("c p n -> p c n")), writes=['kT' + Q], dma=True)
                    P.op('sp', lambda e: e.dma_start(out=vt[q][:], in_=G['vtm'][tok, :]), writes=['vt' + Q], dma=True)
                    if d == 1:
                        P.op('sp', lambda e: e.dma_start(out=oft[q][:], in_=o_f[tok, :]), reads=['o_f'], writes=['oft' + Q], dma=True)
                        P.op('sp', lambda e: e.dma_start(out=gt[q][:], in_=G['gtm'][tok, :]), writes=['gt' + Q], dma=True)
                    for hf in range(2):
                        pm = rM.next()
                        P.op('pe', lambda e, hf=hf, pm=pm: e.matmul(psM[pm][:], lrd[q][:], wup[:, d, hf * 512:(hf + 1) * 512], start=True, stop=False),
                             reads=['lrd' + Q, 'wup'], writes=['psM%d' % pm])
                        P.op('pe', lambda e, hf=hf, pm=pm: e.matmul(psM[pm][:], onesf[0:1, :], bup[:, d, hf * 512:(hf + 1) * 512], start=False, stop=True),
                             reads=['bup'], writes=['psM%d' % pm])
                        P.op('act', lambda e, hf=hf, pm=pm: e.activation(spt[q][:, hf * 512:(hf + 1) * 512], psM[pm][:], AF.Exp, scale=-1.0),
                             reads=['psM%d' % pm], writes=['spt' + Q])
                    P.op('act', lambda e: e.activation(spt[q][:], spt[q][:], AF.Ln, bias=1.0), reads=['spt' + Q], writes=['spt' + Q])

                def prep2(i, q):
                    Q = str(q)
                    for hf in range(2):
                        pm = rM.next()
                        for j in range(4):
                            fb = hf * 4 + j
                            P.op('pe', lambda e, fb=fb, j=j, pm=pm: e.matmul(psM[pm][:, j * 128:(j + 1) * 128], spt[q][:, fb * 128:(fb + 1) * 128], GINC[d], start=True, stop=True),
                                 reads=['spt' + Q], writes=['psM%d' % pm])
                        P.op('act', lambda e, hf=hf, pm=pm: e.activation(e_all[q][:, hf * 4:(hf + 1) * 4, :], psM[pm][:].rearrange("p (a b) -> p a b", a=4), AF.Exp),
                             reads=['psM%d' % pm], writes=['e_all' + Q])
                        P.op('act', lambda e, hf=hf, pm=pm: e.activation(einv[q][:, hf * 4:(hf + 1) * 4, :], psM[pm][:].rearrange("p (a b) -> p a b", a=4), AF.Exp, scale=-1.0),
                             reads=['psM%d' % pm], writes=['einv' + Q])

                def prep3(i, q):
                    Q = str(q)
                    P.op('dve', lambda e: e.scalar_tensor_tensor(qdec[q][:], qT[q][:], 0.0625, e_all[q][:], ALU.mult, ALU.mult),
                         reads=['qT' + Q, 'e_all' + Q], writes=['qdec' + Q])
                    P.op('pool', lambda e: e.tensor_tensor(kinc[q][:], kT[q][:], einv[q][:], ALU.mult), reads=['kT' + Q, 'einv' + Q], writes=['kinc' + Q])
                    P.op('pool', lambda e: e.tensor_tensor(krem[q][:], kinc[q][:], e_all[q][:, :, last:last + 1].broadcast_to([128, 8, 128]), ALU.mult),
                         reads=['kinc' + Q, 'e_all' + Q], writes=['krem' + Q])

                def prep4(i, q):
                    Q = str(q)
                    pt = rT.next()
                    for fb in range(8):
                        P.op('pe', lambda e, fb=fb, pt=pt: e.transpose(psT[pt][:, fb * 128:(fb + 1) * 128], krem[q][:, fb, :], identb[:]),
                             reads=['krem' + Q], writes=['psT%d' % pt])
                    P.op('act', lambda e, pt=pt: e.copy(kremtm[q][:], psT[pt][:]), reads=['psT%d' % pt], writes=['kremtm' + Q])

                def head(i, q, h):
                    Q = str(q)
                    if True:
                        pm = rM.next()
                        for kc in range(2):
                            P.op('pe', lambda e, h=h, kc=kc, pm=pm: e.matmul(psM[pm][:, 0:128], kinc[q][:, 2 * h + kc, :], qdec[q][:, 2 * h + kc, :], start=(kc == 0), stop=(kc == 1)),
                                 reads=['kinc' + Q, 'qdec' + Q], writes=['psM%d' % pm])
                        pb = h % 2
                        P.op('dve', lambda e, pb=pb, pm=pm: e.tensor_tensor(PT[pb][:], psM[pm][:, 0:128], INC[d], ALU.mult),
                             reads=['psM%d' % pm], writes=['PT%d' % pb])
                        po = rM.next()
                        P.op('pe', lambda e, h=h, pb=pb, po=po: e.matmul(psM[po][:], PT[pb][:], vt[q][:, h * 512:(h + 1) * 512], start=True, stop=False),
                             reads=['PT%d' % pb, 'vt' + Q], writes=['psM%d' % po])
                        for kc in range(2):
                            P.op('pe', lambda e, h=h, kc=kc, po=po: e.matmul(psM[po][:], qdec[q][:, 2 * h + kc, :], Sbf[:, 2 * h + kc, :], start=False, stop=(kc == 1)),
                                 reads=['qdec' + Q, 'Sbf%d' % h], writes=['psM%d' % po])
                        if d == 0:
                            P.op('act', lambda e, h=h, po=po: e.copy(ot[:, h * 512:(h + 1) * 512], psM[po][:]), reads=['psM%d' % po], writes=['ot%d' % h])
                        else:
                            P.op('dve', lambda e, h=h, po=po: e.tensor_tensor(ot[:, h * 512:(h + 1) * 512], psM[po][:], oft[q][:, h * 512:(h + 1) * 512], ALU.add),
                                 reads=['psM%d' % po, 'oft' + Q], writes=['ot%d' % h])
                        for kc in range(2):
                            fb = 2 * h + kc
                            pu = rM.next()
                            P.op('pe', lambda e, h=h, fb=fb, pu=pu: e.matmul(psM[pu][:], kremtm[q][:, fb * 128:(fb + 1) * 128], vt[q][:, h * 512:(h + 1) * 512], start=True, stop=True),
                                 reads=['kremtm' + Q, 'vt' + Q], writes=['psM%d' % pu])
                            P.op('dve', lambda e, fb=fb, pu=pu: e.scalar_tensor_tensor(S[:, fb, :], S[:, fb, :], e_all[q][:, fb, last:last + 1], psM[pu][:], ALU.mult, ALU.add),
                                 reads=['psM%d' % pu, 'e_all' + Q, 'S%d' % h], writes=['S%d' % h])
                            P.op('act', lambda e, fb=fb: e.copy(Sbf[:, fb, :], S[:, fb, :]), reads=['S%d' % h], writes=['Sbf%d' % h])

                def finish(i, q):
                    tok = slice(i * 128, (i + 1) * 128)
                    Q = str(q)
                    OK_ = ['ot%d' % h for h in range(4)]
                    if d == 0:
                        P.op('act', lambda e: e.dma_start(out=o_f[tok, :], in_=ot[:]), reads=OK_, writes=['o_f'], dma=True)
                    else:
                        for h in range(4):
                            P.op('act', lambda e, h=h: e.activation(junk[:], ot[:, h * 512:(h + 1) * 512], AF.Square, accum_out=sm[:, h:h + 1]),
                                 reads=['ot%d' % h], writes=['gjunk', 'gsm'])
                        P.op('dve', lambda e: e.tensor_scalar(sm[:, 4:8], sm[:, 0:4], 1.0 / 512, EPS, ALU.mult, ALU.add), reads=['gsm'], writes=['gsm'])
                        P.op('act', lambda e: e.sqrt(sm[:, 8:12], sm[:, 4:8]), reads=['gsm'], writes=['gsm'])
                        P.op('dve', lambda e: e.reciprocal(sm[:, 12:16], sm[:, 8:12]), reads=['gsm'], writes=['gsm'])
                        P.op('act', lambda e: e.activation(sg[:], gt[q][:], AF.Silu), reads=['gt' + Q], writes=['sgl'])
                        for h in range(4):
                            P.op('dve', lambda e, h=h: e.scalar_tensor_tensor(ot[:, h * 512:(h + 1) * 512], ot[:, h * 512:(h + 1) * 512], sm[:, 12 + h:13 + h],
                                                                             gnbc[:, h * 512:(h + 1) * 512], ALU.mult, ALU.mult),
                                 reads=['ot%d' % h, 'gsm', 'gnbc'], writes=['ot%d' % h])
                        P.op('pool', lambda e: e.tensor_tensor(mo[:], ot[:], sg[:], ALU.mult), reads=OK_ + ['sgl'], writes=['mo'])
                        for h8 in range(2):
                            pt = rT.next()
                            for j in range(8):
                                kc = h8 * 8 + j
                                P.op('pe', lambda e, kc=kc, j=j, pt=pt: e.transpose(psT[pt][:, j * 128:(j + 1) * 128], mo[:, kc * 128:(kc + 1) * 128], identb[:]),
                                     reads=['mo'], writes=['psT%d' % pt])
                            P.op('act', lambda e, h8=h8, pt=pt: e.copy(moT[:, h8 * 8:(h8 + 1) * 8, :], psT[pt][:].rearrange("p (a b) -> p a b", a=8)),
                                 reads=['psT%d' % pt], writes=['moT'])
                        P.op('act', lambda e: e.dma_start(out=actT[i, :, 0:16, :], in_=moT[:]), reads=['moT'], writes=['actT'], dma=True)

                preps = (prep1, prep2, prep3, prep4)
                for f_ in preps:
                    f_(order[0], 0)
                for k_, i in enumerate(order):
                    for h in range(4):
                        if k_ + 1 < len(order):
                            preps[h](order[k_ + 1], (k_ + 1) % 2)
                        head(i, k_ % 2, h)
                    finish(i, k_ % 2)
                if var == 0:
                    P.op('sp', lambda e, si=si, d=d: e.dma_start(out=E.ng_out[si, d].rearrange("h (kc p) n -> p (h kc) n", p=128), in_=S[:]),
                         reads=GSK, writes=['ng_out'], dma=True)
    P.barrier()
    if env['mode'] == 'gla':
        return
    Lmax = max(max(n for (_, n, _) in E.seqs) * 128, NP * E.LP)
    with ExitStack() as st:
        wr = P.sb("wr", [128, 2, 16, 128], BF16, st)
        wi = P.sb("wi", [128, 2, 16, 128], BF16, st)
        tb = P.sb("rtb", [128, 8, 2, KC], F32, st)
        cw = P.sb("rcw", [128, KC, 4], F32, st)
        cbs = P.sb("rcb", [128, KC], F32, st)
        hl = P.sb("hl", [128, NP * 32], F32, st)
        hlo = P.sb("hlo", [128, 128], F32, st)
        def mkbufs(tag, Lx, Lpad):
            B = {}
            B['xrp'] = [P.sb("xrp%s%d" % (tag, i), [128, Lpad], F32, st) for i in range(2)]
            B['yr'] = [P.sb("yr%s%d" % (tag, i), [128, Lx], BF16, st) for i in range(2)]
            B['xc'] = P.sb("xc" + tag, [128, Lx], F32, st)
            B['xcb'] = P.sb("xcb" + tag, [128, Lx], BF16, st)
            for nm in ('rr', 'ii', 'aa', 'hh'):
                B[nm] = [P.sb("%s%s%d" % (nm, tag, i), [128, Lx], F32, st) for i in range(2)]
            B['tmp'] = P.sb("rtmp" + tag, [128, Lx], F32, st)
            B['ybf'] = P.sb("ybf" + tag, [128, Lx], BF16, st)
            return B
        bufsets = [mkbufs('A', NP * E.LP, NP * (E.LP + 3)), mkbufs('B', LS, LS + 3)]
        P.op('pool', lambda e: e.dma_start(out=wr[:], in_=E.rnn_w_r.rearrange("d n i j -> i d n j")), writes=['wr'], dma=True)
        P.op('pool', lambda e: e.dma_start(out=wi[:], in_=E.rnn_w_i.rearrange("d n i j -> i d n j")), writes=['wi'], dma=True)
        P.op('sp', lambda e: e.dma_start(out=tb[:, 0], in_=E.rnn_brT), writes=['rtb'], dma=True)
        P.op('sp', lambda e: e.dma_start(out=tb[:, 1], in_=E.rnn_biT), writes=['rtb'], dma=True)
        P.op('sp', lambda e: e.dma_start(out=tb[:, 2], in_=E.rnn_lamT), writes=['rtb'], dma=True)
        P.op('sp', lambda e: e.dma_start(out=tb[:, 4], in_=E.srT_in), writes=['rtb'], dma=True)
        P.op('sp', lambda e: e.dma_start(out=cw[:], in_=E.rnn_cwT), writes=['rcw'], dma=True)
        P.op('sp', lambda e: e.dma_start(out=cbs[:], in_=E.rnn_cbT), writes=['rcb'], dma=True)
        P.op('act', lambda e: e.activation(tb[:, 3], tb[:, 2], AF.Exp, scale=-1.0), reads=['rtb'], writes=['rtb'])
        P.op('act', lambda e: e.activation(tb[:, 3], tb[:, 3], AF.Ln, bias=1.0), reads=['rtb'], writes=['rtb'])
        P.op('dve', lambda e: e.tensor_scalar(tb[:, 3], tb[:, 3], -8.0, None, ALU.mult), reads=['rtb'], writes=['rtb'])
        P.op('dve', lambda e: e.memset(tb[:, 5], 0.0), reads=['rtb'], writes=['rtb'])
        P.op('dve', lambda e: e.memset(hl[:], 0.0), writes=['hl'])
        P.barrier()
        def chain(t0, nb, Lb, var, B, TG):
            xrp, yr, xc, xcb, rr, ii, aa, hh, tmp, ybf = (B[k_] for k_ in ('xrp', 'yr', 'xc', 'xcb', 'rr', 'ii', 'aa', 'hh', 'tmp', 'ybf'))
            it = 0
            L = nb * Lb
            n = L // 128
            tk = slice(t0 * 128, t0 * 128 + L)
            for fc in range(KC):
                xb = it % 2
                it += 1
                XR = xrp[xb][:, 0:nb * (Lb + 3)].rearrange("p (b l) -> p b l", b=nb)
                YR = yr[xb]
                kx, ky = TG + 'xrp%d' % xb, TG + 'yr%d' % xb
                xc3 = xc[:, 0:L].rearrange("p (b l) -> p b l", b=nb)
                P.op('pool', lambda e, XR=XR: e.memset(XR[:, :, 0:2], 0.0), writes=[kx])
                P.op('pool', lambda e, XR=XR, Lb=Lb: e.memset(XR[:, :, Lb + 2:Lb + 3], 0.0), writes=[kx])
                P.op('sp', lambda e, XR=XR, fc=fc, tk=tk, Lb=Lb, nb=nb: e.dma_start(out=XR[:, :, 2:2 + Lb], in_=G['xrT'][fc, :, tk].rearrange("p (b l) -> p b l", b=nb)), writes=[kx], dma=True)
                P.op('sp', lambda e, YR=YR, fc=fc, tk=tk, L=L: e.dma_start(out=YR[:, 0:L], in_=G['yrT'][fc, :, tk]), writes=[ky], dma=True)
                P.op('dve', lambda e, XR=XR, fc=fc, Lb=Lb, xc3=xc3: e.tensor_scalar(xc3, XR[:, :, 0:Lb], cw[:, fc, 0:1], cbs[:, fc:fc + 1], ALU.mult, ALU.add),
                     reads=[kx], writes=[TG + 'xc'])
                for j in range(1, 4):
                    P.op('dve', lambda e, XR=XR, fc=fc, Lb=Lb, j=j, xc3=xc3: e.scalar_tensor_tensor(xc3, XR[:, :, j:j + Lb], cw[:, fc, j:j + 1], xc3, ALU.mult, ALU.add),
                         reads=[kx, TG + 'xc'], writes=[TG + 'xc'])
                P.op('act', lambda e, L=L: e.copy(xcb[:, 0:L], xc[:, 0:L]), reads=[TG + 'xc'], writes=[TG + 'xcb'])
                yield
                for d in range(2):
                    for (gate, wt, dst, bi) in (('r', wr, rr[d], 0), ('i', wi, ii[d], 1)):
                        for c0 in range(0, L, 512):
                            w_ = min(512, L - c0)
                            pm = rM.next()
                            P.op('pe', lambda e, wt=wt, d=d, fc=fc, c0=c0, w_=w_, pm=pm: e.matmul(psM[pm][:, 0:w_], wt[:, d, fc, :], xcb[:, c0:c0 + w_], start=True, stop=True),
                                 reads=['wr', 'wi', TG + 'xcb'], writes=['psM%d' % pm])
                            P.op('act', lambda e, dst=dst, bi=bi, d=d, fc=fc, c0=c0, w_=w_, pm=pm: e.activation(dst[:, c0:c0 + w_], psM[pm][:, 0:w_], AF.Sigmoid, bias=tb[:, bi, d, fc:fc + 1]),
                                 reads=['psM%d' % pm], writes=[TG + 'g%s%d' % (gate, d)])
                yield
                for d in range(2):
                    if d == 1:
                        yield
                    P.op('act', lambda e, d=d, fc=fc, L=L: e.activation(aa[d][:, 0:L], rr[d][:, 0:L], AF.Exp, scale=tb[:, 3, d, fc:fc + 1]), reads=[TG + 'gr%d' % d], writes=[TG + 'aa%d' % d])
                    eng = 'dve'
                    P.op(eng, lambda e, d=d, L=L: e.tensor_tensor(rr[d][:, 0:L], aa[d][:, 0:L], aa[d][:, 0:L], ALU.mult), reads=[TG + 'aa%d' % d], writes=[TG + 'gr%d' % d])
                    P.op('act', lambda e, d=d, L=L: e.activation(rr[d][:, 0:L], rr[d][:, 0:L], AF.Relu, scale=-1.0, bias=1.0), reads=[TG + 'gr%d' % d], writes=[TG + 'gr%d' % d])
                    P.op('act', lambda e, d=d, L=L: e.activation(rr[d][:, 0:L], rr[d][:, 0:L], AF.Sqrt), reads=[TG + 'gr%d' % d], writes=[TG + 'gr%d' % d])
                    P.op(eng, lambda e, d=d, L=L: e.tensor_tensor(ii[d][:, 0:L], ii[d][:, 0:L], xc[:, 0:L], ALU.mult), reads=[TG + 'gi%d' % d, TG + 'xc'], writes=[TG + 'gi%d' % d])
                    P.op(eng, lambda e, d=d, L=L: e.tensor_tensor(ii[d][:, 0:L], ii[d][:, 0:L], rr[d][:, 0:L], ALU.mult), reads=[TG + 'gi%d' % d, TG + 'gr%d' % d], writes=[TG + 'gi%d' % d])
                    if var == 1:
                        h0 = tb[:, 4, d, fc:fc + 1]
                    else:
                        h0 = 0.0
                        bc = 0 if d == 0 else Lb - 1
                        P.op(eng, lambda e, d=d, bc=bc, L=L, Lb=Lb: e.memset(aa[d][:, bc:L:Lb], 0.0), reads=[TG + 'aa%d' % d], writes=[TG + 'aa%d' % d])
                    if d == 0:
                        P.op('dve', lambda e, L=L, h0=h0: e.tensor_tensor_scan(hh[0][:, 0:L], aa[0][:, 0:L], ii[0][:, 0:L], h0, ALU.mult, ALU.add),
                             reads=[TG + 'aa0', TG + 'gi0'], writes=[TG + 'hh0'])
                    else:
                        P.op('dve', lambda e, L=L, h0=h0: e.tensor_tensor_scan(hh[1][:, 0:L][:, ::-1], aa[1][:, 0:L][:, ::-1], ii[1][:, 0:L][:, ::-1], h0, ALU.mult, ALU.add),
                             reads=[TG + 'aa1', TG + 'gi1'], writes=[TG + 'hh1'])
                    if var == 0:
                        lc = Lb - 1 if d == 0 else 0
                        c0_ = d * 16 + fc
                        P.op('act', lambda e, d=d, c0_=c0_, lc=lc, L=L, Lb=Lb, nb=nb: e.copy(hl[:, c0_:c0_ + 32 * (nb - 1) + 1:32], hh[d][:, lc:L:Lb]), reads=[TG + 'hh%d' % d], writes=['hl'])
                yield
                P.op('pool', lambda e, YR=YR, L=L: e.tensor_tensor(tmp[:, 0:L], YR[:, 0:L], YR[:, 0:L], ALU.mult), reads=[ky], writes=[TG + 'rtmp'])
                P.op('pool', lambda e, L=L: e.tensor_scalar(tmp[:, 0:L], tmp[:, 0:L], 0.044715, 1.0, ALU.mult, ALU.add), reads=[TG + 'rtmp'], writes=[TG + 'rtmp'])
                P.op('pool', lambda e, YR=YR, L=L: e.tensor_tensor(tmp[:, 0:L], tmp[:, 0:L], YR[:, 0:L], ALU.mult), reads=[TG + 'rtmp', ky], writes=[TG + 'rtmp'])
                P.op('act', lambda e, L=L: e.activation(tmp[:, 0:L], tmp[:, 0:L], AF.Sigmoid, scale=1.5957691216), reads=[TG + 'rtmp'], writes=[TG + 'rtmp'])
                P.op('pool', lambda e, YR=YR, L=L: e.tensor_tensor(tmp[:, 0:L], tmp[:, 0:L], YR[:, 0:L], ALU.mult), reads=[TG + 'rtmp', ky], writes=[TG + 'rtmp'])
                P.op('dve', lambda e, L=L: e.tensor_tensor(hh[0][:, 0:L], hh[0][:, 0:L], hh[1][:, 0:L], ALU.add), reads=[TG + 'hh0', TG + 'hh1'], writes=[TG + 'hh0'])
                P.op('dve', lambda e, L=L: e.tensor_tensor(ybf[:, 0:L], hh[0][:, 0:L], tmp[:, 0:L], ALU.mult), reads=[TG + 'hh0', TG + 'rtmp'], writes=[TG + 'ybf'])
                P.op('act', lambda e, fc=fc, t0=t0, n=n, L=L: e.dma_start(out=actT[t0:t0 + n, :, 16 + fc, :].rearrange("t p n -> p t n"),
                                                                       in_=ybf[:, 0:L].rearrange("p (t n) -> p t n", n=128)),
                     reads=[TG + 'ybf'], writes=['actT'], dma=True)
                yield

        gens = [chain(0, NP, E.LP, 0, bufsets[0], 'A'), chain(NP * E.LP // 128, 1, LS, 1, bufsets[1], 'B')]
        if env['mode'] in ('full', 'even'):
            wb1 = [P.sb("adb%d" % i, [128, KC, 128], BF16, st) for i in range(2)]
            gens.append(E.adaln_gen(1, wb1, 'adb', 128, psT[1][:].bitcast(F32), 'psT1'))
        alive = list(gens)
        while alive:
            for g_ in list(alive):
                try:
                    next(g_)
                except StopIteration:
                    alive.remove(g_)
        pm = rM.next()
        P.op('pe', lambda e, pm=pm: e.matmul(psM[pm][0:NP * 32, 0:128], hl[:], identf, start=True, stop=True), reads=['hl', 'cst'], writes=['psM%d' % pm])
        P.op('act', lambda e, pm=pm: e.copy(hlo[0:NP * 32, :], psM[pm][0:NP * 32, 0:128]), reads=['psM%d' % pm], writes=['hlo'])
        P.op('sp', lambda e: e.dma_start(out=E.nr_out, in_=hlo[0:NP * 32, :]), reads=['hlo'], writes=['nr_out'], dma=True)
    P.barrier()


def odd_proj(env, hT):
    E = _NS(env)
    P, T = E.P, E.T
    sc = E.scr
    G = env['od'] = {}
    G['ztm'] = sc("ztm", [T, 4096], BF16)
    G['xbcT'] = sc("xbcT", [48, 128, T], BF16)
    G['dtT'] = sc("dtT", [128, T], F32)
    W = E.od_w_in
    E.proj_tm(hT, W, [[(OD_Z + b * 512, 512)] for b in range(8)],
              _dst(lambda b, i: (G['ztm'][i * 128:(i + 1) * 128, b * 512:(b + 1) * 512], 'ztm'), BF16), wname='oz')
    E.proj_fm(hT, W, [([(OD_XBC + b * 512, 512)], 4) for b in range(12)],
              _dst(lambda c, g: (G['xbcT'][c, :, g * 512:(g + 1) * 512], 'xbcT'), BF16), wname='ox')
    E.proj_fm(hT, W, [([(OD_DT, 128)], 1)],
              _dst(lambda c, g: (G['dtT'][:, g * 512:(g + 1) * 512], 'dtT'), F32), wname='od')


def odd_mix(env):
    E = _NS(env)
    P, nc, T, NT, LS, S0, NP = E.P, E.nc, E.T, E.NT, E.LS, E.S0, E.NP
    G = env['od']
    psM, rM, psT, rT = E.psM, E.rM, E.psT, E.rT
    identb, identf, onesf, INC, EXC = E.identb, E.identf, E.onesf, E.INC, E.EXC
    actT = E.actT
    xtm = E.scr("xtm", [T, 4096], BF16)
    Btm = E.scr("Btm", [T, 1024], BF16)
    y_f = E.scr("y_f", [T, 4096], F32)
    Lmax = max(max(n for (_, n, _) in E.seqs) * 128, NP * E.LP)
    nmax = Lmax // 128
    with ExitStack() as st:
        cw = P.sb("scw", [128, 48, 4], F32, st)
        cbs = P.sb("scb", [128, 48], F32, st)
        xp = [P.sb("sxp%d" % i, [128, max(Lmax + 3, NP * (E.LP + 3))], BF16, st) for i in range(2)]
        xc = [P.sb("sxc%d" % i, [128, Lmax], F32, st) for i in range(2)]
        xs = [P.sb("sxs%d" % i, [128, Lmax], BF16, st) for i in range(4)]
        asm = P.sb("sasm", [128, nmax, 512], BF16, st)
        P.op('sp', lambda e: e.dma_start(out=cw[:], in_=E.ssd_cwT), writes=['scw'], dma=True)
        P.op('sp', lambda e: e.dma_start(out=cbs[:], in_=E.ssd_cbT), writes=['scb'], dma=True)
        for (t0, nb, Lb, var) in ((0, NP, E.LP, 0), (NP * E.LP // 128, 1, LS, 1)):
            L = nb * Lb
            n = L // 128
            tk = slice(t0 * 128, t0 * 128 + L)
            for c4 in range(12):
                for cc in range(4):
                    c = c4 * 4 + cc
                    b = c % 2
                    XP = xp[b][:, 0:nb * (Lb + 3)].rearrange("p (b l) -> p b l", b=nb)
                    xc3 = xc[b][:, 0:L].rearrange("p (b l) -> p b l", b=nb)
                    P.op('pool', lambda e, XP=XP: e.memset(XP[:, :, 0:2], 0.0), writes=['sxp%d' % b])
                    P.op('pool', lambda e, XP=XP, Lb=Lb: e.memset(XP[:, :, Lb + 2:Lb + 3], 0.0), writes=['sxp%d' % b])
                    P.op('sp', lambda e, XP=XP, c=c, tk=tk, Lb=Lb, nb=nb: e.dma_start(out=XP[:, :, 2:2 + Lb], in_=G['xbcT'][c, :, tk].rearrange("p (b l) -> p b l", b=nb)), writes=['sxp%d' % b], dma=True)
                    P.op('act', lambda e, XP=XP, xc3=xc3, c=c, Lb=Lb: e.activation(xc3, XP[:, :, 0:Lb], AF.Identity, bias=cbs[:, c:c + 1], scale=cw[:, c, 0:1]),
                         reads=['sxp%d' % b, 'scw', 'scb'], writes=['sxc%d' % b])
                    for j in range(1, 4):
                        eng = 'dve'
                        P.op(eng, lambda e, XP=XP, xc3=xc3, c=c, Lb=Lb, j=j: e.scalar_tensor_tensor(xc3, XP[:, :, j:j + Lb], cw[:, c, j:j + 1], xc3, ALU.mult, ALU.add),
                             reads=['sxp%d' % b, 'scw', 'sxc%d' % b], writes=['sxc%d' % b])
                    P.op('act', lambda e, b=b, cc=cc, L=L: e.activation(xs[cc][:, 0:L], xc[b][:, 0:L], AF.Silu), reads=['sxc%d' % b], writes=['sxs%d' % cc])
                    if c >= 32:
                        P.op('act', lambda e, cc=cc, c=c, tk=tk, L=L: e.dma_start(out=G['xbcT'][c, :, tk], in_=xs[cc][:, 0:L]), reads=['sxs%d' % cc], writes=['xbcT'], dma=True)
                if c4 < 10:
                    for i in range(n):
                        pt = rT.next()
                        for cc in range(4):
                            P.op('pe', lambda e, cc=cc, i=i, pt=pt: e.transpose(psT[pt][:, cc * 128:(cc + 1) * 128], xs[cc][:, i * 128:(i + 1) * 128], identb[:]),
                                 reads=['sxs%d' % cc, 'identb'], writes=['psT%d' % pt])
                        if i % 2 == 0:
                            P.op('act', lambda e, i=i, pt=pt: e.copy(asm[:, i, :], psT[pt][:, 0:512]), reads=['psT%d' % pt], writes=['sasm'])
                        else:
                            P.op('dve', lambda e, i=i, pt=pt: e.tensor_copy(asm[:, i, :], psT[pt][:, 0:512]), reads=['psT%d' % pt], writes=['sasm'])
                    if c4 < 8:
                        dst = xtm[tk, c4 * 512:(c4 + 1) * 512]
                    else:
                        dst = Btm[tk, (c4 - 8) * 512:(c4 - 7) * 512]
                    P.op('act', lambda e, dst=dst, n=n: e.dma_start(out=dst.rearrange("(t p) f -> p t f", p=128), in_=asm[:, 0:n, :]),
                         reads=['sasm'], writes=['xtm'], dma=True)
    P.barrier()
    SK = ['sS%d' % g for g in range(8)]
    SBK = ['sSbf%d' % g for g in range(8)]
    YK = ['syt%d' % g for g in range(8)]
    with ExitStack() as st:
        dtb = P.sb("dtb", [128, 4], F32, st)
        dbc = P.sb("dbc", [128, 64], F32, st)
        ngbc = P.sb("sngbc", [128, 4096], F32, st)
        dl = P.sb("dl", [128, nmax, 256], F32, st)
        S = P.sb("sS", [128, 8, 512], F32, st)
        Sbf = P.sb("sSbf", [128, 8, 512], BF16, st)
        xt = P.sb("sxt", [128, 4096], BF16, st)
        xdt = P.sb("sxdt", [128, 4096], BF16, st)
        xdtw = P.sb("sxdtw", [128, 4096], BF16, st)
        Bt = P.sb("sBt", [128, 1024], BF16, st)
        BT = P.sb("sBT", [128, 8, 128], BF16, st)
        CT = P.sb("sCT", [128, 8, 128], BF16, st)
        yt = P.sb("syt", [128, 4096], F32, st)
        yft = P.sb("syft", [128, 4096], F32, st)
        dtt, lat = yft[:, 0:2048], yft[:, 2048:4096]
        zt = P.sb("szt", [128, 4096], BF16, st)
        sz = zt
        ex = P.sb("sex", [128, 192], F32, st)
        cbm8 = P.sb("scbm8", [128, 8, 128], F32, st)
        Lm = [P.sb("sLm%d" % i, [128, 4, 128], F32, st) for i in range(4)]
        Ee = [P.sb("sEe%d" % i, [128, 4, 128], F32, st) for i in range(4)]
        MT = [P.sb("sMT%d" % i, [128, 4, 128], BF16, st) for i in range(4)]
        tmp = [P.sb("stmp%d" % i, [128, 512], F32, st) for i in range(2)]
        junk = P.sb("sjunk", [128, 512], BF16, st)
        sm = P.sb("ssm", [128, 32], F32, st)
        nso = P.sb("snso", [128, 4, 128], F32, st)
        rL = Ring([0, 1])
        P.op('sp', lambda e: e.dma_start(out=dtb[:, 0:1], in_=E.ssd_dtbT), writes=['dtb'], dma=True)
        P.op('sp', lambda e: e.dma_start(out=dtb[:, 1:2], in_=E.ssd_alogT), writes=['dtb'], dma=True)
        P.op('sp', lambda e: e.dma_start(out=dbc[:], in_=E.ssd_d.partition_broadcast(128)), writes=['dbc'], dma=True)
        P.op('sp', lambda e: e.dma_start(out=ngbc[:], in_=E.ssd_ng.partition_broadcast(128)), writes=['sngbc'], dma=True)
        P.op('act', lambda e: e.activation(dtb[:, 2:3], dtb[:, 1:2], AF.Exp), reads=['dtb'], writes=['dtb'])
        P.op('dve', lambda e: e.tensor_scalar(dtb[:, 2:3], dtb[:, 2:3], -1.0, None, ALU.mult), reads=['dtb'], writes=['dtb'])
        for si, (t0, n, var) in enumerate(E.seqs):
            L = n * 128
            tk = slice(t0 * 128, t0 * 128 + L)
            P.op('sp', lambda e, tk=tk, L=L: e.dma_start(out=dtt[:, 0:L], in_=G['dtT'][:, tk]), writes=['syft'], dma=True)
            P.op('act', lambda e, L=L: e.activation(dtt[:, 0:L], dtt[:, 0:L], AF.Exp, bias=dtb[:, 0:1]), reads=['syft', 'dtb'], writes=['syft'])
            P.op('act', lambda e, L=L: e.activation(dtt[:, 0:L], dtt[:, 0:L], AF.Ln, bias=1.0), reads=['syft'], writes=['syft'])
            P.op('dve', lambda e, L=L: e.tensor_scalar(lat[:, 0:L], dtt[:, 0:L], dtb[:, 2:3], None, ALU.mult), reads=['syft', 'dtb'], writes=['syft'])
            for i in range(n):
                pm = rM.next()
                P.op('pe', lambda e, i=i, pm=pm: e.matmul(psM[pm][:, 0:128], dtt[:, i * 128:(i + 1) * 128], identf, start=True, stop=True),
                     reads=['syft', 'cst'], writes=['psM%d' % pm])
                P.op('pe', lambda e, i=i, pm=pm: e.matmul(psM[pm][:, 128:256], lat[:, i * 128:(i + 1) * 128], identf, start=True, stop=True),
                     reads=['syft', 'cst'], writes=['psM%d' % pm])
                P.op('act', lambda e, i=i, pm=pm: e.copy(dl[:, i, :], psM[pm][:, 0:256]), reads=['psM%d' % pm], writes=['dl'])
            for d in range(2):
                if var == 1:
                    P.op('sp', lambda e, d=d: e.dma_start(out=S[:], in_=E.ssT_in[d].rearrange("g n f -> n g f")), writes=SK, dma=True)
                else:
                    P.op('dve', lambda e: e.memset(S[:], 0.0), writes=SK)
                P.op('act', lambda e: e.copy(Sbf[:], S[:]), reads=SK, writes=SBK)
                order = range(n) if d == 0 else range(n - 1, -1, -1)
                for il in order:
                    i = t0 + il
                    tok = slice(i * 128, (i + 1) * 128)
                    dt_ = dl[:, il, d * 64:(d + 1) * 64]
                    la_ = dl[:, il, 128 + d * 64:128 + (d + 1) * 64]
                    P.op('sp', lambda e, tok=tok: e.dma_start(out=xt[:], in_=xtm[tok, :]), writes=['sxt'], dma=True)
                    P.op('sp', lambda e, tok=tok: e.dma_start(out=Bt[:], in_=Btm[tok, :]), writes=['sBt'], dma=True)
                    P.op('sp', lambda e, tok=tok: e.dma_start(out=BT[:], in_=G['xbcT'][32:40, :, tok].rearrange("g p n -> p g n")), writes=['sBT'], dma=True)
                    P.op('sp', lambda e, tok=tok: e.dma_start(out=CT[:], in_=G['xbcT'][40:48, :, tok].rearrange("g p n -> p g n")), writes=['sCT'], dma=True)
                    if d == 1:
                        P.op('sp', lambda e, tok=tok: e.dma_start(out=yft[:], in_=y_f[tok, :]), reads=['y_f'], writes=['syft'], dma=True)
                        P.op('sp', lambda e, tok=tok: e.dma_start(out=zt[:], in_=G['ztm'][tok, :]), writes=['szt'], dma=True)
                    pm = rM.next()
                    P.op('pe', lambda e, d=d, la_=la_, pm=pm: e.matmul(psM[pm][:, 0:64], INC[d], la_, start=True, stop=True), reads=['cst', 'dl'], writes=['psM%d' % pm])
                    P.op('pe', lambda e, d=d, la_=la_, pm=pm: e.matmul(psM[pm][:, 64:128], EXC[d], la_, start=True, stop=True), reads=['cst', 'dl'], writes=['psM%d' % pm])
                    P.op('pe', lambda e, la_=la_, pm=pm: e.matmul(psM[pm][:, 128:192], onesf, la_, start=True, stop=True), reads=['cst', 'dl'], writes=['psM%d' % pm])
                    P.op('act', lambda e, pm=pm: e.activation(ex[:], psM[pm][:, 0:192], AF.Exp), reads=['psM%d' % pm], writes=['sex'])
                    P.op('dve', lambda e, dt_=dt_: e.tensor_tensor(xdt[:].rearrange("p (h q) -> p h q", q=64), xt[:].rearrange("p (h q) -> p h q", q=64),
                                                                 dt_.unsqueeze(2).broadcast_to([128, 64, 64]), ALU.mult), reads=['sxt', 'dl'], writes=['sxdt'])
                    P.op('pool', lambda e: e.tensor_tensor(xdtw[:].rearrange("p (h q) -> p h q", q=64), xdt[:].rearrange("p (h q) -> p h q", q=64),
                                                        ex[:, 64:128].unsqueeze(2).broadcast_to([128, 64, 64]), ALU.mult), reads=['sxdt', 'sex'], writes=['sxdtw'])
                    PUb = psT[1][:].bitcast(F32)
                    for half in range(2):
                        pcb = rM.next()
                        for gg in range(4):
                            g = half * 4 + gg
                            P.op('pe', lambda e, g=g, gg=gg, pcb=pcb: e.matmul(psM[pcb][:, gg * 128:(gg + 1) * 128], BT[:, g, :], CT[:, g, :], start=True, stop=True),
                                 reads=['sBT', 'sCT'], writes=['psM%d' % pcb])
                        P.op('dve', lambda e, d=d, half=half, pcb=pcb: e.tensor_tensor(cbm8[:, half * 4:(half + 1) * 4, :], psM[pcb][:].rearrange("p (a b) -> p a b", a=4),
                                                                                    INC[d].unsqueeze(1).broadcast_to([128, 4, 128]), ALU.mult),
                             reads=['psM%d' % pcb], writes=['scbm'])

                    def stA1(g):
                        cs = g % 4
                        cb_ = g % 2
                        for hb in range(2):
                            h0 = g * 8 + hb * 4
                            k = (g % 2) * 2 + hb
                            P.op('dve', lambda e, d=d, k=k, la_=la_, h0=h0: e.tensor_tensor(Lm[k][:], EXC[d].unsqueeze(1).broadcast_to([128, 4, 128]),
                                                                                        la_[:, h0:h0 + 4].unsqueeze(2).broadcast_to([128, 4, 128]), ALU.mult),
                                 reads=['dl'], writes=['sLm%d' % k])
                            pg = hb
                            for j in range(4):
                                P.op('pe', lambda e, d=d, k=k, j=j, pg=pg: e.matmul(psM[pg][:, j * 128:(j + 1) * 128], Lm[k][:, j, :], INC[d], start=True, stop=True),
                                     reads=['sLm%d' % k], writes=['psM%d' % pg])
                            P.op('act', lambda e, k=k, pg=pg: e.activation(Ee[k][:], psM[pg][:].rearrange("p (a b) -> p a b", a=4), AF.Exp), reads=['psM%d' % pg], writes=['sEe%d' % k])
                            P.op('pool', lambda e, k=k, g=g: e.tensor_tensor(MT[k][:], Ee[k][:], cbm8[:, g, :].unsqueeze(1).broadcast_to([128, 4, 128]), ALU.mult),
                                 reads=['sEe%d' % k, 'scbm'], writes=['sMT%d' % k])

                    def stA2(g):
                        py, pi = 2 + g % 2, 4 + g % 2
                        for hb in range(2):
                            k = (g % 2) * 2 + hb
                            for j in range(4):
                                h = g * 8 + hb * 4 + j
                                jj = hb * 4 + j
                                P.op('pe', lambda e, k=k, j=j, jj=jj, h=h, py=py: e.matmul(psM[py][:, jj * 64:(jj + 1) * 64], MT[k][:, j, :], xdt[:, h * 64:(h + 1) * 64], start=True, stop=True),
                                     reads=['sMT%d' % k, 'sxdt'], writes=['psM%d' % py])
                        P.op('pe', lambda e, g=g, pi=pi: e.matmul(psM[pi][:], CT[:, g, :], Sbf[:, g, :], start=True, stop=True), reads=['sCT', 'sSbf%d' % g], writes=['psM%d' % pi])
                        P.op('pe', lambda e, g=g: e.matmul(PUb[:, :], Bt[:, g * 128:(g + 1) * 128], xdtw[:, g * 512:(g + 1) * 512], start=True, stop=True),
                             reads=['sBt', 'sxdtw'], writes=['psT1'])

                    def stB(g):
                        py, pi = 2 + g % 2, 4 + g % 2
                        tb_ = g % 2
                        P.op('dve', lambda e, g=g, pi=pi, tb_=tb_: e.tensor_tensor(tmp[tb_][:].rearrange("p (h q) -> p h q", q=64), psM[pi][:].rearrange("p (h q) -> p h q", q=64),
                                                                                ex[:, g * 8:(g + 1) * 8].unsqueeze(2).broadcast_to([128, 8, 64]), ALU.mult),
                             reads=['psM%d' % pi, 'sex'], writes=['stmp%d' % tb_])
                        ysl = yt[:, g * 512:(g + 1) * 512]
                        P.op('dve', lambda e, py=py, tb_=tb_, ysl=ysl: e.tensor_tensor(ysl, tmp[tb_][:], psM[py][:], ALU.add), reads=['psM%d' % py, 'stmp%d' % tb_], writes=['syt%d' % g])
                        if d == 1:
                            P.op('pool', lambda e, g=g, ysl=ysl: e.tensor_tensor(ysl, ysl, yft[:, g * 512:(g + 1) * 512], ALU.add), reads=['syt%d' % g, 'syft'], writes=['syt%d' % g])
                        P.op('dve', lambda e, g=g: e.tensor_tensor(S[:, g, :].rearrange("p (h q) -> p h q", q=64), S[:, g, :].rearrange("p (h q) -> p h q", q=64),
                                                                ex[:, 128 + g * 8:128 + (g + 1) * 8].unsqueeze(2).broadcast_to([128, 8, 64]), ALU.mult),
                             reads=['sS%d' % g, 'sex'], writes=['sS%d' % g])
                        P.op('dve', lambda e, g=g: e.tensor_tensor(S[:, g, :], S[:, g, :], PUb[:, :], ALU.add), reads=['sS%d' % g, 'psT1'], writes=['sS%d' % g])
                        P.op('act', lambda e, g=g: e.copy(Sbf[:, g, :], S[:, g, :]), reads=['sS%d' % g], writes=['sSbf%d' % g])

                    for s_ in range(10):
                        if s_ < 8:
                            stA1(s_)
                        if 2 <= s_:
                            stB(s_ - 2)
                        if 1 <= s_ < 9:
                            stA2(s_ - 1)
                    if d == 0:
                        P.op('sp', lambda e, tok=tok: e.dma_start(out=y_f[tok, :], in_=yt[:]), reads=YK, writes=['y_f'], dma=True)
                    else:
                        mo, moT = xdt, xdtw
                        P.op('dve', lambda e: e.tensor_tensor(yft[:].rearrange("p (h q) -> p h q", q=64), xt[:].rearrange("p (h q) -> p h q", q=64),
                                                           dbc[:].unsqueeze(2).broadcast_to([128, 64, 64]), ALU.mult), reads=['sxt', 'dbc'] + YK, writes=['syft'])
                        P.op('pool', lambda e: e.tensor_tensor(yt[:], yt[:], yft[:], ALU.add), reads=YK + ['syft'], writes=YK)
                        P.op('act', lambda e: e.activation(sz[:], zt[:], AF.Silu), reads=['szt'], writes=['szt'])
                        P.op('dve', lambda e: e.tensor_tensor(yt[:], yt[:], sz[:], ALU.mult), reads=YK + ['szt'], writes=YK)
                        for g in range(8):
                            P.op('act', lambda e, g=g: e.activation(junk[:], yt[:, g * 512:(g + 1) * 512], AF.Square, accum_out=sm[:, g:g + 1]), reads=YK, writes=['sjunk', 'ssm'])
                        P.op('dve', lambda e: e.tensor_scalar(sm[:, 8:16], sm[:, 0:8], 1.0 / 512, EPS, ALU.mult, ALU.add), reads=['ssm'], writes=['ssm'])
                        P.op('act', lambda e: e.sqrt(sm[:, 16:24], sm[:, 8:16]), reads=['ssm'], writes=['ssm'])
                        P.op('dve', lambda e: e.reciprocal(sm[:, 24:32], sm[:, 16:24]), reads=['ssm'], writes=['ssm'])
                        P.op('dve', lambda e: e.tensor_tensor(yt[:].rearrange("p (g q) -> p g q", q=512), yt[:].rearrange("p (g q) -> p g q", q=512),
                                                           sm[:, 24:32].unsqueeze(2).broadcast_to([128, 8, 512]), ALU.mult), reads=YK + ['ssm'], writes=YK)
                        P.op('dve', lambda e: e.tensor_tensor(mo[:], yt[:], ngbc[:], ALU.mult), reads=YK + ['sngbc', 'sxdtw'], writes=['sxdt'])
                        for h8 in range(4):
                            pt = rT.next()
                            for j in range(8):
                                kc = h8 * 8 + j
                                P.op('pe', lambda e, kc=kc, j=j, pt=pt: e.transpose(psT[pt][:, j * 128:(j + 1) * 128], mo[:, kc * 128:(kc + 1) * 128], identb[:]),
                                     reads=['sxdt', 'identb'], writes=['psT%d' % pt])
                            P.op('act', lambda e, h8=h8, pt=pt: e.copy(moT[:, h8 * 1024:(h8 + 1) * 1024], psT[pt][:]), reads=['psT%d' % pt], writes=['sxdtw'])
                        P.op('sp', lambda e, i=i: e.dma_start(out=actT[i, :, 0:32, :], in_=moT[:].rearrange("p (a b) -> p a b", b=128)), reads=['sxdtw'], writes=['actT'], dma=True)
                if var == 0:
                    for g in range(8):
                        pm = rM.next()
                        for q in range(4):
                            P.op('pe', lambda e, g=g, q=q, pm=pm: e.matmul(psM[pm][:, q * 128:(q + 1) * 128], S[:, g, q * 128:(q + 1) * 128], identf, start=True, stop=True),
                                 reads=SK + ['cst'], writes=['psM%d' % pm])
                        P.op('act', lambda e, pm=pm: e.copy(nso[:], psM[pm][:].rearrange("p (a b) -> p a b", a=4)), reads=['psM%d' % pm], writes=['snso'])
                        P.op('sp', lambda e, si=si, d=d, g=g: e.dma_start(out=E.ns_out[si, d, g * 8:(g + 1) * 8].rearrange("(q jj) p n -> (jj p) q n", jj=2), in_=nso[:]),
                             reads=['snso'], writes=['ns_out'], dma=True)
    P.barrier()


def _consts():
    s = np.arange(128)[:, None]
    t = np.arange(128)[None, :]
    c = np.zeros((8, 128, 128), np.float32)
    c[0] = np.eye(128)
    c[1] = 1.0
    c[2] = (s <= t)
    c[3] = (s >= t)
    c[4] = (s > t)
    c[5] = (s < t)
    c[6] = -1.0 * (s <= t) / 16.0
    c[7] = -1.0 * (s >= t) / 16.0
    return c


def _rope_tables(LS):
    GRID_W = 64
    rows = LS // GRID_W
    row = np.repeat(np.arange(rows), GRID_W).astype(np.float32)
    col = np.tile(np.arange(GRID_W), rows).astype(np.float32)
    nf = 64
    inv = (np.float32(10000.0) ** (-np.arange(nf, dtype=np.float32) / nf)).astype(np.float32)
    out = np.zeros((4, 128, LS), np.float32)
    for i, pos in enumerate((row, col)):
        ang = pos[None, :] * inv[:, None]
        cos, sin = np.cos(ang), np.sin(ang)
        out[2 * i, :64], out[2 * i, 64:] = cos, cos
        out[2 * i + 1, :64], out[2 * i + 1, 64:] = -sin, sin
    return out


def _fm(a):
    a = np.asarray(a, np.float32)
    lead = a.shape[:-1]
    k = a.shape[-1] // 128
    a = a.reshape(lead + (k, 128))
    return np.ascontiguousarray(np.moveaxis(a, -1, 0))


def prep(inputs, NP, LP, LS, ncores):
    I = {k: np.asarray(v) for k, v in inputs.items()}
    shared = {
        'w_ada': I['w_ada'], 'b_adaT': np.ascontiguousarray(I['b_ada'].reshape(2, 96, 128).transpose(0, 2, 1)),
        'ngT': np.ascontiguousarray(I['norm_g'].reshape(2, 4, KC, 128).transpose(0, 3, 1, 2)),
        'ev_w_in': I['ev_w_in'][0], 'ev_w_out': I['ev_w_out'][0], 'gla_w_up': I['gla_w_up'][0],
        'gla_b_up': I['gla_b_up'][0], 'gla_ng': I['gla_norm_g'][0][None],
        'rnn_cwT': np.ascontiguousarray(I['rnn_conv_w'][0].reshape(4, KC, 128).transpose(2, 1, 0)),
        'rnn_cbT': _fm(I['rnn_conv_b'][0]), 'rnn_w_r': I['rnn_w_r'][0], 'rnn_w_i': I['rnn_w_i'][0],
        'rnn_brT': _fm(I['rnn_b_r'][0]), 'rnn_biT': _fm(I['rnn_b_i'][0]), 'rnn_lamT': _fm(I['rnn_lam'][0]),
        'od_w_in': I['od_w_in'][0], 'od_w_out': I['od_w_out'][0],
        'ssd_cwT': np.ascontiguousarray(I['ssd_conv_w'][0].reshape(4, 48, 128).transpose(2, 1, 0)),
        'ssd_cbT': _fm(I['ssd_conv_b'][0]),
        'ssd_dtbT': np.ascontiguousarray(I['ssd_dt_bias'][0].reshape(128, 1)),
        'ssd_alogT': np.ascontiguousarray(I['ssd_a_log'][0].reshape(128, 1)),
        'ssd_d': I['ssd_d'][0][None], 'ssd_ng': I['ssd_norm_g'][0][None],
        'ffn_wg': I['ffn_w_gate'], 'ffn_wu': I['ffn_w_up'], 'ffn_wd': I['ffn_w_down'],
        'consts': _consts(), 'rope': _rope_tables(LS),
    }
    maps = []
    for c in range(ncores):
        xp = I['x_prompt'][c * NP:(c + 1) * NP].reshape(NP * LP, D)
        xs = I['x_sample'][c]
        cv = np.stack([I['c_ctx'], I['c'][c]], 0)
        m = dict(shared)
        m['x'] = np.ascontiguousarray(np.concatenate([xp, xs], 0))
        m['cT'] = np.ascontiguousarray(cv.reshape(2, KC, 128).transpose(2, 1, 0))
        m['sg'] = np.ascontiguousarray(I['state_gla'][c, 0])
        m['srT'] = _fm(I['state_rglru'][c, 0])
        ss = I['state_ssd'][c, 0].reshape(2, 8, 8, 64, 128)
        m['ssT'] = np.ascontiguousarray(ss.transpose(0, 1, 4, 2, 3).reshape(2, 8, 128, 512))
        maps.append(m)
    return maps


_NC_CACHE = {}


def kernel(**inputs):
    NP, LP, LS, ncores = 4, 256, 2048, 8
    key = (NP, LP, LS)
    if key not in _NC_CACHE:
        _NC_CACHE[key] = build(NP, LP, LS)
    nc = _NC_CACHE[key]
    maps = prep(inputs, NP, LP, LS, ncores)
    res = run_bass_kernel_spmd(nc, maps, core_ids=list(range(ncores)))
    R = res.results
    T = NP * LP + LS
    yp = np.concatenate([r['y'][:NP * LP].reshape(NP, LP, D) for r in R], 0)
    ys = np.stack([r['y'][NP * LP:] for r in R], 0)
    ng = np.concatenate([r['ng'] for r in R], 0)[:, None]
    nr = np.concatenate([r['nr'].reshape(NP, 2, D) for r in R], 0)[:, None]
    ns = np.concatenate([r['ns'] for r in R], 0)[:, None]
    return (yp.astype(np.float32), ys.astype(np.float32), ng.astype(np.float32), nr.astype(np.float32), ns.astype(np.float32))
```
